# Optimizing a Trainium2 kernel written in Bass

```python
import math
import jax
import jax.numpy as jnp
from jax import lax
import numpy as np

D_MODEL = 1024
BATCH = 32
SEQ = 2048
DEPTH = 4

GRID_W = 64
CTX_LEN = 256
HEAD_DIM = 64
MIX_GROUP_WIDTH = D_MODEL // 2
ROPE_THETA = 10000.0
BLOCK_Q = 128
NORM_EPS = 1e-6
NEG_INF = -1e30
MLP_HIDDEN = 4 * D_MODEL

A_HEADS = MIX_GROUP_WIDTH // HEAD_DIM
A_KV_HEADS = A_HEADS // 4
B_HEADS = MIX_GROUP_WIDTH // HEAD_DIM
B_WIDTH = B_HEADS * HEAD_DIM
B_DECAY_LORA = 64
B_AAA_LORA = 64
B_GATE_LORA = 128
B_GN_EPS = 64e-5
B_IN = 3 * B_WIDTH + 2 * B_DECAY_LORA + 2 * B_AAA_LORA + B_GATE_LORA
C_HEADS = MIX_GROUP_WIDTH // HEAD_DIM
C_KV_HEADS = C_HEADS // 4
WINDOW = 128
D_HEAD_DIM = 64
D_HEADS = MIX_GROUP_WIDTH // D_HEAD_DIM
D_INNER = D_HEADS * D_HEAD_DIM
D_GROUPS = 2
D_STATE = 128
D_CONV = 5
D_XBC = D_INNER + 2 * D_GROUPS * D_STATE
CHUNK = 128

N_EVEN = (DEPTH + 1) // 2
N_ODD = DEPTH // 2
AB_SIZES = (A_HEADS * HEAD_DIM, A_KV_HEADS * HEAD_DIM, A_KV_HEADS * HEAD_DIM, B_IN)
CD_SIZES = (C_HEADS * HEAD_DIM, C_KV_HEADS * HEAD_DIM, C_KV_HEADS * HEAD_DIM, D_INNER, D_XBC, 2 * D_HEADS)
AB_IN = sum(AB_SIZES)
CD_IN = sum(CD_SIZES)
F32 = jnp.float32

kernel_name = 'hybrid_dit_gqa_rwkv7_swa_ssd'


def rmsnorm(x, w):
    xf = x.astype(F32)
    y = xf * lax.rsqrt(jnp.mean(xf * xf, axis=-1, keepdims=True) + NORM_EPS)
    return (y * w.astype(F32)).astype(x.dtype)


def modulate(h, shift, scale):
    return h * (1.0 + scale) + shift


def sq_relu_mlp(h, w1, w2):
    return jnp.square(jax.nn.relu(h @ w1)) @ w2


def split_cols(t, sizes):
    return jnp.split(t, [int(s) for s in np.cumsum(sizes)[:-1]], axis=-1)


def heads(t, n_heads):
    return t.reshape(t.shape[0], t.shape[1], n_heads, HEAD_DIM)


def joint_softmax(parts):
    s = jnp.concatenate([p.astype(F32) for p in parts], axis=-1)
    prob = jax.nn.softmax(s, axis=-1)
    return jnp.split(prob, [int(v) for v in np.cumsum([p.shape[-1] for p in parts])[:-1]], axis=-1)


def axial_rope_tables(n_tokens):
    rows = n_tokens // GRID_W
    row = jnp.repeat(jnp.arange(rows, dtype=F32), GRID_W)
    col = jnp.tile(jnp.arange(GRID_W, dtype=F32), rows)
    n_freq = HEAD_DIM // 4
    inv_freq = ROPE_THETA ** (-jnp.arange(n_freq, dtype=F32) / n_freq)
    ang_r = row[:, None] * inv_freq
    ang_c = col[:, None] * inv_freq
    return (jnp.cos(ang_r), jnp.sin(ang_r), jnp.cos(ang_c), jnp.sin(ang_c))


def _rotate_half(x, cos, sin):
    x1, x2 = jnp.split(x, 2, axis=-1)
    return jnp.concatenate([x1 * cos - x2 * sin, x2 * cos + x1 * sin], axis=-1)


def apply_axial_rope(x, rope):
    cr, sr, cc, sc = (t[:, None, :].astype(x.dtype) for t in rope)
    xr, xc = jnp.split(x, 2, axis=-1)
    return jnp.concatenate([_rotate_half(xr, cr, sr), _rotate_half(xc, cc, sc)], axis=-1)


def global_gqa(q_l, k_l, v_l, q_c, k_c, v_c, q_norm, k_norm, rope, need_ctx):
    bsz, n_lat = q_l.shape[:2]
    rep = A_HEADS // A_KV_HEADS
    scale = HEAD_DIM ** -0.5
    ql = apply_axial_rope(rmsnorm(heads(q_l, A_HEADS), q_norm), rope)
    kl = apply_axial_rope(rmsnorm(heads(k_l, A_KV_HEADS), k_norm), rope)
    vl = heads(v_l, A_KV_HEADS)
    kc = rmsnorm(heads(k_c, A_KV_HEADS), k_norm)
    vc = heads(v_c, A_KV_HEADS)
    n_blk = n_lat // BLOCK_Q
    qb = jnp.moveaxis(ql.reshape(bsz, n_blk, BLOCK_Q, A_KV_HEADS, rep, HEAD_DIM), 1, 0)

    def block(q):
        s_lat = jnp.einsum('bqgrd,bkgd->bgrqk', q, kl).astype(F32) * scale
        s_ctx = jnp.einsum('bqgrd,bkgd->bgrqk', q, kc).astype(F32) * scale
        p_lat, p_ctx = joint_softmax([s_lat, s_ctx])
        return (jnp.einsum('bgrqk,bkgd->bqgrd', p_lat.astype(vl.dtype), vl)
                + jnp.einsum('bgrqk,bkgd->bqgrd', p_ctx.astype(vc.dtype), vc))

    o_l = jnp.moveaxis(lax.map(block, qb), 0, 1).reshape(bsz, n_lat, A_HEADS * HEAD_DIM)
    o_c = None
    if need_ctx:
        n_ctx = q_c.shape[1]
        qc = rmsnorm(heads(q_c, A_HEADS), q_norm).reshape(bsz, n_ctx, A_KV_HEADS, rep, HEAD_DIM)
        p = jax.nn.softmax(jnp.einsum('bqgrd,bkgd->bgrqk', qc, kc).astype(F32) * scale, axis=-1)
        o_c = jnp.einsum('bgrqk,bkgd->bqgrd', p.astype(vc.dtype), vc).reshape(bsz, n_ctx, A_HEADS * HEAD_DIM)
    return o_l, o_c


def window_gqa_sink(q_l, k_l, v_l, q_c, k_c, v_c, sink, rope, need_ctx):
    bsz, n_lat = q_l.shape[:2]
    rep = C_HEADS // C_KV_HEADS
    scale = HEAD_DIM ** -0.5
    ql = apply_axial_rope(heads(q_l, C_HEADS), rope)
    kl = apply_axial_rope(heads(k_l, C_KV_HEADS), rope)
    vl = heads(v_l, C_KV_HEADS)
    kc = heads(k_c, C_KV_HEADS)
    vc = heads(v_c, C_KV_HEADS)
    pad = ((0, 0), (WINDOW, WINDOW), (0, 0), (0, 0))
    kl_pad = jnp.pad(kl, pad)
    vl_pad = jnp.pad(vl, pad)
    span = BLOCK_Q + 2 * WINDOW
    rel = jnp.arange(span)[None, :] - jnp.arange(BLOCK_Q)[:, None]
    band = (rel >= 0) & (rel <= 2 * WINDOW)
    sink_logit = sink.astype(F32).reshape(1, C_KV_HEADS, rep, 1, 1)
    n_blk = n_lat // BLOCK_Q
    qb = jnp.moveaxis(ql.reshape(bsz, n_blk, BLOCK_Q, C_KV_HEADS, rep, HEAD_DIM), 1, 0)

    def block(args):
        i, q = args
        start = i * BLOCK_Q
        kb = lax.dynamic_slice_in_dim(kl_pad, start, span, axis=1)
        vb = lax.dynamic_slice_in_dim(vl_pad, start, span, axis=1)
        kpos = start - WINDOW + jnp.arange(span)
        valid = band & ((kpos >= 0) & (kpos < n_lat))[None, :]
        s_lat = jnp.where(valid, jnp.einsum('bqgrd,bkgd->bgrqk', q, kb).astype(F32) * scale, NEG_INF)
        s_ctx = jnp.einsum('bqgrd,bkgd->bgrqk', q, kc).astype(F32) * scale
        s_sink = jnp.broadcast_to(sink_logit, s_ctx.shape[:-1] + (1,))
        p_lat, p_ctx, _ = joint_softmax([s_lat, s_ctx, s_sink])
        return (jnp.einsum('bgrqk,bkgd->bqgrd', p_lat.astype(vb.dtype), vb)
                + jnp.einsum('bgrqk,bkgd->bqgrd', p_ctx.astype(vc.dtype), vc))

    o_l = jnp.moveaxis(lax.map(block, (jnp.arange(n_blk), qb)), 0, 1).reshape(bsz, n_lat, C_HEADS * HEAD_DIM)
    o_c = None
    if need_ctx:
        n_ctx = q_c.shape[1]
        qc = heads(q_c, C_HEADS).reshape(bsz, n_ctx, C_KV_HEADS, rep, HEAD_DIM)
        s_ctx = jnp.einsum('bqgrd,bkgd->bgrqk', qc, kc).astype(F32) * scale
        s_sink = jnp.broadcast_to(sink_logit, s_ctx.shape[:-1] + (1,))
        p_ctx, _ = joint_softmax([s_ctx, s_sink])
        o_c = jnp.einsum('bgrqk,bkgd->bqgrd', p_ctx.astype(vc.dtype), vc).reshape(bsz, n_ctx, C_HEADS * HEAD_DIM)
    return o_l, o_c


def token_shift_centred(f, mu_prev, mu_next):
    prev = jnp.pad(f[:, :-1], ((0, 0), (1, 0), (0, 0)))
    nxt = jnp.pad(f[:, 1:], ((0, 0), (0, 1), (0, 0)))
    return f + mu_prev * (prev - f) + mu_next * (nxt - f)


def rwkv7_scan(state0, r, w, k, v, kk, a, reverse):
    def step(S, inp):
        r_t, w_t, k_t, v_t, kk_t, a_t = inp
        sa = jnp.einsum('bhvk,bhk->bhv', S, kk_t)
        S = (S * w_t[:, :, None, :] - sa[..., None] * (kk_t * a_t)[:, :, None, :]
             + v_t[..., None] * k_t[:, :, None, :])
        return S, jnp.einsum('bhvk,bhk->bhv', S, r_t)
    xs = tuple(jnp.moveaxis(t, 1, 0) for t in (r, w, k, v, kk, a))
    s_fin, y = lax.scan(step, state0, xs, reverse=reverse)
    return jnp.moveaxis(y, 0, 1), s_fin


def rwkv7_bidir(f_l, f_c, mu_prev, mu_next, w0, w2, a0, a2, g2, k_k, k_a, r_k, ln_w, ln_b, need_ctx):
    def prep(f):
        bsz, T = f.shape[:2]
        f = token_shift_centred(f, mu_prev, mu_next)
        r, k, v, wd, ad, gd = split_cols(f, (B_WIDTH, B_WIDTH, B_WIDTH, 2 * B_DECAY_LORA, 2 * B_AAA_LORA, B_GATE_LORA))
        wd = wd.reshape(bsz, T, 2, B_DECAY_LORA)
        ad = ad.reshape(bsz, T, 2, B_AAA_LORA)
        wlog = (w0 + jnp.einsum('btdl,dlc->btdc', jnp.tanh(wd), w2)).astype(F32)
        decay = jnp.exp(-jnp.exp(-jax.nn.softplus(-wlog) - 0.5))
        a = jax.nn.sigmoid((a0 + jnp.einsum('btdl,dlc->btdc', ad, a2)).astype(F32))
        kk = (k * k_k).astype(F32).reshape(bsz, T, B_HEADS, HEAD_DIM)
        kk = kk * lax.rsqrt(jnp.sum(kk * kk, axis=-1, keepdims=True) + 1e-12)
        k_dir = k.astype(F32)[:, :, None] * (1.0 + (a - 1.0) * k_a.astype(F32))
        g = jax.nn.sigmoid(gd) @ g2
        hd2 = lambda t: t.reshape(bsz, T, 2, B_HEADS, HEAD_DIM)
        return (r.astype(F32).reshape(bsz, T, B_HEADS, HEAD_DIM), v.astype(F32).reshape(bsz, T, B_HEADS, HEAD_DIM),
                kk, hd2(decay), hd2(a), hd2(k_dir), g)

    rc, vc, kkc, wc, ac, kc, gc = prep(f_c)
    rl, vl, kkl, wl, al, kl, gl = prep(f_l)
    bsz = f_l.shape[0]
    zero = jnp.zeros((bsz, B_HEADS, HEAD_DIM, HEAD_DIM), F32)
    y_c, y_l = [], []
    for d, rev in ((0, False), (1, True)):
        yc_d, s_ctx = rwkv7_scan(zero, rc, wc[:, :, d], kc[:, :, d], vc, kkc, ac[:, :, d], rev)
        yl_d, _ = rwkv7_scan(s_ctx, rl, wl[:, :, d], kl[:, :, d], vl, kkl, al[:, :, d], rev)
        y_c.append(yc_d)
        y_l.append(yl_d)

    def post(y, r, k_dir, v, g):
        mu = jnp.mean(y, axis=-1, keepdims=True)
        var = jnp.mean(jnp.square(y - mu), axis=-1, keepdims=True)
        y = ((y - mu) * lax.rsqrt(var + B_GN_EPS) * ln_w.astype(F32).reshape(B_HEADS, HEAD_DIM)
             + ln_b.astype(F32).reshape(B_HEADS, HEAD_DIM))
        bonus = jnp.sum(r[:, :, None] * k_dir * r_k.astype(F32), axis=-1, keepdims=True)
        y = y + jnp.sum(bonus, axis=2) * v
        return y.reshape(y.shape[0], y.shape[1], B_WIDTH) * g

    o_l = post(y_l[0] + y_l[1], rl, kl, vl, gl)
    o_c = post(y_c[0] + y_c[1], rc, kc, vc, gc) if need_ctx else None
    return o_l, o_c


def depthwise_conv_centred(u, w, b):
    out = lax.conv_general_dilated(u, w[:, None, :].astype(u.dtype), window_strides=(1,),
                                   padding=((D_CONV // 2, D_CONV // 2),),
                                   dimension_numbers=('NWC', 'WIO', 'NWC'), feature_group_count=u.shape[-1])
    return out + b


def ssd_chunked(x, dt, A, Bm, Cm, state0):
    bsz, T = x.shape[:2]
    nc = T // CHUNK
    R = D_HEADS // D_GROUPS
    xc = x.astype(F32).reshape(bsz, nc, CHUNK, D_GROUPS, R, D_HEAD_DIM)
    Bc = Bm.astype(F32).reshape(bsz, nc, CHUNK, D_GROUPS, D_STATE)
    Cc = Cm.astype(F32).reshape(bsz, nc, CHUNK, D_GROUPS, D_STATE)
    dtc = dt.reshape(bsz, nc, CHUNK, D_GROUPS, R)
    acs = jnp.cumsum(dtc * A.astype(F32).reshape(D_GROUPS, R), axis=2)
    acs_t = jnp.moveaxis(acs, 2, -1)
    seg = acs_t[..., :, None] - acs_t[..., None, :]
    lower = jnp.tril(jnp.ones((CHUNK, CHUNK), dtype=bool))
    Lmat = jnp.exp(jnp.where(lower, seg, -jnp.inf))
    CB = jnp.einsum('bcign,bcjgn->bcgij', Cc, Bc)
    M = CB[:, :, :, None] * Lmat * jnp.moveaxis(dtc, 2, -1)[..., None, :]
    y_diag = jnp.einsum('bcgrij,bcjgrp->bcigrp', M, xc)
    decay_to_end = jnp.exp(acs[:, :, -1:] - acs)
    chunk_states = jnp.einsum('bcjgn,bcjgr,bcjgrp->bcgrpn', Bc, decay_to_end * dtc, xc)
    chunk_decay = jnp.exp(acs[:, :, -1])

    def carry(h, inp):
        st, dec = inp
        return h * dec[..., None, None] + st, h

    h0 = state0.reshape(bsz, D_GROUPS, R, D_HEAD_DIM, D_STATE)
    h_fin, h_prev = lax.scan(carry, h0, (jnp.moveaxis(chunk_states, 1, 0), jnp.moveaxis(chunk_decay, 1, 0)))
    h_prev = jnp.moveaxis(h_prev, 0, 1)
    y_off = jnp.einsum('bcign,bcgrpn->bcigrp', Cc, h_prev) * jnp.exp(acs)[..., None]
    y = (y_diag + y_off).reshape(bsz, T, D_HEADS, D_HEAD_DIM)
    return y, h_fin.reshape(bsz, D_HEADS, D_HEAD_DIM, D_STATE)


def gated_group_rmsnorm(y, z, w):
    bsz, T = y.shape[:2]
    gs = D_INNER // D_GROUPS
    u = y.reshape(bsz, T, D_GROUPS, gs) * jax.nn.silu(z.astype(F32)).reshape(bsz, T, D_GROUPS, gs)
    u = u * lax.rsqrt(jnp.mean(u * u, axis=-1, keepdims=True) + NORM_EPS)
    return u.reshape(bsz, T, D_INNER) * w.astype(F32)


def ssd_bidir(z_l, xbc_l, dtr_l, z_c, xbc_c, dtr_c, conv_w, conv_b, dt_bias, A_log, D_skip, norm_w, need_ctx):
    def prep(xbc, dt_raw):
        bsz, T = xbc.shape[:2]
        u = jax.nn.silu(depthwise_conv_centred(xbc, conv_w, conv_b))
        xs, bm, cm = split_cols(u, (D_INNER, D_GROUPS * D_STATE, D_GROUPS * D_STATE))
        dt = jax.nn.softplus(dt_raw.astype(F32).reshape(bsz, T, 2, D_HEADS) + dt_bias.astype(F32))
        return (xs.reshape(bsz, T, D_HEADS, D_HEAD_DIM), bm.reshape(bsz, T, D_GROUPS, D_STATE),
                cm.reshape(bsz, T, D_GROUPS, D_STATE), dt)

    A = -jnp.exp(A_log.astype(F32))
    xs_c, b_c, c_c, dt_c = prep(xbc_c, dtr_c)
    xs_l, b_l, c_l, dt_l = prep(xbc_l, dtr_l)
    bsz = xbc_l.shape[0]
    zero = jnp.zeros((bsz, D_HEADS, D_HEAD_DIM, D_STATE), F32)
    fl = lambda t: jnp.flip(t, axis=1)
    yc_f, s_f = ssd_chunked(xs_c, dt_c[:, :, 0], A[0], b_c, c_c, zero)
    yl_f, _ = ssd_chunked(xs_l, dt_l[:, :, 0], A[0], b_l, c_l, s_f)
    yc_b, s_b = ssd_chunked(fl(xs_c), fl(dt_c[:, :, 1]), A[1], fl(b_c), fl(c_c), zero)
    yl_b, _ = ssd_chunked(fl(xs_l), fl(dt_l[:, :, 1]), A[1], fl(b_l), fl(c_l), s_b)
    skip = D_skip.astype(F32)[:, None]
    o_l = gated_group_rmsnorm(yl_f + fl(yl_b) + skip * xs_l.astype(F32), z_l, norm_w)
    o_c = gated_group_rmsnorm(yc_f + fl(yc_b) + skip * xs_c.astype(F32), z_c, norm_w) if need_ctx else None
    return o_l, o_c


def attn_rwkv_mixers(h_lat, h_ctx, w_in, q_norm, k_norm, mu_prev, mu_next, w0, w2, a0, a2, g2,
                     k_k, k_a, r_k, ln_w, ln_b, rope, need_ctx):
    q_l, k_l, v_l, f_l = split_cols(h_lat @ w_in, AB_SIZES)
    q_c, k_c, v_c, f_c = split_cols(h_ctx @ w_in, AB_SIZES)
    a_l, a_c = global_gqa(q_l, k_l, v_l, q_c, k_c, v_c, q_norm, k_norm, rope, need_ctx)
    b_l, b_c = rwkv7_bidir(f_l, f_c, mu_prev, mu_next, w0, w2, a0, a2, g2, k_k, k_a, r_k, ln_w, ln_b, need_ctx)
    dt = h_lat.dtype
    o_l = jnp.concatenate([a_l.astype(dt), b_l.astype(dt)], axis=-1)
    o_c = jnp.concatenate([a_c.astype(dt), b_c.astype(dt)], axis=-1) if need_ctx else None
    return o_l, o_c


def swa_ssd_mixers(h_lat, h_ctx, w_in, sink, conv_w, conv_b, dt_bias, A_log, D_skip, norm_w, rope, need_ctx):
    q_l, k_l, v_l, z_l, xbc_l, dtr_l = split_cols(h_lat @ w_in, CD_SIZES)
    q_c, k_c, v_c, z_c, xbc_c, dtr_c = split_cols(h_ctx @ w_in, CD_SIZES)
    c_l, c_c = window_gqa_sink(q_l, k_l, v_l, q_c, k_c, v_c, sink, rope, need_ctx)
    d_l, d_c = ssd_bidir(z_l, xbc_l, dtr_l, z_c, xbc_c, dtr_c, conv_w, conv_b, dt_bias, A_log, D_skip, norm_w, need_ctx)
    dt = h_lat.dtype
    o_l = jnp.concatenate([c_l.astype(dt), d_l.astype(dt)], axis=-1)
    o_c = jnp.concatenate([c_c.astype(dt), d_c.astype(dt)], axis=-1) if need_ctx else None
    return o_l, o_c


def setup_inputs(seed: int = 0) -> dict:
    key = jax.random.key(seed)
    keys = jax.random.split(key, 40)
    counter = [0]

    def nxt():
        k = keys[counter[0]]
        counter[0] += 1
        return k

    def nrm(shape, scale):
        return jax.random.normal(nxt(), shape, F32) * scale

    def unif(shape, lo, hi):
        return jax.random.uniform(nxt(), shape, F32, lo, hi)

    D = D_MODEL
    NE, NO = N_EVEN, N_ODD
    out = {
        'x': nrm((BATCH, SEQ, D), 1.0),
        'c': nrm((BATCH, D), 1.0),
        'ctx': nrm((BATCH, CTX_LEN, D), 1.0),
        'c_ctx': nrm((D,), 1.0),
        'w_mod': nrm((DEPTH, D, 6 * D), 0.5 * D ** -0.5),
        'b_mod': nrm((DEPTH, 6 * D), 0.02),
        'norm_mix': 1.0 + nrm((DEPTH, D), 0.05),
        'norm_mlp': 1.0 + nrm((DEPTH, D), 0.05),
        'w_out': nrm((DEPTH, D, D), D ** -0.5),
        'mlp_w1': nrm((DEPTH, D, MLP_HIDDEN), D ** -0.5),
        'mlp_w2': nrm((DEPTH, MLP_HIDDEN, D), MLP_HIDDEN ** -0.5),
        'final_norm': 1.0 + nrm((D,), 0.05),
        'ab_w_in': nrm((NE, D, AB_IN), D ** -0.5),
        'a_q_norm': 1.0 + nrm((NE, HEAD_DIM), 0.05),
        'a_k_norm': 1.0 + nrm((NE, HEAD_DIM), 0.05),
        'b_mu_prev': unif((NE, B_IN), 0.0, 0.4),
        'b_mu_next': unif((NE, B_IN), 0.0, 0.4),
        'b_w0': unif((NE, 2, B_WIDTH), -4.0, 1.0),
        'b_w2': nrm((NE, 2, B_DECAY_LORA, B_WIDTH), 0.1),
        'b_a0': nrm((NE, 2, B_WIDTH), 0.5),
        'b_a2': nrm((NE, 2, B_AAA_LORA, B_WIDTH), 0.1),
        'b_g2': nrm((NE, B_GATE_LORA, B_WIDTH), B_GATE_LORA ** -0.5),
        'b_k_k': 0.85 + nrm((NE, B_WIDTH), 0.05),
        'b_k_a': 1.0 + nrm((NE, B_WIDTH), 0.05),
        'b_r_k': nrm((NE, B_HEADS, HEAD_DIM), 0.1),
        'b_ln_w': 1.0 + nrm((NE, B_WIDTH), 0.05),
        'b_ln_b': nrm((NE, B_WIDTH), 0.02),
        'cd_w_in': nrm((NO, D, CD_IN), D ** -0.5),
        'c_sink': nrm((NO, C_HEADS), 0.5),
        'd_conv_w': nrm((NO, D_CONV, D_XBC), D_CONV ** -0.5),
        'd_conv_b': nrm((NO, D_XBC), 0.02),
    }
    dt_init = jnp.exp(unif((NO, 2, D_HEADS), math.log(1e-3), math.log(1e-1)))
    out['d_dt_bias'] = dt_init + jnp.log(-jnp.expm1(-dt_init))
    out['d_A_log'] = jnp.log(unif((NO, 2, D_HEADS), 1.0, 16.0))
    out['d_D'] = 1.0 + nrm((NO, D_HEADS), 0.05)
    out['d_norm_w'] = 1.0 + nrm((NO, D_INNER), 0.05)
    return out


def reference(x, c, ctx, c_ctx, w_mod, b_mod, norm_mix, norm_mlp, w_out, mlp_w1, mlp_w2, final_norm,
              ab_w_in, a_q_norm, a_k_norm, b_mu_prev, b_mu_next, b_w0, b_w2, b_a0, b_a2, b_g2,
              b_k_k, b_k_a, b_r_k, b_ln_w, b_ln_b,
              cd_w_in, c_sink, d_conv_w, d_conv_b, d_dt_bias, d_A_log, d_D, d_norm_w):
    n_lat = x.shape[1]
    rope = axial_rope_tables(n_lat)
    x_lat, x_ctx = x, ctx
    for i in range(DEPTH):
        need_ctx = i < DEPTH - 1
        j = i // 2
        mod_lat = jnp.split((jax.nn.silu(c) @ w_mod[i] + b_mod[i])[:, None, :], 6, axis=-1)
        mod_ctx = jnp.split(jax.nn.silu(c_ctx) @ w_mod[i] + b_mod[i], 6, axis=-1)
        h_lat = modulate(rmsnorm(x_lat, norm_mix[i]), mod_lat[0], mod_lat[1])
        h_ctx = modulate(rmsnorm(x_ctx, norm_mix[i]), mod_ctx[0], mod_ctx[1])
        if i % 2 == 0:
            o_lat, o_ctx = attn_rwkv_mixers(h_lat, h_ctx, ab_w_in[j], a_q_norm[j], a_k_norm[j], b_mu_prev[j],
                                            b_mu_next[j], b_w0[j], b_w2[j], b_a0[j], b_a2[j], b_g2[j], b_k_k[j],
                                            b_k_a[j], b_r_k[j], b_ln_w[j], b_ln_b[j], rope, need_ctx)
        else:
            o_lat, o_ctx = swa_ssd_mixers(h_lat, h_ctx, cd_w_in[j], c_sink[j], d_conv_w[j], d_conv_b[j],
                                          d_dt_bias[j], d_A_log[j], d_D[j], d_norm_w[j], rope, need_ctx)
        x_lat = x_lat + mod_lat[2] * (o_lat @ w_out[i])
        h2 = modulate(rmsnorm(x_lat, norm_mlp[i]), mod_lat[3], mod_lat[4])
        x_lat = x_lat + mod_lat[5] * sq_relu_mlp(h2, mlp_w1[i], mlp_w2[i])
        if need_ctx:
            x_ctx = x_ctx + mod_ctx[2] * (o_ctx @ w_out[i])
            h2c = modulate(rmsnorm(x_ctx, norm_mlp[i]), mod_ctx[3], mod_ctx[4])
            x_ctx = x_ctx + mod_ctx[5] * sq_relu_mlp(h2c, mlp_w1[i], mlp_w2[i])
    return rmsnorm(x_lat, final_norm)
```

```python
import numpy as np
from contextlib import ExitStack
import concourse.bass as bass
import concourse.mybir as mybir
from concourse.bass_utils import run_bass_kernel_spmd

F32 = mybir.dt.float32
BF16 = mybir.dt.bfloat16
ALU = mybir.AluOpType
AF = mybir.ActivationFunctionType
AX = mybir.AxisListType

WRITE_KW = ("out", "ap", "accum_out")
NDMASEM = 6


class V:
    __slots__ = ("t", "key", "ap")

    def __init__(self, t, key, ap):
        self.t, self.key, self.ap = t, key, ap


class T:
    def __init__(self, name, handle, is_ap=False):
        self.name = name
        self.h = handle
        self.is_ap = is_ap
        self.w = {}
        self.r = {}

    def __getitem__(self, idx):
        return V(self, None, self.h[idx])

    def k(self, key):
        return _Keyed(self, key)

    def v(self, ap, key=None):
        return V(self, key, ap)


class _Sub:
    def __init__(self, t, ap):
        self.t, self.ap = t, ap

    def __getitem__(self, idx):
        return V(self.t, None, self.ap[idx])


class _Keyed:
    def __init__(self, t, key):
        self.t, self.key = t, key

    def __getitem__(self, idx):
        return V(self.t, self.key, self.t.h[idx])


class Prog:
    ENG = ("pe", "act", "dve", "pool", "sp")

    def __init__(self, nc):
        self.nc = nc
        self.es = ExitStack()
        self.ops = {e: [] for e in self.ENG}
        self.cnt = {e: 0 for e in self.ENG}
        self.dcnt = {e: 0 for e in self.ENG}
        self.sems = {}
        self.known = {e: {} for e in self.ENG}
        for e in self.ENG:
            self.sems[e] = self.es.enter_context(nc.semaphore("s_" + e))
            for j in range(NDMASEM):
                self.sems[(e, j)] = self.es.enter_context(nc.semaphore("d_%s%d" % (e, j)))
        self.n_psum = 0

    def sb(self, name, shape, dtype=F32):
        h = self.es.enter_context(self.nc.sbuf_tensor(name, list(shape), dtype))
        return T(name, h)

    def ps(self, name, shape, dtype=F32):
        h = self.es.enter_context(self.nc.psum_tensor(name, list(shape), dtype))
        return T(name, h)

    def dram(self, name, shape, dtype=F32, kind="Internal"):
        h = self.nc.dram_tensor(name, list(shape), dtype, kind=kind).ap()
        return T(name, h, True)

    def _conflicts(self, d, key):
        if key is None:
            for kk, ev in d.items():
                yield kk, ev
        else:
            if key in d:
                yield key, d[key]
            if None in d:
                yield None, d[None]

    def barrier(self):
        evs = []
        for e in self.ENG:
            if self.cnt[e]:
                evs.append((e, self.cnt[e]))
            for j in range(NDMASEM):
                n = self.dcnt[e]
                k = (n - j + NDMASEM - 1) // NDMASEM if n > j else 0
                if k:
                    evs.append(((e, j), 16 * k))
        for e in self.ENG:
            kn = self.known[e]
            wl = []
            for s, v in evs:
                if s == e and e == "pe":
                    continue
                if kn.get(s, 0) >= v:
                    continue
                kn[s] = v
                wl.append((s, v))
            if wl:
                self.ops[e].append((wl, None, None, None, None, None))

    def _issue(self, eng, name, args, kw, is_dma):
        kw = dict(kw)
        reads, writes = list(kw.pop("rd", [])), list(kw.pop("wr", []))
        nargs = []
        for i, a in enumerate(args):
            if isinstance(a, V):
                (writes if i == 0 else reads).append(a)
                nargs.append(a.ap)
            else:
                nargs.append(a)
        nkw = {}
        for k_, a in kw.items():
            if isinstance(a, V):
                (writes if k_ in WRITE_KW else reads).append(a)
                nkw[k_] = a.ap
            else:
                nkw[k_] = a
        waits = {}

        def need(ev):
            s, val = ev
            if waits.get(s, 0) < val:
                waits[s] = val

        for v in reads:
            for _, ev in self._conflicts(v.t.w, v.key):
                need(ev)
        for v in writes:
            for _, ev in self._conflicts(v.t.w, v.key):
                need(ev)
            for _, evs in self._conflicts(v.t.r, v.key):
                for ev in evs:
                    need(ev)
        if is_dma:
            n = self.dcnt[eng]
            self.dcnt[eng] += 1
            j = n % NDMASEM
            semk = (eng, j)
            val = 16 * (n // NDMASEM + 1)
            inc = 16
            if n >= NDMASEM:
                need((semk, val - 16))
        else:
            self.cnt[eng] += 1
            semk = eng
            val = self.cnt[eng]
            inc = 1
        ev = (semk, val)
        kn = self.known[eng]
        wl = []
        for s, val_ in waits.items():
            if s == eng and eng == "pe":
                continue
            if kn.get(s, 0) >= val_:
                continue
            kn[s] = val_
            wl.append((s, val_))
        for v in writes:
            t, key = v.t, v.key
            if key is None:
                t.w = {None: ev}
                t.r = {}
            else:
                t.w[key] = ev
                t.r[key] = []
        for v in reads:
            t, key = v.t, v.key
            t.r.setdefault(key, []).append(ev)
            if len(t.r[key]) > 12:
                d = {}
                for s, vv in t.r[key]:
                    if d.get(s, 0) < vv:
                        d[s] = vv
                t.r[key] = list(d.items())
        self.ops[eng].append((wl, name, nargs, nkw, semk, inc))
        return ev

    def I(self, eng, name, *args, **kw):
        return self._issue(eng, name, args, kw, False)

    def dma(self, eng, out, in_, **kw):
        return self._issue(eng, "dma_start", (), dict(out=out, in_=in_, **kw), True)

    def pe(self, name, *a, **k):
        return self.I("pe", name, *a, **k)

    def act(self, name, *a, **k):
        return self.I("act", name, *a, **k)

    def dve(self, name, *a, **k):
        return self.I("dve", name, *a, **k)

    def pool(self, name, *a, **k):
        return self.I("pool", name, *a, **k)

    def emit(self, final_events):
        nc = self.nc
        engobj = {"pe": "tensor", "act": "scalar", "dve": "vector", "pool": "gpsimd", "sp": "sync"}
        with nc.Block() as block:
            for e in self.ENG:
                ops = self.ops[e]
                if e == "sp":
                    ops = list(ops)
                    fin = {}
                    for (s, v) in final_events:
                        fin[s] = max(fin.get(s, 0), v)
                    ops.append(([(s, v) for s, v in fin.items()], None, None, None, None, None))
                if not ops:
                    continue

                def body(eng, ops=ops):
                    for (wl, name, nargs, nkw, semk, inc) in ops:
                        for s, v in wl:
                            eng.wait_ge(self.sems[s], v)
                        if name is None:
                            continue
                        ins = getattr(eng, name)(*nargs, **nkw)
                        ins.then_inc(self.sems[semk], inc)
                getattr(block, engobj[e])(body)

    def close(self):
        self.es.close()

D = 1024
EPS = 1e-6
WDEC = 0.6065306597126334


class Cfg:
    def __init__(self, NB=4, TC=256, TL=2048, DEPTH=4, GRID_W=64, dbg=None):
        self.NB, self.TC, self.TL, self.DEPTH, self.GRID_W = NB, TC, TL, DEPTH, GRID_W
        self.T = TC + TL
        self.NT = self.T // 128
        self.NTC = TC // 128
        self.R = NB + 1
        self.dbg = dbg or {}


def make_consts(cfg):
    c = {}
    idx = np.arange(128)
    s, t = idx[:, None], idx[None, :]
    c["ident"] = np.eye(128, dtype=np.float32)
    c["bones"] = ((s // 64) == (t // 64)).astype(np.float32)
    c["ones"] = np.ones((128, 128), np.float32)
    selI = np.zeros((128, 128), np.float32)
    selR = np.zeros((128, 128), np.float32)
    for hp in range(2):
        for d in range(64):
            selI[hp * 64 + d, hp * 64 + d] = 1.0
            q = d % 32
            if q < 16:
                selR[hp * 64 + d + 16, hp * 64 + d] = -1.0
            else:
                selR[hp * 64 + d - 16, hp * 64 + d] = 1.0
    c["selI"] = selI
    c["selR"] = selR
    su = (s < t).astype(np.float32); iu = (s <= t).astype(np.float32)
    sl = (s > t).astype(np.float32); il = (s >= t).astype(np.float32)
    c["m_su"] = su; c["m_iu"] = iu
    c["m_sl"] = sl; c["m_il"] = il
    c["nm_su"] = -c["m_su"]; c["nm_sl"] = -c["m_sl"]
    c["nm_iu"] = -c["m_iu"]; c["nm_il"] = -c["m_il"]
    c["triF"] = (-WDEC) * iu; c["triB"] = (-WDEC) * il; c["allS"] = (-WDEC) * np.ones((128, 128), np.float32)
    c["neg_iu"] = ((1.0 - iu) * (-30000.0)).astype(np.float32)
    c["neg_il"] = ((1.0 - il) * (-30000.0)).astype(np.float32)
    c["hsel"] = np.zeros((128, 2), np.float32); c["hsel"][:64, 0] = 1; c["hsel"][64:, 1] = 1
    offs = {}
    cols = []
    o = 0
    for k_, v in c.items():
        offs[k_] = (o, v.shape[1]); cols.append(v.astype(np.float32)); o += v.shape[1]
    arr = np.concatenate(cols, axis=1)
    TL, TC, GW = cfg.TL, cfg.TC, cfg.GRID_W
    tl = np.arange(TL)
    row = (tl // GW).astype(np.float32); col = (tl % GW).astype(np.float32)
    inv = (10000.0 ** (-np.arange(16, dtype=np.float32) / 16)).astype(np.float32)
    ang_r = row[:, None] * inv[None, :]; ang_c = col[:, None] * inv[None, :]
    ang = np.concatenate([ang_r, ang_r, ang_c, ang_c], axis=1)
    rope = np.zeros((64, 2, cfg.T), np.float32)
    rope[:, 0, :TC] = 1.0
    rope[:, 0, TC:] = np.cos(ang).T
    rope[:, 1, TC:] = np.sin(ang).T
    return arr, offs, rope


def build(cfg):
    NB, TC, TL, DEPTH, T_, NT, NTC, R = cfg.NB, cfg.TC, cfg.TL, cfg.DEPTH, cfg.T, cfg.NT, cfg.NTC, cfg.R
    NE, NO = (DEPTH + 1) // 2, DEPTH // 2
    nc = bass.Bass("TRN2", target_bir_lowering=False)
    P = Prog(nc)
    carr, coffs, _ = make_consts(cfg)
    NCC = carr.shape[1]
    A = {}

    def inp(name, shape):
        A[name] = nc.dram_tensor(name, list(shape), F32, kind="ExternalInput").ap()
        return A[name]

    inp("x", [NB, TL, D]); inp("ctx", [NB, TC, D]); inp("c5T", [D, R])
    inp("w_mod", [DEPTH, D, 6 * D]); inp("b_mod", [DEPTH, 6 * D]); inp("norm_mix", [DEPTH, D]); inp("norm_mlp", [DEPTH, D])
    inp("w_out", [DEPTH, D, D]); inp("mlp_w1", [DEPTH, D, 4 * D]); inp("mlp_w2", [DEPTH, 4 * D, D]); inp("final_norm", [1, D])
    inp("ab_w_in", [NE, D, 2688]); inp("a_q_norm", [NE, 64]); inp("a_k_norm", [NE, 64])
    inp("b_mu_prev", [NE, 1920]); inp("b_mu_next", [NE, 1920]); inp("b_w0", [NE, 1024]); inp("b_w2", [NE, 128, 512])
    inp("b_a0", [NE, 1024]); inp("b_a2", [NE, 128, 512]); inp("b_g2", [NE, 128, 512]); inp("b_k_k", [NE, 512]); inp("b_k_a", [NE, 512])
    inp("b_r_k", [NE, 512]); inp("b_ln_w", [NE, 512]); inp("b_ln_b", [NE, 512])
    inp("b_mu_prev_col", [NE, 128, 15]); inp("b_mu_next_col", [NE, 128, 15]); inp("b_w0_col", [NE, 128, 8]); inp("b_a0_col", [NE, 128, 8])
    inp("b_k_k_col", [NE, 128, 4]); inp("b_k_a_col", [NE, 128, 4]); inp("b_r_k_col", [NE, 128, 4])
    if NO:
        inp("cd_w_in", [NO, D, 2320]); inp("c_sink", [NO, 8]); inp("d_conv_w", [NO, 5, 1024]); inp("d_conv_b", [NO, 1024])
        inp("d_dt_bias", [NO, 16]); inp("d_A_log", [NO, 16]); inp("d_D", [NO, 8]); inp("d_norm_w", [NO, 512])
        inp("d_conv_wT", [NO, 128, 8, 5]); inp("d_conv_b_col", [NO, 128, 8])
    inp("consts", [128, NCC]); inp("rope", [64, 2, T_])
    if "O_in" in cfg.dbg:
        inp("O_in", [DEPTH, NB, T_, D])
    out_ap = nc.dram_tensor("out", [NB, TL, D], F32, kind="ExternalOutput").ap()
    dbg_out = {}
    for nm, shp in cfg.dbg.get("outs", {}).items():
        dbg_out[nm] = nc.dram_tensor(nm, list(shp), F32, kind="ExternalOutput").ap()

    def scratch(name, shape):
        return nc.dram_tensor(name, list(shape), F32, kind="Internal").ap()

    MOD = scratch("MOD", [DEPTH, R, 6 * D])
    X = scratch("X", [T_, D])
    PT = scratch("PT", [2688, T_])
    O = scratch("O", [T_, D])

    CT = P.sb("consts_sb", [128, NCC])
    PS = [P.ps("ps%d" % j, [128, 512]) for j in range(8)]
    st = {"ps": 0, "q": 0, "ev": 0}

    def ps():
        st["ps"] = (st["ps"] + 1) % 8
        return PS[st["ps"]]

    def q():
        st["q"] ^= 1
        return "sp" if st["q"] else "pool"

    def C(name, rows=slice(0, 128), lo=0, n=None):
        o, w = coffs[name]
        n = w - lo if n is None else n
        return CT[rows, o + lo:o + lo + n]

    def evac(dst, src):
        st["ev"] ^= 1
        if st["ev"]:
            P.act("copy", out=dst, in_=src)
        else:
            P.dve("tensor_copy", out=dst, in_=src)

    def sbt(es, name, shape):
        st["uid"] = st.get("uid", 0) + 1
        name = "%s_%d" % (name, st["uid"])
        h = es.enter_context(nc.sbuf_tensor(name, list(shape), F32))
        return T(name, h)

    def r3(v, a):
        return V(v.t, v.key, v.ap.rearrange("p (a b) -> p a b", a=a))

    def bc(t_, ap, shape, axis):
        return V(t_, None, ap.unsqueeze(axis).to_broadcast(list(shape)))

    def bcast_row(ap2d, n=128):
        return ap2d.partition_broadcast(n)

    def colvec(ap1d_row):
        return ap1d_row.rearrange("o (c p) -> p (o c)", p=128)

    P.dma("sp", CT[:], A["consts"][:, :])
    ident = C("ident")

    def transpose(dst, src, npart_in, nfree_in, pp=None, off=0):
        pp = pp or ps()
        P.pe("transpose", out=pp[0:nfree_in, off:off + npart_in], in_=src, identity=C("ident", slice(0, npart_in), 0, npart_in))
        if dst is not None:
            evac(dst, pp[0:nfree_in, off:off + npart_in])
        return pp

    def rms(es_tiles, xt, Gt, St, outv):
        sq, ss, rs = es_tiles
        P.pool("memset", ap=ss[:], constant=0.0)
        P.act("activation", out=sq[:], in_=xt, func=AF.Square, accum_out=ss[:])
        P.dve("tensor_scalar", out=rs[:], in0=ss[:], scalar1=1.0 / D, scalar2=EPS, op0=ALU.mult, op1=ALU.add)
        P.act("activation", out=rs[:], in_=rs[:], func=AF.Sqrt)
        P.dve("reciprocal", out=rs[:], in_=rs[:])
        P.dve("scalar_tensor_tensor", out=outv, in0=xt, scalar=rs[:, 0:1], in1=Gt, op0=ALU.mult, op1=ALU.mult)
        if St is not None:
            P.dve("tensor_tensor", out=outv, in0=outv, in1=St, op=ALU.add)

    def phase_mod():
        with ExitStack() as es:
            c5 = sbt(es, "c5", [128, 8, R]); sc5 = sbt(es, "sc5", [128, 8, R])
            wb = [sbt(es, "wmb%d" % j, [128, 8, 512]) for j in range(2)]
            bt = [sbt(es, "bmb%d" % j, [R, 512]) for j in range(2)]
            res = [sbt(es, "mres%d" % j, [R, 512]) for j in range(2)]
            P.dma("sp", c5[:], A["c5T"].rearrange("(kc p) r -> p kc r", p=128))
            P.act("activation", out=sc5[:], in_=c5[:], func=AF.Silu)
            k = 0
            for i in range(DEPTH):
                for n in range(12):
                    w_, b_, r_ = wb[k % 2], bt[k % 2], res[k % 2]
                    P.dma(q(), w_[:], A["w_mod"][i][:, n * 512:(n + 1) * 512].rearrange("(kc p) n -> p kc n", p=128))
                    P.dma(q(), b_[:], A["b_mod"][i:i + 1, n * 512:(n + 1) * 512].partition_broadcast(R))
                    pp = ps()
                    for kc in range(8):
                        P.pe("matmul", out=pp[0:R, :], lhsT=sc5[:, kc, :], rhs=w_[:, kc, :], start=(kc == 0), stop=(kc == 7))
                    P.dve("tensor_tensor", out=r_[:], in0=pp[0:R, :], in1=b_[:], op=ALU.add)
                    P.dma(q(), MOD[i, :, n * 512:(n + 1) * 512], r_[:])
                    k += 1
            P.barrier()

    def phase_inproj(i, b, w_in, NF, need_ctx):
        with ExitStack() as es:
            hT = sbt(es, "hT", [128, 8, T_])
            xt = [sbt(es, "xt%d" % j, [128, D]) for j in range(2)]
            h = [sbt(es, "h%d" % j, [128, D]) for j in range(2)]
            sq = sbt(es, "sq", [128, D]); ss = sbt(es, "ss", [128, 1]); rs = sbt(es, "rs", [128, 1])
            G = {s_: sbt(es, "G" + s_, [128, D]) for s_ in "cl"}
            S = {s_: sbt(es, "S" + s_, [128, D]) for s_ in "cl"}
            nw = sbt(es, "nw", [128, D])
            Wb = [sbt(es, "Wb%d" % j, [128, 8, 512]) for j in range(2)]
            stg = [sbt(es, "stg%d" % j, [128, 512]) for j in range(2)]
            P.dma(q(), nw[:], A["norm_mix"][i:i + 1, :].partition_broadcast(128))
            for s_, r_ in (("c", NB), ("l", b)):
                P.dma(q(), sq[:], MOD[i, r_:r_ + 1, D:2 * D].partition_broadcast(128))
                P.dma(q(), S[s_][:], MOD[i, r_:r_ + 1, 0:D].partition_broadcast(128))
                P.dve("scalar_tensor_tensor", out=G[s_][:], in0=sq[:], scalar=1.0, in1=nw[:], op0=ALU.add, op1=ALU.mult)
            for t in range(NT):
                s_ = "c" if t < NTC else "l"
                x_ = xt[t % 2]; h_ = h[t % 2]
                P.dma(q(), x_[:], X[t * 128:(t + 1) * 128, :])
                rms((sq, ss, rs), x_[:], G[s_][:], S[s_][:], h_[:])
                for half in range(2):
                    pp = ps()
                    for j in range(4):
                        kc = half * 4 + j
                        P.pe("transpose", out=pp[:, j * 128:(j + 1) * 128], in_=h_[:, kc * 128:(kc + 1) * 128], identity=ident)
                    evac(hT[:, half * 4:(half + 1) * 4, t * 128:(t + 1) * 128], r3(pp[:, :], 4))
            k = 0
            for n0 in range(0, NF, 512):
                nn = min(512, NF - n0)
                W_ = Wb[(n0 // 512) % 2]
                P.dma(q(), W_[:, :, 0:nn], w_in[:, n0:n0 + nn].rearrange("(kc p) n -> p kc n", p=128))
                for f0 in range(0, nn, 128):
                    m = min(128, nn - f0)
                    for t0 in range(0, T_, 512):
                        n = min(512, T_ - t0)
                        pp = ps()
                        for kc in range(8):
                            P.pe("matmul", out=pp[0:m, 0:n], lhsT=W_[:, kc, f0:f0 + m], rhs=hT[:, kc, t0:t0 + n], start=(kc == 0), stop=(kc == 7))
                        s2 = stg[k % 2]; k += 1
                        evac(s2[0:m, 0:n], pp[0:m, 0:n])
                        P.dma(q(), PT[n0 + f0:n0 + f0 + m, t0:t0 + n], s2[0:m, 0:n])
            P.barrier()

    def phase_mlp(i, b, need_ctx):
        with ExitStack() as es:
            hid = sbt(es, "hid", [128, 32, 512])
            Wout = V(hid, None, hid.h[:, 0:16, :].rearrange("p (a c) n -> p a (c n)", a=8))
            xb = sbt(es, "xb", [128, 4, D]); ot = sbt(es, "ot", [128, D]); oT = sbt(es, "oT", [128, 8, 128])
            h2 = sbt(es, "h2", [128, D]); sq = sbt(es, "sq2", [128, D]); ss = sbt(es, "ss2", [128, 1]); rs = sbt(es, "rs2", [128, 1])
            h2T = sbt(es, "h2T", [128, 8, 512])
            W1b = [sbt(es, "W1b%d" % j, [128, 8, 512]) for j in range(2)]
            W2r = [sbt(es, "W2r%d" % j, [128, D]) for j in range(4)]
            g1 = sbt(es, "g1", [128, D]); G2 = sbt(es, "G2", [128, D]); S2 = sbt(es, "S2", [128, D]); g2 = sbt(es, "g2", [128, D])
            blocks = []
            if need_ctx:
                for t0 in range(0, NTC, 4):
                    blocks.append(("c", list(range(t0, min(NTC, t0 + 4)))))
            for t0 in range(NTC, NT, 4):
                blocks.append(("l", list(range(t0, min(NT, t0 + 4)))))
            cur = None
            for s_, tiles in blocks:
                if s_ != cur:
                    cur = s_
                    r_ = NB if s_ == "c" else b
                    P.dma(q(), g1[:], MOD[i, r_:r_ + 1, 2 * D:3 * D].partition_broadcast(128))
                    P.dma(q(), S2[:], MOD[i, r_:r_ + 1, 3 * D:4 * D].partition_broadcast(128))
                    P.dma(q(), sq[:], MOD[i, r_:r_ + 1, 4 * D:5 * D].partition_broadcast(128))
                    P.dma(q(), g2[:], MOD[i, r_:r_ + 1, 5 * D:6 * D].partition_broadcast(128))
                    P.dma(q(), h2[:], A["norm_mlp"][i:i + 1, :].partition_broadcast(128))
                    P.dve("scalar_tensor_tensor", out=G2[:], in0=sq[:], scalar=1.0, in1=h2[:], op0=ALU.add, op1=ALU.mult)
                ntl = len(tiles)
                P.dma(q(), Wout, A["w_out"][i].rearrange("(kc p) n -> p kc n", p=128))
                for j, t in enumerate(tiles):
                    P.dma(q(), xb[:, j, :], X[t * 128:(t + 1) * 128, :])
                    P.dma(q(), ot[:], O[t * 128:(t + 1) * 128, :])
                    for half in range(2):
                        pp = ps()
                        for jj in range(4):
                            kc = half * 4 + jj
                            P.pe("transpose", out=pp[:, jj * 128:(jj + 1) * 128], in_=ot[:, kc * 128:(kc + 1) * 128], identity=ident)
                        evac(oT[:, half * 4:(half + 1) * 4, :], r3(pp[:, :], 4))
                    for nh in range(2):
                        pp = ps()
                        for kc in range(8):
                            P.pe("matmul", out=pp[:, :], lhsT=oT[:, kc, :], rhs=V(hid, None, Wout.ap[:, kc, nh * 512:(nh + 1) * 512]), start=(kc == 0), stop=(kc == 7))
                        P.dve("tensor_tensor", out=sq[:, 0:512], in0=pp[:, :], in1=g1[:, nh * 512:(nh + 1) * 512], op=ALU.mult)
                        P.dve("tensor_tensor", out=xb[:, j, nh * 512:(nh + 1) * 512], in0=xb[:, j, nh * 512:(nh + 1) * 512], in1=sq[:, 0:512], op=ALU.add)
                    rms((sq, ss, rs), xb[:, j, :], G2[:], S2[:], h2[:])
                    for half in range(2):
                        pp = ps()
                        for jj in range(4):
                            kc = half * 4 + jj
                            P.pe("transpose", out=pp[:, jj * 128:(jj + 1) * 128], in_=h2[:, kc * 128:(kc + 1) * 128], identity=ident)
                        evac(h2T[:, half * 4:(half + 1) * 4, j * 128:(j + 1) * 128], r3(pp[:, :], 4))
                ntok = ntl * 128
                for n8 in range(8):
                    W_ = W1b[n8 % 2]
                    P.dma(q(), W_[:], A["mlp_w1"][i][:, n8 * 512:(n8 + 1) * 512].rearrange("(kc p) n -> p kc n", p=128))
                    for f4 in range(4):
                        fc = n8 * 4 + f4
                        pp = ps()
                        for kc in range(8):
                            P.pe("matmul", out=pp[:, 0:ntok], lhsT=W_[:, kc, f4 * 128:(f4 + 1) * 128], rhs=h2T[:, kc, 0:ntok], start=(kc == 0), stop=(kc == 7))
                        P.act("activation", out=hid[:, fc, 0:ntok], in_=pp[:, 0:ntok], func=AF.Relu)
                        P.pool("tensor_tensor", out=hid[:, fc, 0:ntok], in0=hid[:, fc, 0:ntok], in1=hid[:, fc, 0:ntok], op=ALU.mult)
                for fc in range(32):
                    W_ = W2r[fc % 4]
                    P.dma(q(), W_[:], A["mlp_w2"][i][fc * 128:(fc + 1) * 128, :])
                    for j in range(ntl):
                        for nh in range(2):
                            P.pe("matmul", out=PS[j * 2 + nh][:, :], lhsT=hid[:, fc, j * 128:(j + 1) * 128], rhs=W_[:, nh * 512:(nh + 1) * 512], start=(fc == 0), stop=(fc == 31))
                for j, t in enumerate(tiles):
                    for nh in range(2):
                        P.dve("tensor_tensor", out=sq[:, nh * 512:(nh + 1) * 512], in0=PS[j * 2 + nh][:, :], in1=g2[:, nh * 512:(nh + 1) * 512], op=ALU.mult)
                    P.pool("tensor_tensor", out=xb[:, j, :], in0=xb[:, j, :], in1=sq[:], op=ALU.add)
                    P.dma(q(), X[t * 128:(t + 1) * 128, :], xb[:, j, :])
            P.barrier()

    def phase_final(b):
        with ExitStack() as es:
            xt = [sbt(es, "fx%d" % j, [128, D]) for j in range(2)]
            yo = [sbt(es, "fy%d" % j, [128, D]) for j in range(2)]
            sq = sbt(es, "fsq", [128, D]); ss = sbt(es, "fss", [128, 1]); rs = sbt(es, "frs", [128, 1])
            Gf = sbt(es, "Gf", [128, D])
            P.dma(q(), Gf[:], A["final_norm"][0:1, :].partition_broadcast(128))
            evs = []
            for t in range(NTC, NT):
                x_, y_ = xt[t % 2], yo[t % 2]
                P.dma(q(), x_[:], X[t * 128:(t + 1) * 128, :])
                rms((sq, ss, rs), x_[:], Gf[:], None, y_[:])
                evs.append(P.dma(q(), out_ap[b, (t - NTC) * 128:(t - NTC + 1) * 128, :], y_[:]))
            P.barrier()
            return evs

    def attn(i, j, b, need_ctx, kind):
        with ExitStack() as es:
            QT = sbt(es, "QT", [64, 8, T_]); KT = sbt(es, "KT", [64, 2, T_])
            Vtm = sbt(es, "Vtm", [128, NT, 2, 65])
            Cc = sbt(es, "Cc", [64, T_]); Ss = sbt(es, "Ss", [64, T_])
            ch = [sbt(es, "ch%d" % k_, [128, 512]) for k_ in range(2)]
            sq = sbt(es, "asq", [128, 512]); rstd = sbt(es, "arstd", [128, 512]); qn = sbt(es, "aqn", [128, 512])
            t1 = sbt(es, "at1", [64, 512]); t2 = sbt(es, "at2", [64, 512])
            ptb = [sbt(es, "ptb%d" % k_, [128, 512]) for k_ in range(2)]
            osb = [sbt(es, "osb%d" % k_, [128, 512]) for k_ in range(2)]
            dn = sbt(es, "dn", [128, 4]); nwq = sbt(es, "nwq", [128, 1]); nwk = sbt(es, "nwk", [128, 1])
            esink = sbt(es, "esink", [128, 8]); vch = sbt(es, "vch", [128, T_])
            P.dma(q(), Cc[:], A["rope"][:, 0, :]); P.dma(q(), Ss[:], A["rope"][:, 1, :])
            P.pool("memset", ap=Vtm[:], constant=1.0)
            if kind == "a":
                for hp in range(2):
                    P.dma(q(), nwq[hp * 64:(hp + 1) * 64, :], A["a_q_norm"][j:j + 1, :].rearrange("o d -> d o"))
                    P.dma(q(), nwk[hp * 64:(hp + 1) * 64, :], A["a_k_norm"][j:j + 1, :].rearrange("o d -> d o"))
            else:
                P.dma(q(), esink[:], A["c_sink"][j:j + 1, :].partition_broadcast(128))
                P.act("activation", out=esink[:], in_=esink[:], func=AF.Exp)
            k_ = 0
            for c in range(5):
                for t0 in range(0, T_, 512):
                    n = min(512, T_ - t0)
                    ch_ = ch[k_ % 2]; k_ += 1
                    P.dma(q(), ch_[:, 0:n], PT[c * 128:(c + 1) * 128, t0:t0 + n])
                    if kind == "a":
                        P.act("activation", out=sq[:, 0:n], in_=ch_[:, 0:n], func=AF.Square)
                        pp = ps()
                        P.pe("matmul", out=pp[:, 0:n], lhsT=C("bones"), rhs=sq[:, 0:n], start=True, stop=True)
                        P.dve("tensor_scalar", out=rstd[:, 0:n], in0=pp[:, 0:n], scalar1=1.0 / 64, scalar2=EPS, op0=ALU.mult, op1=ALU.add)
                        P.act("activation", out=rstd[:, 0:n], in_=rstd[:, 0:n], func=AF.Sqrt)
                        P.dve("reciprocal", out=rstd[:, 0:n], in_=rstd[:, 0:n])
                        nw_ = nwq if c < 4 else nwk
                        P.dve("scalar_tensor_tensor", out=qn[:, 0:n], in0=ch_[:, 0:n], scalar=nw_[:, 0:1], in1=rstd[:, 0:n], op0=ALU.mult, op1=ALU.mult)
                        src = qn
                    else:
                        src = ch_
                    for hp in range(2):
                        p1 = ps(); p2 = ps()
                        P.pe("matmul", out=p1[0:64, 0:n], lhsT=C("selI", lo=hp * 64, n=64), rhs=src[:, 0:n], start=True, stop=True)
                        P.pe("matmul", out=p2[0:64, 0:n], lhsT=C("selR", lo=hp * 64, n=64), rhs=src[:, 0:n], start=True, stop=True)
                        P.dve("tensor_tensor", out=t1[:, 0:n], in0=p1[0:64, 0:n], in1=Cc[:, t0:t0 + n], op=ALU.mult)
                        P.dve("tensor_tensor", out=t2[:, 0:n], in0=p2[0:64, 0:n], in1=Ss[:, t0:t0 + n], op=ALU.mult)
                        dst = QT[:, 2 * c + hp, t0:t0 + n] if c < 4 else KT[:, hp, t0:t0 + n]
                        P.pool("tensor_tensor", out=dst, in0=t1[:, 0:n], in1=t2[:, 0:n], op=ALU.add)
            P.dma(q(), vch[:], PT[640:768, :])
            for t in range(NT):
                pp = ps()
                P.pe("transpose", out=pp[:, 0:128], in_=vch[:, t * 128:(t + 1) * 128], identity=ident)
                evac(Vtm[:, t, :, 0:64], r3(pp[:, 0:128], 2))
            qblocks = list(range(NTC, NT)) + (list(range(NTC)) if need_ctx else [])
            kk_ = 0
            for qt in qblocks:
                if qt < NTC:
                    keys = [(kt, None) for kt in range(NTC)]
                elif kind == "a":
                    keys = [(kt, None) for kt in range(NT)]
                else:
                    keys = [(kt, None) for kt in range(NTC)]
                    if qt - 1 >= NTC:
                        keys.append((qt - 1, "m_il"))
                    keys.append((qt, None))
                    if qt + 1 < NT:
                        keys.append((qt + 1, "m_iu"))
                o_ = osb[kk_ % 2]; kk_ += 1
                for g in range(2):
                    acc = PS[g]
                    for idx, (kt, m) in enumerate(keys):
                        st["aps"] = st.get("aps", 0) + 1
                        sp_ = PS[2 + st["aps"] % 6]
                        P.pe("matmul", out=r3(sp_[:, :], 4), lhsT=KT[:, g, kt * 128:(kt + 1) * 128], rhs=QT[:, 4 * g:4 * g + 4, qt * 128:(qt + 1) * 128], start=True, stop=True)
                        pt_ = ptb[(idx) % 2]
                        P.act("activation", out=pt_[:], in_=sp_[:, :], func=AF.Exp, scale=0.125)
                        if m:
                            P.dve("tensor_tensor", out=r3(pt_[:], 4), in0=r3(pt_[:], 4), in1=bc(CT, C(m).ap, [128, 4, 128], 1), op=ALU.mult)
                        for r in range(4):
                            P.pe("matmul", out=acc[:, r * 65:(r + 1) * 65], lhsT=pt_[:, r * 128:(r + 1) * 128], rhs=Vtm[:, kt, g, :], start=(idx == 0 and r == 0), stop=(idx == len(keys) - 1 and r == 3))
                    accv = V(acc, None, acc.h[:, 0:260].rearrange("p (a b) -> p a b", a=4))
                    den = V(acc, None, accv.ap[:, :, 64])
                    if kind == "c":
                        P.dve("tensor_tensor", out=dn[:], in0=den, in1=esink[:, 4 * g:4 * g + 4], op=ALU.add)
                    else:
                        P.dve("tensor_copy", out=dn[:], in_=den)
                    P.dve("reciprocal", out=dn[:], in_=dn[:])
                    P.dve("tensor_tensor", out=r3(o_[:, g * 256:(g + 1) * 256], 4), in0=V(acc, None, accv.ap[:, :, 0:64]),
                          in1=V(dn, None, dn.h[:, :].unsqueeze(2).to_broadcast([128, 4, 64])), op=ALU.mult)
                P.dma(q(), O[qt * 128:(qt + 1) * 128, 0:512], o_[:])
            P.barrier()


    def ssd(i, j, b, need_ctx):
        TP = T_ + 8
        with ExitStack() as es:
            Xtm = sbt(es, "Xtm", [128, NT, 512]); BT = sbt(es, "BT", [128, 2, T_]); Btm = sbt(es, "Btm", [128, NT, 256])
            CTt = sbt(es, "CTt", [128, 2, T_]); Yacc = sbt(es, "Yacc", [128, NT, 512])
            buf = [sbt(es, "cbuf%d" % k_, [128, TP]) for k_ in range(1)]
            up = sbt(es, "up", [128, TP]); uo = sbt(es, "uo", [128, T_])
            cw = sbt(es, "cw", [128, 8, 5]); cb = sbt(es, "cb", [128, 8]); Abc = sbt(es, "Abc", [128, 16]); dtb = sbt(es, "dtb", [16, 1])
            Dsk = sbt(es, "Dsk", [128, 8]); nwd = sbt(es, "nwd", [128, 512])
            dt_tm = sbt(es, "dt_tm", [128, NT, 16]); a_tm = sbt(es, "a_tm", [128, NT, 16])
            ST = sbt(es, "ST", [128, 2, 256]); aTri = sbt(es, "aTri", [128, 8, 128]); acs = sbt(es, "acs", [128, 16])
            tmp = sbt(es, "stmp", [128, 8, 128]); CBs = sbt(es, "CBs", [128, 2, 128]); eacs = sbt(es, "eacs", [128, 8])
            dte = sbt(es, "dte", [128, 8]); cdec = sbt(es, "cdec", [128, 8]); xw = sbt(es, "xw", [128, 512]); tY = sbt(es, "tY", [128, 512])
            zt = sbt(es, "zt", [128, 4, 128]); u = sbt(es, "su", [128, 512]); ss2 = sbt(es, "ss2", [128, 2]); osb = [sbt(es, "sosb%d" % k_, [128, 512]) for k_ in range(2)]
            sqd = sbt(es, "sqd", [128, 256])
            P.dma(q(), cw[:], A["d_conv_wT"][j]); P.dma(q(), cb[:], A["d_conv_b_col"][j])
            P.dma(q(), Abc[:], A["d_A_log"][j:j + 1, :].partition_broadcast(128))
            P.act("activation", out=Abc[:], in_=Abc[:], func=AF.Exp)
            P.dve("tensor_scalar", out=Abc[:], in0=Abc[:], scalar1=-1.0, scalar2=None, op0=ALU.mult)
            P.dma(q(), dtb[:], A["d_dt_bias"][j:j + 1, :].rearrange("o d -> d o"))
            P.dma(q(), Dsk[:], A["d_D"][j:j + 1, :].partition_broadcast(128))
            P.dma(q(), nwd[:], A["d_norm_w"][j:j + 1, :].partition_broadcast(128))
            P.pool("memset", ap=buf[0][:], constant=0.0)
            P.pool("memset", ap=Yacc[:], constant=0.0)
            for c in range(8):
                bf = buf[0]
                r0 = 1280 + c * 128
                P.dma(q(), bf[:, 2:2 + TC], PT[r0:r0 + 128, 0:TC])
                P.dma(q(), bf[:, TC + 6:TC + 6 + TL], PT[r0:r0 + 128, TC:T_])
                W_ = T_ + 4
                P.dve("tensor_scalar", out=up[:, 2:2 + W_], in0=bf[:, 0:W_], scalar1=cw[:, c, 0:1], scalar2=None, op0=ALU.mult)
                for k_ in range(1, 5):
                    P.dve("scalar_tensor_tensor", out=up[:, 2:2 + W_], in0=bf[:, k_:k_ + W_], scalar=cw[:, c, k_:k_ + 1], in1=up[:, 2:2 + W_], op0=ALU.mult, op1=ALU.add)
                if c < 4:
                    dst = uo
                elif c < 6:
                    dst = V(BT, None, BT.h[:, c - 4, :])
                else:
                    dst = V(CTt, None, CTt.h[:, c - 6, :])
                dv = (lambda lo, hi: dst[:, lo:hi]) if c < 4 else (lambda lo, hi: V(dst.t, None, dst.ap[:, lo:hi]))
                P.act("activation", out=dv(0, TC), in_=up[:, 2:2 + TC], func=AF.Silu, bias=cb[:, c:c + 1])
                P.act("activation", out=dv(TC, T_), in_=up[:, TC + 6:TC + 6 + TL], func=AF.Silu, bias=cb[:, c:c + 1])
                if c < 6:
                    for t in range(NT):
                        pp = ps()
                        src = uo[:, t * 128:(t + 1) * 128] if c < 4 else BT[:, c - 4, t * 128:(t + 1) * 128]
                        P.pe("transpose", out=pp[:, 0:128], in_=src, identity=ident)
                        dd = Xtm[:, t, c * 128:(c + 1) * 128] if c < 4 else Btm[:, t, (c - 4) * 128:(c - 3) * 128]
                        evac(dd, pp[:, 0:128])
            dtT = _Sub(up, up.h[0:16, 0:T_])
            P.dma(q(), dtT[:, :], PT[2304:2320, :])
            P.act("activation", out=dtT[:, :], in_=dtT[:, :], func=AF.Exp, bias=dtb[:, 0:1])
            P.act("activation", out=dtT[:, :], in_=dtT[:, :], func=AF.Ln, bias=1.0)
            for t in range(NT):
                pp = ps()
                P.pe("transpose", out=pp[:, 0:16], in_=dtT[:, t * 128:(t + 1) * 128], identity=C("ident", slice(0, 16), 0, 16))
                evac(dt_tm[:, t, :], pp[:, 0:16])
                P.dve("tensor_tensor", out=a_tm[:, t, :], in0=dt_tm[:, t, :], in1=Abc[:], op=ALU.mult)
            for d in range(2):
                order = list(range(NT)) if d == 0 else (list(range(NTC - 1, -1, -1)) + list(range(NT - 1, NTC - 1, -1)))
                tri = C("m_iu") if d == 0 else C("m_il")
                neg = "neg_iu" if d == 0 else "neg_il"
                P.pool("memset", ap=ST[:], constant=0.0)
                for t in order:
                    a_ = a_tm[:, t, d * 8:(d + 1) * 8]; dt_ = dt_tm[:, t, d * 8:(d + 1) * 8]
                    pA = ps()
                    P.pe("matmul", out=pA[:, 0:8], lhsT=tri, rhs=a_, start=True, stop=False)
                    P.pe("matmul", out=pA[:, 8:16], lhsT=C("ones"), rhs=a_, start=False, stop=True)
                    P.dve("tensor_tensor", out=aTri[:], in0=bc(a_tm, a_tm.h[:, t, d * 8:(d + 1) * 8], [128, 8, 128], 2),
                          in1=bc(CT, tri.ap, [128, 8, 128], 1), op=ALU.mult)
                    evac(acs[:], pA[:, 0:16])
                    pC = ps()
                    for g in range(2):
                        P.pe("matmul", out=pC[:, g * 128:(g + 1) * 128], lhsT=BT[:, g, t * 128:(t + 1) * 128], rhs=CTt[:, g, t * 128:(t + 1) * 128], start=(g == 0), stop=(g == 1))
                    evac(CBs[:], r3(pC[:, 0:256], 2))
                    for g in range(2):
                        pR = ps()
                        P.pe("matmul", out=pR[:, :], lhsT=C("ones"), rhs=V(aTri, None, aTri.h[:, 4 * g:4 * g + 4, :].rearrange("p a b -> p (a b)")), start=True, stop=True)
                        tg = V(tmp, None, tmp.h[:, 4 * g:4 * g + 4, :])
                        P.dve("tensor_tensor", out=tg, in0=r3(pR[:, :], 4), in1=bc(CT, C(neg).ap, [128, 4, 128], 1), op=ALU.add)
                        P.dve("tensor_tensor", out=tg, in0=tg, in1=bc(acs, acs.h[:, 4 * g:4 * g + 4], [128, 4, 128], 2), op=ALU.subtract)
                        P.act("activation", out=tg, in_=tg, func=AF.Exp)
                        P.dve("tensor_tensor", out=tg, in0=tg, in1=bc(CBs, CBs.h[:, g, :], [128, 4, 128], 1), op=ALU.mult)
                        P.dve("tensor_tensor", out=tg, in0=tg, in1=bc(dt_tm, dt_tm.h[:, t, d * 8 + 4 * g:d * 8 + 4 * g + 4], [128, 4, 128], 2), op=ALU.mult)
                    pY = ps()
                    for hh in range(8):
                        P.pe("matmul", out=pY[:, hh * 64:(hh + 1) * 64], lhsT=tmp[:, hh, :], rhs=Xtm[:, t, hh * 64:(hh + 1) * 64], start=(hh == 0), stop=(hh == 7))
                    pO = ps()
                    for g in range(2):
                        P.pe("matmul", out=pO[:, g * 256:(g + 1) * 256], lhsT=CTt[:, g, t * 128:(t + 1) * 128], rhs=ST[:, g, :], start=(g == 0), stop=(g == 1))
                    P.act("activation", out=eacs[:], in_=acs[:, 0:8], func=AF.Exp)
                    P.dve("tensor_tensor", out=r3(tY[:], 8), in0=r3(pO[:, :], 8), in1=bc(eacs, eacs.h[:, :], [128, 8, 64], 2), op=ALU.mult)
                    P.dve("tensor_tensor", out=tY[:], in0=tY[:], in1=pY[:, :], op=ALU.add)
                    P.pool("tensor_tensor", out=Yacc[:, t, :], in0=Yacc[:, t, :], in1=tY[:], op=ALU.add)
                    P.dve("tensor_tensor", out=dte[:], in0=acs[:, 8:16], in1=acs[:, 0:8], op=ALU.subtract)
                    P.act("activation", out=dte[:], in_=dte[:], func=AF.Exp)
                    P.dve("tensor_tensor", out=dte[:], in0=dte[:], in1=dt_, op=ALU.mult)
                    P.dve("tensor_tensor", out=r3(xw[:], 8), in0=r3(Xtm[:, t, :], 8), in1=bc(dte, dte.h[:, :], [128, 8, 64], 2), op=ALU.mult)
                    pS = ps()
                    for g in range(2):
                        P.pe("matmul", out=pS[:, g * 256:(g + 1) * 256], lhsT=Btm[:, t, g * 128:(g + 1) * 128], rhs=xw[:, g * 256:(g + 1) * 256], start=(g == 0), stop=(g == 1))
                    P.act("activation", out=cdec[:], in_=acs[:, 8:16], func=AF.Exp)
                    st3 = V(ST, None, ST.h[:, :, :].rearrange("p g (a b) -> p (g a) b", a=4))
                    P.dve("tensor_tensor", out=st3, in0=st3, in1=bc(cdec, cdec.h[:, :], [128, 8, 64], 2), op=ALU.mult)
                    P.dve("tensor_tensor", out=V(ST, None, ST.h[:, :, :].rearrange("p g b -> p (g b)")), in0=V(ST, None, ST.h[:, :, :].rearrange("p g b -> p (g b)")), in1=pS[:, :], op=ALU.add)
            tiles = list(range(NTC, NT)) + (list(range(NTC)) if need_ctx else [])
            for kk_, t in enumerate(tiles):
                o_ = osb[kk_ % 2]
                P.dma(q(), zt[:], PT[768:1280, t * 128:(t + 1) * 128].rearrange("(c p) t -> p c t", p=128))
                P.act("activation", out=zt[:], in_=zt[:], func=AF.Silu)
                pZ = ps()
                for c in range(4):
                    P.pe("transpose", out=pZ[:, c * 128:(c + 1) * 128], in_=zt[:, c, :], identity=ident)
                P.dve("tensor_tensor", out=r3(u[:], 8), in0=r3(Xtm[:, t, :], 8), in1=bc(Dsk, Dsk.h[:, :], [128, 8, 64], 2), op=ALU.mult)
                P.dve("tensor_tensor", out=u[:], in0=u[:], in1=Yacc[:, t, :], op=ALU.add)
                P.dve("tensor_tensor", out=u[:], in0=u[:], in1=pZ[:, :], op=ALU.mult)
                P.pool("memset", ap=ss2[:], constant=0.0)
                for g in range(2):
                    P.act("activation", out=sqd[:], in_=u[:, g * 256:(g + 1) * 256], func=AF.Square, accum_out=ss2[:, g:g + 1])
                P.dve("tensor_scalar", out=ss2[:], in0=ss2[:], scalar1=1.0 / 256, scalar2=EPS, op0=ALU.mult, op1=ALU.add)
                P.act("activation", out=ss2[:], in_=ss2[:], func=AF.Sqrt)
                P.dve("reciprocal", out=ss2[:], in_=ss2[:])
                for g in range(2):
                    P.dve("scalar_tensor_tensor", out=o_[:, g * 256:(g + 1) * 256], in0=u[:, g * 256:(g + 1) * 256], scalar=ss2[:, g:g + 1], in1=nwd[:, g * 256:(g + 1) * 256], op0=ALU.mult, op1=ALU.mult)
                P.dma(q(), O[t * 128:(t + 1) * 128, 512:1024], o_[:])
            P.barrier()

    def rwkv(i, j, b, need_ctx):
        TP = T_ + 8
        W_ = T_ + 4
        with ExitStack() as es:
            big = lambda nm: sbt(es, nm, [128, T_])
            rT, kT, kkT, sgT, kdT, beT, twd, adT, sgd = [big(n_) for n_ in ("rT", "kT", "kkT", "sgT", "kdT", "beT", "twd", "adT", "sgd")]
            bf = sbt(es, "rbf", [128, TP]); up = sbt(es, "rup", [128, TP])
            Yp = sbt(es, "Yp", [128, NT, 128]); Vp = sbt(es, "Vp", [128, NT, 128]); bon = sbt(es, "bon", [128, NT, 2])
            mp = sbt(es, "mp", [128, 15]); mn = sbt(es, "mn", [128, 15]); m0 = sbt(es, "m0", [128, 15])
            w0c = sbt(es, "w0c", [128, 8]); a0c = sbt(es, "a0c", [128, 8]); kkc = sbt(es, "kkc", [128, 4]); kac = sbt(es, "kac", [128, 4])
            omka = sbt(es, "omka", [128, 4]); rkc = sbt(es, "rkc", [128, 4])
            w2s = sbt(es, "w2s", [128, 512]); a2s = sbt(es, "a2s", [128, 512]); g2s = sbt(es, "g2s", [128, 512])
            lnw = sbt(es, "lnw", [128, 512]); lnb = sbt(es, "lnb", [128, 512])
            sm = lambda nm, w=128: sbt(es, nm, [128, w])
            sgtm, cums, epos, eneg, eexc, etc_, kap, kti, bti, rti, K2T, B2T = [sm(n_) for n_ in ("sgtm", "cums", "epos", "eneg", "eexc", "etc", "kap", "kti", "bti", "rti", "K2T", "B2T")]
            cums = sm("cums2", 256); KB2 = sm("KB2", 256); gC = sm("gC", 1)
            Ns = [sm("Ns%d" % k_, 512) for k_ in range(2)]
            AukT = sm("AukT", 256); ArkT = sm("ArkT", 256); nArbT = sm("nArbT", 256)
            Wsb = sm("Wsb", 128); H = sm("Hst", 64)
            t512 = sm("t512", 512); t512b = sm("t512b", 512)
            ypost = sm("ypost", 128); cen = sm("cen", 128); mu = sm("mu", 2); var = sm("var", 2); ob = [sm("rob%d" % k_, 128) for k_ in range(2)]
            P.pool("memset", ap=bf[:], constant=0.0)
            P.dma(q(), mp[:], A["b_mu_prev_col"][j]); P.dma(q(), mn[:], A["b_mu_next_col"][j])
            P.dve("tensor_tensor", out=m0[:], in0=mp[:], in1=mn[:], op=ALU.add)
            P.dve("tensor_scalar", out=m0[:], in0=m0[:], scalar1=-1.0, scalar2=1.0, op0=ALU.mult, op1=ALU.add)
            P.dma(q(), w0c[:], A["b_w0_col"][j]); P.dma(q(), a0c[:], A["b_a0_col"][j])
            P.dma(q(), kkc[:], A["b_k_k_col"][j]); P.dma(q(), kac[:], A["b_k_a_col"][j]); P.dma(q(), rkc[:], A["b_r_k_col"][j])
            P.dve("tensor_scalar", out=omka[:], in0=kac[:], scalar1=-1.0, scalar2=1.0, op0=ALU.mult, op1=ALU.add)
            P.dma(q(), w2s[:], A["b_w2"][j]); P.dma(q(), a2s[:], A["b_a2"][j]); P.dma(q(), g2s[:], A["b_g2"][j])
            P.dma(q(), lnw[:], A["b_ln_w"][j:j + 1, :].partition_broadcast(128)); P.dma(q(), lnb[:], A["b_ln_b"][j:j + 1, :].partition_broadcast(128))

            def shift(fidx, dst, func):
                r0 = 768 + fidx * 128
                P.dma(q(), bf[:, 2:2 + TC], PT[r0:r0 + 128, 0:TC])
                P.dma(q(), bf[:, TC + 6:TC + 6 + TL], PT[r0:r0 + 128, TC:T_])
                P.dve("tensor_scalar", out=up[:, 2:2 + W_], in0=bf[:, 2:2 + W_], scalar1=m0[:, fidx:fidx + 1], scalar2=None, op0=ALU.mult)
                P.dve("scalar_tensor_tensor", out=up[:, 2:2 + W_], in0=bf[:, 1:1 + W_], scalar=mp[:, fidx:fidx + 1], in1=up[:, 2:2 + W_], op0=ALU.mult, op1=ALU.add)
                P.dve("scalar_tensor_tensor", out=up[:, 2:2 + W_], in0=bf[:, 3:3 + W_], scalar=mn[:, fidx:fidx + 1], in1=up[:, 2:2 + W_], op0=ALU.mult, op1=ALU.add)
                P.act("activation", out=dst[:, 0:TC], in_=up[:, 2:2 + TC], func=func)
                P.act("activation", out=dst[:, TC:T_], in_=up[:, TC + 6:TC + 6 + TL], func=func)

            shift(12, twd, AF.Tanh); shift(13, adT, AF.Copy); shift(14, sgd, AF.Sigmoid)
            out_tiles = list(range(NTC, NT)) + (list(range(NTC)) if need_ctx else [])
            for c in range(4):
                shift(c, rT, AF.Copy); shift(4 + c, kT, AF.Copy); shift(8 + c, sgT, AF.Copy)
                for t in range(NT):
                    pp = ps()
                    P.pe("transpose", out=pp[:, 0:128], in_=sgT[:, t * 128:(t + 1) * 128], identity=ident)
                    evac(Vp[:, t, :], pp[:, 0:128])
                P.dve("tensor_scalar", out=kkT[:], in0=kT[:], scalar1=kkc[:, c:c + 1], scalar2=None, op0=ALU.mult)
                for t0 in range(0, T_, 512):
                    n = min(512, T_ - t0)
                    P.act("activation", out=t512[:, 0:n], in_=kkT[:, t0:t0 + n], func=AF.Square)
                    pp = ps()
                    P.pe("matmul", out=pp[:, 0:n], lhsT=C("bones"), rhs=t512[:, 0:n], start=True, stop=True)
                    P.dve("tensor_scalar", out=t512[:, 0:n], in0=pp[:, 0:n], scalar1=1e-12, scalar2=None, op0=ALU.add)
                    P.act("activation", out=t512[:, 0:n], in_=t512[:, 0:n], func=AF.Sqrt)
                    P.dve("reciprocal", out=t512[:, 0:n], in_=t512[:, 0:n])
                    P.dve("tensor_tensor", out=kkT[:, t0:t0 + n], in0=kkT[:, t0:t0 + n], in1=t512[:, 0:n], op=ALU.mult)
                P.pool("memset", ap=Yp[:], constant=0.0)
                P.pool("memset", ap=bon[:], constant=0.0)
                for d in range(2):
                    aT = V(up, None, up.h[:, 0:T_])
                    for t0 in range(0, T_, 512):
                        n = min(512, T_ - t0)
                        pp = ps()
                        P.pe("matmul", out=pp[:, 0:n], lhsT=w2s[d * 64:(d + 1) * 64, c * 128:(c + 1) * 128], rhs=twd[d * 64:(d + 1) * 64, t0:t0 + n], start=True, stop=True)
                        P.act("activation", out=sgT[:, t0:t0 + n], in_=pp[:, 0:n], func=AF.Sigmoid, bias=w0c[:, d * 4 + c:d * 4 + c + 1])
                        pp = ps()
                        P.pe("matmul", out=pp[:, 0:n], lhsT=a2s[d * 64:(d + 1) * 64, c * 128:(c + 1) * 128], rhs=adT[d * 64:(d + 1) * 64, t0:t0 + n], start=True, stop=True)
                        P.act("activation", out=V(up, None, up.h[:, t0:t0 + n]), in_=pp[:, 0:n], func=AF.Sigmoid, bias=a0c[:, d * 4 + c:d * 4 + c + 1])
                    P.dve("tensor_tensor", out=beT[:], in0=aT, in1=kkT[:], op=ALU.mult)
                    P.dve("tensor_scalar", out=kdT[:], in0=aT, scalar1=kac[:, c:c + 1], scalar2=omka[:, c:c + 1], op0=ALU.mult, op1=ALU.add)
                    P.dve("tensor_tensor", out=kdT[:], in0=kdT[:], in1=kT[:], op=ALU.mult)
                    P.dve("scalar_tensor_tensor", out=aT, in0=rT[:], scalar=rkc[:, c:c + 1], in1=kdT[:], op0=ALU.mult, op1=ALU.mult)
                    pB = ps()
                    for t in range(NT):
                        P.pe("matmul", out=pB[:, t * 2:(t + 1) * 2], lhsT=V(up, None, up.h[:, t * 128:(t + 1) * 128]), rhs=C("hsel"), start=(t == 0), stop=(t == NT - 1))
                    P.dve("tensor_tensor", out=V(bon, None, bon.h[:, :, :].rearrange("p a b -> p (a b)")), in0=V(bon, None, bon.h[:, :, :].rearrange("p a b -> p (a b)")), in1=pB[:, 0:2 * NT], op=ALU.add)
                    order = list(range(NT)) if d == 0 else (list(range(NTC - 1, -1, -1)) + list(range(NT - 1, NTC - 1, -1)))
                    triS = C("triF") if d == 0 else C("triB")
                    nmA, nmB = ("nm_sl", "nm_su") if d == 0 else ("nm_su", "nm_sl")
                    mB = "m_su" if d == 0 else "m_sl"
                    iB, niB = ("m_iu", "nm_iu") if d == 0 else ("m_il", "nm_il")
                    P.pool("memset", ap=H[:], constant=0.0)
                    mk2 = lambda nm: bc(CT, C(nm).ap, [128, 2, 128], 1)
                    for t in order:
                        tl = slice(t * 128, (t + 1) * 128)
                        pp = ps()
                        P.pe("transpose", out=pp[:, 0:128], in_=sgT[:, tl], identity=ident)
                        evac(sgtm[:], pp[:, 0:128])
                        pc = ps()
                        P.pe("matmul", out=pc[:, 0:128], lhsT=sgtm[:], rhs=triS, start=True, stop=False)
                        P.pe("matmul", out=pc[:, 128:256], lhsT=sgtm[:], rhs=C("allS"), start=False, stop=True)
                        evac(cums[:], pc[:, 0:256])
                        P.act("activation", out=epos[:], in_=cums[:, 0:128], func=AF.Exp)
                        P.act("activation", out=eneg[:], in_=cums[:, 0:128], func=AF.Exp, scale=-1.0)
                        P.dve("scalar_tensor_tensor", out=eexc[:], in0=sgT[:, tl], scalar=WDEC, in1=cums[:, 0:128], op0=ALU.mult, op1=ALU.add)
                        P.act("activation", out=eexc[:], in_=eexc[:], func=AF.Exp)
                        P.dve("tensor_tensor", out=etc_[:], in0=cums[:, 128:256], in1=cums[:, 0:128], op=ALU.subtract)
                        P.act("activation", out=etc_[:], in_=etc_[:], func=AF.Exp)
                        P.act("activation", out=gC[:], in_=cums[:, 128:129], func=AF.Exp)
                        P.pool("tensor_tensor", out=kap[:], in0=kkT[:, tl], in1=eexc[:], op=ALU.mult)
                        P.dve("tensor_tensor", out=kti[:], in0=kdT[:, tl], in1=eneg[:], op=ALU.mult)
                        P.pool("tensor_tensor", out=bti[:], in0=beT[:, tl], in1=eneg[:], op=ALU.mult)
                        P.dve("tensor_tensor", out=rti[:], in0=rT[:, tl], in1=epos[:], op=ALU.mult)
                        P.pool("tensor_tensor", out=K2T[:], in0=kdT[:, tl], in1=etc_[:], op=ALU.mult)
                        P.dve("tensor_tensor", out=B2T[:], in0=beT[:, tl], in1=etc_[:], op=ALU.mult)
                        pT = ps()
                        P.pe("transpose", out=pT[:, 0:128], in_=K2T[:], identity=ident)
                        P.pe("transpose", out=pT[:, 128:256], in_=B2T[:], identity=ident)
                        P.act("copy", out=KB2[:, 0:128], in_=pT[:, 0:128])
                        P.dve("tensor_scalar", out=KB2[:, 128:256], in0=pT[:, 128:256], scalar1=-1.0, scalar2=None, op0=ALU.mult)
                        pN = ps(); pA = ps(); pB2 = ps()
                        for hp in range(2):
                            sl = slice(hp * 64, (hp + 1) * 64)
                            P.pe("matmul", out=pN[:, hp * 128:(hp + 1) * 128], lhsT=kap[sl, :], rhs=bti[sl, :], start=(hp == 0), stop=False)
                            P.pe("matmul", out=pN[:, 256 + hp * 128:256 + (hp + 1) * 128], lhsT=bti[sl, :], rhs=kap[sl, :], start=False, stop=(hp == 1))
                            P.pe("matmul", out=pA[:, hp * 128:(hp + 1) * 128], lhsT=kti[sl, :], rhs=kap[sl, :], start=(hp == 0), stop=False)
                            P.pe("matmul", out=pA[:, 256 + hp * 128:256 + (hp + 1) * 128], lhsT=kti[sl, :], rhs=rti[sl, :], start=False, stop=(hp == 1))
                            P.pe("matmul", out=pB2[:, hp * 128:(hp + 1) * 128], lhsT=bti[sl, :], rhs=rti[sl, :], start=(hp == 0), stop=(hp == 1))
                        N0 = Ns[0]
                        P.dve("tensor_tensor", out=r3(N0[:, 0:256], 2), in0=r3(pN[:, 0:256], 2), in1=mk2(nmA), op=ALU.mult)
                        P.dve("tensor_tensor", out=r3(N0[:, 256:512], 2), in0=r3(pN[:, 256:512], 2), in1=mk2(nmB), op=ALU.mult)
                        P.dve("tensor_tensor", out=r3(AukT[:], 2), in0=r3(pA[:, 0:256], 2), in1=mk2(mB), op=ALU.mult)
                        P.dve("tensor_tensor", out=r3(ArkT[:], 2), in0=r3(pA[:, 256:512], 2), in1=mk2(iB), op=ALU.mult)
                        P.dve("tensor_tensor", out=r3(nArbT[:], 2), in0=r3(pB2[:, 0:256], 2), in1=mk2(niB), op=ALU.mult)
                        pW = ps()
                        for hp in range(2):
                            sl = slice(hp * 64, (hp + 1) * 64)
                            P.pe("matmul", out=pW[:, hp * 64:(hp + 1) * 64], lhsT=kap[sl, :], rhs=H[sl, :], start=(hp == 0), stop=False)
                            P.pe("matmul", out=pW[:, hp * 64:(hp + 1) * 64], lhsT=AukT[:, hp * 128:(hp + 1) * 128], rhs=Vp[:, t, hp * 64:(hp + 1) * 64], start=False, stop=(hp == 1))
                        evac(Wsb[:], pW[:, 0:128])
                        cur = 0
                        for lv in range(7):
                            Nc = Ns[cur]
                            pU = ps()
                            for hp in range(2):
                                P.pe("matmul", out=pU[:, hp * 64:(hp + 1) * 64], lhsT=Nc[:, 256 + hp * 128:256 + (hp + 1) * 128], rhs=Wsb[:, hp * 64:(hp + 1) * 64], start=(hp == 0), stop=(hp == 1))
                            if lv < 6:
                                pQ = ps()
                                for hp in range(2):
                                    P.pe("matmul", out=pQ[:, hp * 128:(hp + 1) * 128], lhsT=Nc[:, 256 + hp * 128:256 + (hp + 1) * 128], rhs=Nc[:, hp * 128:(hp + 1) * 128], start=(hp == 0), stop=False)
                                    P.pe("matmul", out=pQ[:, 256 + hp * 128:256 + (hp + 1) * 128], lhsT=Nc[:, hp * 128:(hp + 1) * 128], rhs=Nc[:, 256 + hp * 128:256 + (hp + 1) * 128], start=False, stop=(hp == 1))
                            P.dve("tensor_tensor", out=Wsb[:], in0=Wsb[:], in1=pU[:, 0:128], op=ALU.add)
                            if lv < 6:
                                cur ^= 1
                                P.act("copy", out=Ns[cur][:], in_=pQ[:, :])
                        pYh = ps()
                        for hp in range(2):
                            sl = slice(hp * 64, (hp + 1) * 64)
                            o_ = pYh[:, hp * 64:(hp + 1) * 64]
                            P.pe("matmul", out=o_, lhsT=rti[sl, :], rhs=H[sl, :], start=(hp == 0), stop=False)
                            P.pe("matmul", out=o_, lhsT=ArkT[:, hp * 128:(hp + 1) * 128], rhs=Vp[:, t, hp * 64:(hp + 1) * 64], start=False, stop=False)
                            P.pe("matmul", out=o_, lhsT=nArbT[:, hp * 128:(hp + 1) * 128], rhs=Wsb[:, hp * 64:(hp + 1) * 64], start=False, stop=(hp == 1))
                        P.dve("tensor_tensor", out=Yp[:, t, :], in0=Yp[:, t, :], in1=pYh[:, 0:128], op=ALU.add)
                        pH = ps()
                        P.pe("matmul", out=pH[:, 0:128], lhsT=KB2[:, 0:128], rhs=Vp[:, t, :], start=True, stop=False)
                        P.pe("matmul", out=pH[:, 0:128], lhsT=KB2[:, 128:256], rhs=Wsb[:], start=False, stop=True)
                        for hp in range(2):
                            sl = slice(hp * 64, (hp + 1) * 64)
                            P.dve("scalar_tensor_tensor", out=H[sl, :], in0=H[sl, :], scalar=gC[sl, 0:1], in1=pH[sl, hp * 64:(hp + 1) * 64], op0=ALU.mult, op1=ALU.add)
                for kk_, t in enumerate(out_tiles):
                    o_ = ob[kk_ % 2]
                    y3 = r3(Yp[:, t, :], 2)
                    P.dve("tensor_reduce", out=mu[:], in_=y3, axis=AX.X, op=ALU.add)
                    P.dve("tensor_scalar", out=mu[:], in0=mu[:], scalar1=1.0 / 64, scalar2=None, op0=ALU.mult)
                    P.dve("tensor_tensor", out=r3(cen[:], 2), in0=y3, in1=bc(mu, mu.h[:, :], [128, 2, 64], 2), op=ALU.subtract)
                    P.dve("tensor_tensor", out=ypost[:], in0=cen[:], in1=cen[:], op=ALU.mult)
                    P.dve("tensor_reduce", out=var[:], in_=r3(ypost[:], 2), axis=AX.X, op=ALU.add)
                    P.dve("tensor_scalar", out=var[:], in0=var[:], scalar1=1.0 / 64, scalar2=64e-5, op0=ALU.mult, op1=ALU.add)
                    P.act("activation", out=var[:], in_=var[:], func=AF.Sqrt)
                    P.dve("reciprocal", out=var[:], in_=var[:])
                    P.dve("tensor_tensor", out=r3(cen[:], 2), in0=r3(cen[:], 2), in1=bc(var, var.h[:, :], [128, 2, 64], 2), op=ALU.mult)
                    P.dve("tensor_tensor", out=cen[:], in0=cen[:], in1=lnw[:, c * 128:(c + 1) * 128], op=ALU.mult)
                    P.dve("tensor_tensor", out=cen[:], in0=cen[:], in1=lnb[:, c * 128:(c + 1) * 128], op=ALU.add)
                    P.dve("tensor_tensor", out=r3(ypost[:], 2), in0=r3(Vp[:, t, :], 2), in1=bc(bon, bon.h[:, t, :], [128, 2, 64], 2), op=ALU.mult)
                    P.dve("tensor_tensor", out=cen[:], in0=cen[:], in1=ypost[:], op=ALU.add)
                    pG = ps()
                    P.pe("matmul", out=pG[:, 0:128], lhsT=sgd[:, t * 128:(t + 1) * 128], rhs=g2s[:, c * 128:(c + 1) * 128], start=True, stop=True)
                    P.dve("tensor_tensor", out=o_[:], in0=cen[:], in1=pG[:, 0:128], op=ALU.mult)
                    P.dma(q(), O[t * 128:(t + 1) * 128, 512 + c * 128:512 + (c + 1) * 128], o_[:])
            P.barrier()

    def mix_ab(i, j, b, need_ctx):
        if "attn" not in cfg.dbg.get("skip", ()):
            attn(i, j, b, need_ctx, "a")
        if "rwkv" not in cfg.dbg.get("skip", ()):
            rwkv(i, j, b, need_ctx)

    def mix_cd(i, j, b, need_ctx):
        if "attn" not in cfg.dbg.get("skip", ()):
            attn(i, j, b, need_ctx, "c")
        if "ssd" not in cfg.dbg.get("skip", ()):
            ssd(i, j, b, need_ctx)

    def dcopy(dst, src, rows):
        for r0 in range(0, rows, 128):
            P.dma(q(), dst[r0:r0 + 128, :], src[r0:r0 + 128, :])

    phase_mod()
    final_evs = []
    for b in range(NB):
        dcopy(X[0:TC, :], A["ctx"][b], TC)
        dcopy(X[TC:T_, :], A["x"][b], TL)
        P.barrier()
        for i in range(DEPTH):
            need_ctx = i < DEPTH - 1
            j = i // 2
            if i % 2 == 0:
                phase_inproj(i, b, A["ab_w_in"][j], 2688, need_ctx)
            else:
                phase_inproj(i, b, A["cd_w_in"][j], 2320, need_ctx)
            if ("PT%d" % i) in dbg_out and b == 0:
                dcopy(dbg_out["PT%d" % i], PT, 2688)
                P.barrier()
            if "O_in" in cfg.dbg:
                dcopy(O, A["O_in"][i, b], T_)
                P.barrier()
            if i % 2 == 0:
                mix_ab(i, j, b, need_ctx)
            else:
                mix_cd(i, j, b, need_ctx)
            if ("O%d" % i) in dbg_out and b == 0:
                dcopy(dbg_out["O%d" % i], O, T_)
                P.barrier()
            phase_mlp(i, b, need_ctx)
            if ("X%d" % i) in dbg_out and b == 0:
                dcopy(dbg_out["X%d" % i], X, T_)
                P.barrier()
        final_evs += phase_final(b)
    for nm in dbg_out:
        pass
    P.barrier()
    P.emit(final_evs)
    P.close()
    return nc


def make_in_maps(cfg, inputs, n_cores):
    carr, _, rope = make_consts(cfg)
    NB = cfg.NB
    f = lambda a: np.ascontiguousarray(np.asarray(a, dtype=np.float32))
    shared = {}
    for k_ in ("w_mod", "b_mod", "norm_mix", "norm_mlp", "w_out", "mlp_w1", "mlp_w2", "ab_w_in", "a_q_norm", "a_k_norm",
               "b_mu_prev", "b_mu_next", "b_g2", "b_k_k", "b_k_a", "b_ln_w", "b_ln_b"):
        shared[k_] = f(inputs[k_])
    NE = shared["ab_w_in"].shape[0]
    shared["final_norm"] = f(inputs["final_norm"]).reshape(1, D)
    shared["b_w0"] = f(inputs["b_w0"]).reshape(NE, 1024); shared["b_a0"] = f(inputs["b_a0"]).reshape(NE, 1024)
    shared["b_w2"] = f(inputs["b_w2"]).reshape(NE, 128, 512); shared["b_a2"] = f(inputs["b_a2"]).reshape(NE, 128, 512)
    shared["b_r_k"] = f(inputs["b_r_k"]).reshape(NE, 512)
    col = lambda a, n: f(f(a).reshape(NE, n, 128).transpose(0, 2, 1))
    shared["b_mu_prev_col"] = col(inputs["b_mu_prev"], 15); shared["b_mu_next_col"] = col(inputs["b_mu_next"], 15)
    shared["b_w0_col"] = col(inputs["b_w0"], 8); shared["b_a0_col"] = col(inputs["b_a0"], 8)
    shared["b_k_k_col"] = col(inputs["b_k_k"], 4); shared["b_k_a_col"] = col(inputs["b_k_a"], 4); shared["b_r_k_col"] = col(inputs["b_r_k"], 4)
    if cfg.DEPTH // 2:
        NO = cfg.DEPTH // 2
        for k_ in ("cd_w_in", "c_sink", "d_conv_w", "d_conv_b", "d_D", "d_norm_w"):
            shared[k_] = f(inputs[k_])
        shared["d_conv_wT"] = f(f(inputs["d_conv_w"]).reshape(NO, 5, 8, 128).transpose(0, 3, 2, 1))
        shared["d_conv_b_col"] = f(f(inputs["d_conv_b"]).reshape(NO, 8, 128).transpose(0, 2, 1))
        shared["d_dt_bias"] = f(inputs["d_dt_bias"]).reshape(NO, 16); shared["d_A_log"] = f(inputs["d_A_log"]).reshape(NO, 16)
    shared["consts"] = carr; shared["rope"] = rope
    maps = []
    x, c, ctx, c_ctx = f(inputs["x"]), f(inputs["c"]), f(inputs["ctx"]), f(inputs["c_ctx"])
    for k_ in range(n_cores):
        m = dict(shared)
        sl = slice(k_ * NB, (k_ + 1) * NB)
        m["x"] = x[sl]; m["ctx"] = ctx[sl]
        m["c5T"] = np.ascontiguousarray(np.concatenate([c[sl], c_ctx[None, :]], axis=0).T)
        maps.append(m)
    return maps


_CACHE = {}


def kernel(**inputs):
    cfg = Cfg()
    n_cores = 8
    if "nc" not in _CACHE:
        _CACHE["nc"] = build(cfg)
    nc = _CACHE["nc"]
    maps = make_in_maps(cfg, inputs, n_cores)
    res = run_bass_kernel_spmd(nc, maps, core_ids=list(range(n_cores)))
    return np.concatenate([np.asarray(r["out"]) for r in res.results], axis=0).astype(np.float32)
```

```python
import numpy as np
from contextlib import ExitStack
import concourse.bass as bass
import concourse.mybir as mybir
from concourse.bass_utils import run_bass_kernel_spmd

F32 = mybir.dt.float32
BF16 = mybir.dt.bfloat16
ALU = mybir.AluOpType
AF = mybir.ActivationFunctionType
AX = mybir.AxisListType

WRITE_KW = ("out", "ap", "accum_out")
NDMASEM = 6


class V:
    __slots__ = ("t", "key", "ap")

    def __init__(self, t, key, ap):
        self.t, self.key, self.ap = t, key, ap


class T:
    def __init__(self, name, handle, is_ap=False):
        self.name = name
        self.h = handle
        self.is_ap = is_ap
        self.w = {}
        self.r = {}

    def __getitem__(self, idx):
        return V(self, None, self.h[idx])

    def k(self, key):
        return _Keyed(self, key)

    def v(self, ap, key=None):
        return V(self, key, ap)


class _Sub:
    def __init__(self, t, ap):
        self.t, self.ap = t, ap

    def __getitem__(self, idx):
        return V(self.t, None, self.ap[idx])


class _Keyed:
    def __init__(self, t, key):
        self.t, self.key = t, key

    def __getitem__(self, idx):
        return V(self.t, self.key, self.t.h[idx])


class Prog:
    ENG = ("pe", "act", "dve", "pool", "sp")

    def __init__(self, nc):
        self.nc = nc
        self.es = ExitStack()
        self.ops = {e: [] for e in self.ENG}
        self.cnt = {e: 0 for e in self.ENG}
        self.dcnt = {e: 0 for e in self.ENG}
        self.sems = {}
        self.known = {e: {} for e in self.ENG}
        for e in self.ENG:
            self.sems[e] = self.es.enter_context(nc.semaphore("s_" + e))
            for j in range(NDMASEM):
                self.sems[(e, j)] = self.es.enter_context(nc.semaphore("d_%s%d" % (e, j)))
        self.n_psum = 0

    def sb(self, name, shape, dtype=F32):
        h = self.es.enter_context(self.nc.sbuf_tensor(name, list(shape), dtype))
        return T(name, h)

    def ps(self, name, shape, dtype=F32):
        h = self.es.enter_context(self.nc.psum_tensor(name, list(shape), dtype))
        return T(name, h)

    def dram(self, name, shape, dtype=F32, kind="Internal"):
        h = self.nc.dram_tensor(name, list(shape), dtype, kind=kind).ap()
        return T(name, h, True)

    def _conflicts(self, d, key):
        if key is None:
            for kk, ev in d.items():
                yield kk, ev
        else:
            if key in d:
                yield key, d[key]
            if None in d:
                yield None, d[None]

    def barrier(self):
        evs = []
        for e in self.ENG:
            if self.cnt[e]:
                evs.append((e, self.cnt[e]))
            for j in range(NDMASEM):
                n = self.dcnt[e]
                k = (n - j + NDMASEM - 1) // NDMASEM if n > j else 0
                if k:
                    evs.append(((e, j), 16 * k))
        for e in self.ENG:
            kn = self.known[e]
            wl = []
            for s, v in evs:
                if s == e and e == "pe":
                    continue
                if kn.get(s, 0) >= v:
                    continue
                kn[s] = v
                wl.append((s, v))
            if wl:
                self.ops[e].append((wl, None, None, None, None, None))

    def _issue(self, eng, name, args, kw, is_dma):
        kw = dict(kw)
        reads, writes = list(kw.pop("rd", [])), list(kw.pop("wr", []))
        nargs = []
        for i, a in enumerate(args):
            if isinstance(a, V):
                (writes if i == 0 else reads).append(a)
                nargs.append(a.ap)
            else:
                nargs.append(a)
        nkw = {}
        for k_, a in kw.items():
            if isinstance(a, V):
                (writes if k_ in WRITE_KW else reads).append(a)
                nkw[k_] = a.ap
            else:
                nkw[k_] = a
        waits = {}

        def need(ev):
            s, val = ev
            if waits.get(s, 0) < val:
                waits[s] = val

        for v in reads:
            for _, ev in self._conflicts(v.t.w, v.key):
                need(ev)
        for v in writes:
            for _, ev in self._conflicts(v.t.w, v.key):
                need(ev)
            for _, evs in self._conflicts(v.t.r, v.key):
                for ev in evs:
                    need(ev)
        if is_dma:
            n = self.dcnt[eng]
            self.dcnt[eng] += 1
            j = n % NDMASEM
            semk = (eng, j)
            val = 16 * (n // NDMASEM + 1)
            inc = 16
            if n >= NDMASEM:
                need((semk, val - 16))
        else:
            self.cnt[eng] += 1
            semk = eng
            val = self.cnt[eng]
            inc = 1
        ev = (semk, val)
        kn = self.known[eng]
        wl = []
        for s, val_ in waits.items():
            if s == eng and eng == "pe":
                continue
            if kn.get(s, 0) >= val_:
                continue
            kn[s] = val_
            wl.append((s, val_))
        for v in writes:
            t, key = v.t, v.key
            if key is None:
                t.w = {None: ev}
                t.r = {}
            else:
                t.w[key] = ev
                t.r[key] = []
        for v in reads:
            t, key = v.t, v.key
            t.r.setdefault(key, []).append(ev)
            if len(t.r[key]) > 12:
                d = {}
                for s, vv in t.r[key]:
                    if d.get(s, 0) < vv:
                        d[s] = vv
                t.r[key] = list(d.items())
        self.ops[eng].append((wl, name, nargs, nkw, semk, inc))
        return ev

    def I(self, eng, name, *args, **kw):
        return self._issue(eng, name, args, kw, False)

    def dma(self, eng, out, in_, **kw):
        return self._issue(eng, "dma_start", (), dict(out=out, in_=in_, **kw), True)

    def pe(self, name, *a, **k):
        return self.I("pe", name, *a, **k)

    def act(self, name, *a, **k):
        return self.I("act", name, *a, **k)

    def dve(self, name, *a, **k):
        return self.I("dve", name, *a, **k)

    def pool(self, name, *a, **k):
        return self.I("pool", name, *a, **k)

    def emit(self, final_events):
        nc = self.nc
        engobj = {"pe": "tensor", "act": "scalar", "dve": "vector", "pool": "gpsimd", "sp": "sync"}
        with nc.Block() as block:
            for e in self.ENG:
                ops = self.ops[e]
                if e == "sp":
                    ops = list(ops)
                    fin = {}
                    for (s, v) in final_events:
                        fin[s] = max(fin.get(s, 0), v)
                    ops.append(([(s, v) for s, v in fin.items()], None, None, None, None, None))
                if not ops:
                    continue

                def body(eng, ops=ops):
                    for (wl, name, nargs, nkw, semk, inc) in ops:
                        for s, v in wl:
                            eng.wait_ge(self.sems[s], v)
                        if name is None:
                            continue
                        ins = getattr(eng, name)(*nargs, **nkw)
                        ins.then_inc(self.sems[semk], inc)
                getattr(block, engobj[e])(body)

    def close(self):
        self.es.close()

D = 1024
EPS = 1e-6
WDEC = 0.6065306597126334


class Cfg:
    def __init__(self, NB=4, TC=256, TL=2048, DEPTH=4, GRID_W=64, dbg=None):
        self.NB, self.TC, self.TL, self.DEPTH, self.GRID_W = NB, TC, TL, DEPTH, GRID_W
        self.T = TC + TL
        self.NT = self.T // 128
        self.NTC = TC // 128
        self.R = NB + 1
        self.dbg = dbg or {}


def make_consts(cfg):
    c = {}
    idx = np.arange(128)
    s, t = idx[:, None], idx[None, :]
    c["ident"] = np.eye(128, dtype=np.float32)
    c["bones"] = ((s // 64) == (t // 64)).astype(np.float32)
    c["ones"] = np.ones((128, 128), np.float32)
    selI = np.zeros((128, 128), np.float32)
    selR = np.zeros((128, 128), np.float32)
    for hp in range(2):
        for d in range(64):
            selI[hp * 64 + d, hp * 64 + d] = 1.0
            q = d % 32
            if q < 16:
                selR[hp * 64 + d + 16, hp * 64 + d] = -1.0
            else:
                selR[hp * 64 + d - 16, hp * 64 + d] = 1.0
    c["selI"] = selI
    c["selR"] = selR
    su = (s < t).astype(np.float32); iu = (s <= t).astype(np.float32)
    sl = (s > t).astype(np.float32); il = (s >= t).astype(np.float32)
    c["m_su"] = su; c["m_iu"] = iu
    c["m_sl"] = sl; c["m_il"] = il
    c["nm_su"] = -c["m_su"]; c["nm_sl"] = -c["m_sl"]
    c["nm_iu"] = -c["m_iu"]; c["nm_il"] = -c["m_il"]
    c["triF"] = (-WDEC) * iu; c["triB"] = (-WDEC) * il; c["allS"] = (-WDEC) * np.ones((128, 128), np.float32)
    c["neg_iu"] = ((1.0 - iu) * (-30000.0)).astype(np.float32)
    c["neg_il"] = ((1.0 - il) * (-30000.0)).astype(np.float32)
    c["hsel"] = np.zeros((128, 2), np.float32); c["hsel"][:64, 0] = 1; c["hsel"][64:, 1] = 1
    offs = {}
    cols = []
    o = 0
    for k_, v in c.items():
        offs[k_] = (o, v.shape[1]); cols.append(v.astype(np.float32)); o += v.shape[1]
    arr = np.concatenate(cols, axis=1)
    TL, TC, GW = cfg.TL, cfg.TC, cfg.GRID_W
    tl = np.arange(TL)
    row = (tl // GW).astype(np.float32); col = (tl % GW).astype(np.float32)
    inv = (10000.0 ** (-np.arange(16, dtype=np.float32) / 16)).astype(np.float32)
    ang_r = row[:, None] * inv[None, :]; ang_c = col[:, None] * inv[None, :]
    ang = np.concatenate([ang_r, ang_r, ang_c, ang_c], axis=1)
    rope = np.zeros((64, 2, cfg.T), np.float32)
    rope[:, 0, :TC] = 1.0
    rope[:, 0, TC:] = np.cos(ang).T
    rope[:, 1, TC:] = np.sin(ang).T
    return arr, offs, rope


def build(cfg):
    NB, TC, TL, DEPTH, T_, NT, NTC, R = cfg.NB, cfg.TC, cfg.TL, cfg.DEPTH, cfg.T, cfg.NT, cfg.NTC, cfg.R
    NE, NO = (DEPTH + 1) // 2, DEPTH // 2
    nc = bass.Bass("TRN2", target_bir_lowering=False)
    P = Prog(nc)
    carr, coffs, _ = make_consts(cfg)
    NCC = carr.shape[1]
    A = {}

    def inp(name, shape):
        A[name] = nc.dram_tensor(name, list(shape), F32, kind="ExternalInput").ap()
        return A[name]

    inp("x", [NB, TL, D]); inp("ctx", [NB, TC, D]); inp("c5T", [D, R])
    inp("w_mod", [DEPTH, D, 6 * D]); inp("b_mod", [DEPTH, 6 * D]); inp("norm_mix", [DEPTH, D]); inp("norm_mlp", [DEPTH, D])
    inp("w_out", [DEPTH, D, D]); inp("mlp_w1", [DEPTH, D, 4 * D]); inp("mlp_w2", [DEPTH, 4 * D, D]); inp("final_norm", [1, D])
    inp("ab_w_in", [NE, D, 2688]); inp("a_q_norm", [NE, 64]); inp("a_k_norm", [NE, 64])
    inp("b_mu_prev", [NE, 1920]); inp("b_mu_next", [NE, 1920]); inp("b_w0", [NE, 1024]); inp("b_w2", [NE, 128, 512])
    inp("b_a0", [NE, 1024]); inp("b_a2", [NE, 128, 512]); inp("b_g2", [NE, 128, 512]); inp("b_k_k", [NE, 512]); inp("b_k_a", [NE, 512])
    inp("b_r_k", [NE, 512]); inp("b_ln_w", [NE, 512]); inp("b_ln_b", [NE, 512])
    inp("b_mu_prev_col", [NE, 128, 15]); inp("b_mu_next_col", [NE, 128, 15]); inp("b_w0_col", [NE, 128, 8]); inp("b_a0_col", [NE, 128, 8])
    inp("b_k_k_col", [NE, 128, 4]); inp("b_k_a_col", [NE, 128, 4]); inp("b_r_k_col", [NE, 128, 4])
    if NO:
        inp("cd_w_in", [NO, D, 2320]); inp("c_sink", [NO, 8]); inp("d_conv_w", [NO, 5, 1024]); inp("d_conv_b", [NO, 1024])
        inp("d_dt_bias", [NO, 16]); inp("d_A_log", [NO, 16]); inp("d_D", [NO, 8]); inp("d_norm_w", [NO, 512])
        inp("d_conv_wT", [NO, 128, 8, 5]); inp("d_conv_b_col", [NO, 128, 8])
    inp("consts", [128, NCC]); inp("rope", [64, 2, T_])
    if "O_in" in cfg.dbg:
        inp("O_in", [DEPTH, NB, T_, D])
    out_ap = nc.dram_tensor("out", [NB, TL, D], F32, kind="ExternalOutput").ap()
    dbg_out = {}
    for nm, shp in cfg.dbg.get("outs", {}).items():
        dbg_out[nm] = nc.dram_tensor(nm, list(shp), F32, kind="ExternalOutput").ap()

    def scratch(name, shape):
        return nc.dram_tensor(name, list(shape), F32, kind="Internal").ap()

    MOD = scratch("MOD", [DEPTH, R, 6 * D])
    X = scratch("X", [T_, D])
    PT = scratch("PT", [2688, T_])
    O = scratch("O", [T_, D])

    CT = P.sb("consts_sb", [128, NCC])
    PS = [P.ps("ps%d" % j, [128, 512]) for j in range(8)]
    st = {"ps": 0, "q": 0, "ev": 0}

    def ps():
        st["ps"] = (st["ps"] + 1) % 8
        return PS[st["ps"]]

    def q():
        st["q"] ^= 1
        return "sp" if st["q"] else "pool"

    def C(name, rows=slice(0, 128), lo=0, n=None):
        o, w = coffs[name]
        n = w - lo if n is None else n
        return CT[rows, o + lo:o + lo + n]

    def evac(dst, src):
        st["ev"] ^= 1
        if st["ev"]:
            P.act("copy", out=dst, in_=src)
        else:
            P.dve("tensor_copy", out=dst, in_=src)

    def sbt(es, name, shape, dt=F32):
        st["uid"] = st.get("uid", 0) + 1
        name = "%s_%d" % (name, st["uid"])
        h = es.enter_context(nc.sbuf_tensor(name, list(shape), dt))
        return T(name, h)

    def r3(v, a):
        return V(v.t, v.key, v.ap.rearrange("p (a b) -> p a b", a=a))

    def bc(t_, ap, shape, axis):
        return V(t_, None, ap.unsqueeze(axis).to_broadcast(list(shape)))

    def bcast_row(ap2d, n=128):
        return ap2d.partition_broadcast(n)

    def colvec(ap1d_row):
        return ap1d_row.rearrange("o (c p) -> p (o c)", p=128)

    P.dma("sp", CT[:], A["consts"][:, :])
    ident = C("ident")

    def transpose(dst, src, npart_in, nfree_in, pp=None, off=0):
        pp = pp or ps()
        P.pe("transpose", out=pp[0:nfree_in, off:off + npart_in], in_=src, identity=C("ident", slice(0, npart_in), 0, npart_in))
        if dst is not None:
            evac(dst, pp[0:nfree_in, off:off + npart_in])
        return pp

    def rms(es_tiles, xt, Gt, St, outv):
        sq, ss, rs = es_tiles
        P.pool("memset", ap=ss[:], constant=0.0)
        P.act("activation", out=sq[:], in_=xt, func=AF.Square, accum_out=ss[:])
        P.dve("tensor_scalar", out=rs[:], in0=ss[:], scalar1=1.0 / D, scalar2=EPS, op0=ALU.mult, op1=ALU.add)
        P.act("activation", out=rs[:], in_=rs[:], func=AF.Sqrt)
        P.dve("reciprocal", out=rs[:], in_=rs[:])
        P.dve("scalar_tensor_tensor", out=outv, in0=xt, scalar=rs[:, 0:1], in1=Gt, op0=ALU.mult, op1=ALU.mult)
        if St is not None:
            P.dve("tensor_tensor", out=outv, in0=outv, in1=St, op=ALU.add)

    def phase_mod():
        with ExitStack() as es:
            c5 = sbt(es, "c5", [128, 8, R]); sc5 = sbt(es, "sc5", [128, 8, R])
            wb = [sbt(es, "wmb%d" % j, [128, 8, 512]) for j in range(2)]
            bt = [sbt(es, "bmb%d" % j, [R, 512]) for j in range(2)]
            res = [sbt(es, "mres%d" % j, [R, 512]) for j in range(2)]
            P.dma("sp", c5[:], A["c5T"].rearrange("(kc p) r -> p kc r", p=128))
            P.act("activation", out=sc5[:], in_=c5[:], func=AF.Silu)
            k = 0
            for i in range(DEPTH):
                for n in range(12):
                    w_, b_, r_ = wb[k % 2], bt[k % 2], res[k % 2]
                    P.dma(q(), w_[:], A["w_mod"][i][:, n * 512:(n + 1) * 512].rearrange("(kc p) n -> p kc n", p=128))
                    P.dma(q(), b_[:], A["b_mod"][i:i + 1, n * 512:(n + 1) * 512].partition_broadcast(R))
                    pp = ps()
                    for kc in range(8):
                        P.pe("matmul", out=pp[0:R, :], lhsT=sc5[:, kc, :], rhs=w_[:, kc, :], start=(kc == 0), stop=(kc == 7))
                    P.dve("tensor_tensor", out=r_[:], in0=pp[0:R, :], in1=b_[:], op=ALU.add)
                    P.dma(q(), MOD[i, :, n * 512:(n + 1) * 512], r_[:])
                    k += 1
            P.barrier()

    def phase_inproj(i, b, w_in, NF, need_ctx):
        with ExitStack() as es:
            hT = sbt(es, "hT", [128, 8, T_], BF16)
            Wbb = [sbt(es, "Wbb%d" % j, [128, 8, 512], BF16) for j in range(2)]
            xt = [sbt(es, "xt%d" % j, [128, D]) for j in range(2)]
            h = [sbt(es, "h%d" % j, [128, D]) for j in range(2)]
            sq = sbt(es, "sq", [128, D]); ss = sbt(es, "ss", [128, 1]); rs = sbt(es, "rs", [128, 1])
            G = {s_: sbt(es, "G" + s_, [128, D]) for s_ in "cl"}
            S = {s_: sbt(es, "S" + s_, [128, D]) for s_ in "cl"}
            nw = sbt(es, "nw", [128, D])
            Wb = [sbt(es, "Wb%d" % j, [128, 8, 512]) for j in range(2)]
            stg = [sbt(es, "stg%d" % j, [128, 512]) for j in range(2)]
            P.dma(q(), nw[:], A["norm_mix"][i:i + 1, :].partition_broadcast(128))
            for s_, r_ in (("c", NB), ("l", b)):
                P.dma(q(), sq[:], MOD[i, r_:r_ + 1, D:2 * D].partition_broadcast(128))
                P.dma(q(), S[s_][:], MOD[i, r_:r_ + 1, 0:D].partition_broadcast(128))
                P.dve("scalar_tensor_tensor", out=G[s_][:], in0=sq[:], scalar=1.0, in1=nw[:], op0=ALU.add, op1=ALU.mult)
            for t in range(NT):
                s_ = "c" if t < NTC else "l"
                x_ = xt[t % 2]; h_ = h[t % 2]
                P.dma(q(), x_[:], X[t * 128:(t + 1) * 128, :])
                rms((sq, ss, rs), x_[:], G[s_][:], S[s_][:], h_[:])
                for half in range(2):
                    pp = ps()
                    for j in range(4):
                        kc = half * 4 + j
                        P.pe("transpose", out=pp[:, j * 128:(j + 1) * 128], in_=h_[:, kc * 128:(kc + 1) * 128], identity=ident)
                    evac(hT[:, half * 4:(half + 1) * 4, t * 128:(t + 1) * 128], r3(pp[:, :], 4))
            k = 0
            for n0 in range(0, NF, 512):
                nn = min(512, NF - n0)
                W_ = Wb[(n0 // 512) % 2]
                P.dma(q(), W_[:, :, 0:nn], w_in[:, n0:n0 + nn].rearrange("(kc p) n -> p kc n", p=128))
                Wf_ = W_
                W_ = Wbb[(n0 // 512) % 2]
                P.pool("tensor_copy", out=W_[:, :, 0:nn], in_=Wf_[:, :, 0:nn])
                for f0 in range(0, nn, 128):
                    m = min(128, nn - f0)
                    for t0 in range(0, T_, 512):
                        n = min(512, T_ - t0)
                        pp = ps()
                        for kc in range(8):
                            P.pe("matmul", out=pp[0:m, 0:n], lhsT=W_[:, kc, f0:f0 + m], rhs=hT[:, kc, t0:t0 + n], start=(kc == 0), stop=(kc == 7))
                        s2 = stg[k % 2]; k += 1
                        evac(s2[0:m, 0:n], pp[0:m, 0:n])
                        P.dma(q(), PT[n0 + f0:n0 + f0 + m, t0:t0 + n], s2[0:m, 0:n])
            P.barrier()

    def phase_mlp(i, b, need_ctx):
        with ExitStack() as es:
            hid = sbt(es, "hid", [128, 32, 512], BF16)
            Wout = sbt(es, "Woutb", [128, 8, D], BF16)
            xb = sbt(es, "xb", [128, 4, D]); ot = sbt(es, "ot", [128, D]); oT = sbt(es, "oT", [128, 8, 128], BF16)
            h2 = sbt(es, "h2", [128, D]); sq = sbt(es, "sq2", [128, D]); ss = sbt(es, "ss2", [128, 1]); rs = sbt(es, "rs2", [128, 1])
            h2T = sbt(es, "h2T", [128, 8, 512], BF16)
            W1b = [sbt(es, "W1b%d" % j, [128, 8, 512]) for j in range(2)]
            W1bb = [sbt(es, "W1bb%d" % j, [128, 8, 512], BF16) for j in range(2)]
            W2r = [sbt(es, "W2r%d" % j, [128, D]) for j in range(2)]
            W2rb = [sbt(es, "W2rb%d" % j, [128, D], BF16) for j in range(2)]
            g1 = sbt(es, "g1", [128, D]); G2 = sbt(es, "G2", [128, D]); S2 = sbt(es, "S2", [128, D]); g2 = sbt(es, "g2", [128, D])
            blocks = []
            if need_ctx:
                for t0 in range(0, NTC, 4):
                    blocks.append(("c", list(range(t0, min(NTC, t0 + 4)))))
            for t0 in range(NTC, NT, 4):
                blocks.append(("l", list(range(t0, min(NT, t0 + 4)))))
            cur = None
            for s_, tiles in blocks:
                if s_ != cur:
                    cur = s_
                    r_ = NB if s_ == "c" else b
                    P.dma(q(), g1[:], MOD[i, r_:r_ + 1, 2 * D:3 * D].partition_broadcast(128))
                    P.dma(q(), S2[:], MOD[i, r_:r_ + 1, 3 * D:4 * D].partition_broadcast(128))
                    P.dma(q(), sq[:], MOD[i, r_:r_ + 1, 4 * D:5 * D].partition_broadcast(128))
                    P.dma(q(), g2[:], MOD[i, r_:r_ + 1, 5 * D:6 * D].partition_broadcast(128))
                    P.dma(q(), h2[:], A["norm_mlp"][i:i + 1, :].partition_broadcast(128))
                    P.dve("scalar_tensor_tensor", out=G2[:], in0=sq[:], scalar=1.0, in1=h2[:], op0=ALU.add, op1=ALU.mult)
                ntl = len(tiles)
                for nh in range(2):
                    P.dma(q(), W1b[nh][:], A["w_out"][i][:, nh * 512:(nh + 1) * 512].rearrange("(kc p) n -> p kc n", p=128))
                    P.pool("tensor_copy", out=Wout[:, :, nh * 512:(nh + 1) * 512], in_=W1b[nh][:])
                for j, t in enumerate(tiles):
                    P.dma(q(), xb[:, j, :], X[t * 128:(t + 1) * 128, :])
                    P.dma(q(), ot[:], O[t * 128:(t + 1) * 128, :])
                    for half in range(2):
                        pp = ps()
                        for jj in range(4):
                            kc = half * 4 + jj
                            P.pe("transpose", out=pp[:, jj * 128:(jj + 1) * 128], in_=ot[:, kc * 128:(kc + 1) * 128], identity=ident)
                        evac(oT[:, half * 4:(half + 1) * 4, :], r3(pp[:, :], 4))
                    for nh in range(2):
                        pp = ps()
                        for kc in range(8):
                            P.pe("matmul", out=pp[:, :], lhsT=oT[:, kc, :], rhs=Wout[:, kc, nh * 512:(nh + 1) * 512], start=(kc == 0), stop=(kc == 7))
                        P.dve("tensor_tensor", out=sq[:, 0:512], in0=pp[:, :], in1=g1[:, nh * 512:(nh + 1) * 512], op=ALU.mult)
                        P.dve("tensor_tensor", out=xb[:, j, nh * 512:(nh + 1) * 512], in0=xb[:, j, nh * 512:(nh + 1) * 512], in1=sq[:, 0:512], op=ALU.add)
                    rms((sq, ss, rs), xb[:, j, :], G2[:], S2[:], h2[:])
                    for half in range(2):
                        pp = ps()
                        for jj in range(4):
                            kc = half * 4 + jj
                            P.pe("transpose", out=pp[:, jj * 128:(jj + 1) * 128], in_=h2[:, kc * 128:(kc + 1) * 128], identity=ident)
                        evac(h2T[:, half * 4:(half + 1) * 4, j * 128:(j + 1) * 128], r3(pp[:, :], 4))
                ntok = ntl * 128
                for n8 in range(8):
                    Wf_ = W1b[n8 % 2]
                    P.dma(q(), Wf_[:], A["mlp_w1"][i][:, n8 * 512:(n8 + 1) * 512].rearrange("(kc p) n -> p kc n", p=128))
                    W_ = W1bb[n8 % 2]
                    P.pool("tensor_copy", out=W_[:], in_=Wf_[:])
                    for f4 in range(4):
                        fc = n8 * 4 + f4
                        pp = ps()
                        for kc in range(8):
                            P.pe("matmul", out=pp[:, 0:ntok], lhsT=W_[:, kc, f4 * 128:(f4 + 1) * 128], rhs=h2T[:, kc, 0:ntok], start=(kc == 0), stop=(kc == 7))
                        P.act("activation", out=sq[:, 0:ntok], in_=pp[:, 0:ntok], func=AF.Relu)
                        P.dve("tensor_tensor", out=hid[:, fc, 0:ntok], in0=sq[:, 0:ntok], in1=sq[:, 0:ntok], op=ALU.mult)
                for fc in range(32):
                    Wf_ = W2r[fc % 2]
                    P.dma(q(), Wf_[:], A["mlp_w2"][i][fc * 128:(fc + 1) * 128, :])
                    W_ = W2rb[fc % 2]
                    P.pool("tensor_copy", out=W_[:], in_=Wf_[:])
                    for j in range(ntl):
                        for nh in range(2):
                            P.pe("matmul", out=PS[j * 2 + nh][:, :], lhsT=hid[:, fc, j * 128:(j + 1) * 128], rhs=W_[:, nh * 512:(nh + 1) * 512], start=(fc == 0), stop=(fc == 31))
                for j, t in enumerate(tiles):
                    for nh in range(2):
                        P.dve("tensor_tensor", out=sq[:, nh * 512:(nh + 1) * 512], in0=PS[j * 2 + nh][:, :], in1=g2[:, nh * 512:(nh + 1) * 512], op=ALU.mult)
                    P.pool("tensor_tensor", out=xb[:, j, :], in0=xb[:, j, :], in1=sq[:], op=ALU.add)
                    P.dma(q(), X[t * 128:(t + 1) * 128, :], xb[:, j, :])
            P.barrier()

    def phase_final(b):
        with ExitStack() as es:
            xt = [sbt(es, "fx%d" % j, [128, D]) for j in range(2)]
            yo = [sbt(es, "fy%d" % j, [128, D]) for j in range(2)]
            sq = sbt(es, "fsq", [128, D]); ss = sbt(es, "fss", [128, 1]); rs = sbt(es, "frs", [128, 1])
            Gf = sbt(es, "Gf", [128, D])
            P.dma(q(), Gf[:], A["final_norm"][0:1, :].partition_broadcast(128))
            evs = []
            for t in range(NTC, NT):
                x_, y_ = xt[t % 2], yo[t % 2]
                P.dma(q(), x_[:], X[t * 128:(t + 1) * 128, :])
                rms((sq, ss, rs), x_[:], Gf[:], None, y_[:])
                evs.append(P.dma(q(), out_ap[b, (t - NTC) * 128:(t - NTC + 1) * 128, :], y_[:]))
            P.barrier()
            return evs

    def attn(i, j, b, need_ctx, kind):
        with ExitStack() as es:
            QT = sbt(es, "QT", [64, 8, T_], BF16); KT = sbt(es, "KT", [64, 2, T_], BF16)
            Vtm = sbt(es, "Vtm", [128, NT, 2, 65], BF16)
            Cc = sbt(es, "Cc", [64, T_]); Ss = sbt(es, "Ss", [64, T_])
            ch = [sbt(es, "ch%d" % k_, [128, 512]) for k_ in range(2)]
            sq = sbt(es, "asq", [128, 512]); rstd = sbt(es, "arstd", [128, 512]); qn = sbt(es, "aqn", [128, 512])
            t1 = sbt(es, "at1", [64, 512]); t2 = sbt(es, "at2", [64, 512])
            ptb = [sbt(es, "ptb%d" % k_, [128, 512], BF16) for k_ in range(3)]
            osb = [sbt(es, "osb%d" % k_, [128, 512]) for k_ in range(2)]
            dn = sbt(es, "dn", [128, 4]); nwq = sbt(es, "nwq", [128, 1]); nwk = sbt(es, "nwk", [128, 1])
            esink = sbt(es, "esink", [128, 8]); vch = sbt(es, "vch", [128, T_])
            P.dma(q(), Cc[:], A["rope"][:, 0, :]); P.dma(q(), Ss[:], A["rope"][:, 1, :])
            P.pool("memset", ap=Vtm[:], constant=1.0)
            if kind == "a":
                for hp in range(2):
                    P.dma(q(), nwq[hp * 64:(hp + 1) * 64, :], A["a_q_norm"][j:j + 1, :].rearrange("o d -> d o"))
                    P.dma(q(), nwk[hp * 64:(hp + 1) * 64, :], A["a_k_norm"][j:j + 1, :].rearrange("o d -> d o"))
            else:
                P.dma(q(), esink[:], A["c_sink"][j:j + 1, :].partition_broadcast(128))
                P.act("activation", out=esink[:], in_=esink[:], func=AF.Exp)
            k_ = 0
            for c in range(5):
                for t0 in range(0, T_, 512):
                    n = min(512, T_ - t0)
                    ch_ = ch[k_ % 2]; k_ += 1
                    P.dma(q(), ch_[:, 0:n], PT[c * 128:(c + 1) * 128, t0:t0 + n])
                    if kind == "a":
                        P.act("activation", out=sq[:, 0:n], in_=ch_[:, 0:n], func=AF.Square)
                        pp = ps()
                        P.pe("matmul", out=pp[:, 0:n], lhsT=C("bones"), rhs=sq[:, 0:n], start=True, stop=True)
                        P.dve("tensor_scalar", out=rstd[:, 0:n], in0=pp[:, 0:n], scalar1=1.0 / 64, scalar2=EPS, op0=ALU.mult, op1=ALU.add)
                        P.act("activation", out=rstd[:, 0:n], in_=rstd[:, 0:n], func=AF.Sqrt)
                        P.dve("reciprocal", out=rstd[:, 0:n], in_=rstd[:, 0:n])
                        nw_ = nwq if c < 4 else nwk
                        P.dve("scalar_tensor_tensor", out=qn[:, 0:n], in0=ch_[:, 0:n], scalar=nw_[:, 0:1], in1=rstd[:, 0:n], op0=ALU.mult, op1=ALU.mult)
                        src = qn
                    else:
                        src = ch_
                    for hp in range(2):
                        p1 = ps(); p2 = ps()
                        P.pe("matmul", out=p1[0:64, 0:n], lhsT=C("selI", lo=hp * 64, n=64), rhs=src[:, 0:n], start=True, stop=True)
                        P.pe("matmul", out=p2[0:64, 0:n], lhsT=C("selR", lo=hp * 64, n=64), rhs=src[:, 0:n], start=True, stop=True)
                        P.dve("tensor_tensor", out=t1[:, 0:n], in0=p1[0:64, 0:n], in1=Cc[:, t0:t0 + n], op=ALU.mult)
                        P.dve("tensor_tensor", out=t2[:, 0:n], in0=p2[0:64, 0:n], in1=Ss[:, t0:t0 + n], op=ALU.mult)
                        dst = QT[:, 2 * c + hp, t0:t0 + n] if c < 4 else KT[:, hp, t0:t0 + n]
                        P.pool("tensor_tensor", out=dst, in0=t1[:, 0:n], in1=t2[:, 0:n], op=ALU.add)
            P.dma(q(), vch[:], PT[640:768, :])
            for t in range(NT):
                pp = ps()
                P.pe("transpose", out=pp[:, 0:128], in_=vch[:, t * 128:(t + 1) * 128], identity=ident)
                evac(Vtm[:, t, :, 0:64], r3(pp[:, 0:128], 2))
            qblocks = list(range(NTC, NT)) + (list(range(NTC)) if need_ctx else [])
            kk_ = 0
            for qt in qblocks:
                if qt < NTC:
                    keys = [(kt, None) for kt in range(NTC)]
                elif kind == "a":
                    keys = [(kt, None) for kt in range(NT)]
                else:
                    keys = [(kt, None) for kt in range(NTC)]
                    if qt - 1 >= NTC:
                        keys.append((qt - 1, "m_il"))
                    keys.append((qt, None))
                    if qt + 1 < NT:
                        keys.append((qt + 1, "m_iu"))
                o_ = osb[kk_ % 2]; kk_ += 1
                for g in range(2):
                    acc = PS[g]
                    nk = len(keys)

                    def smm(idx):
                        kt, m = keys[idx]
                        st["aps"] = st.get("aps", 0) + 1
                        sp_ = PS[2 + st["aps"] % 6]
                        P.pe("matmul", out=r3(sp_[:, :], 4), lhsT=KT[:, g, kt * 128:(kt + 1) * 128], rhs=QT[:, 4 * g:4 * g + 4, qt * 128:(qt + 1) * 128], start=True, stop=True)
                        return sp_

                    pend = smm(0)
                    for idx, (kt, m) in enumerate(keys):
                        sp_ = pend
                        if idx + 1 < nk:
                            pend = smm(idx + 1)
                        pt_ = ptb[idx % 3]
                        P.act("activation", out=pt_[:], in_=sp_[:, :], func=AF.Exp, scale=0.125)
                        if m:
                            P.dve("tensor_tensor", out=r3(pt_[:], 4), in0=r3(pt_[:], 4), in1=bc(CT, C(m).ap, [128, 4, 128], 1), op=ALU.mult)
                        for r in range(4):
                            P.pe("matmul", out=acc[:, r * 65:(r + 1) * 65], lhsT=pt_[:, r * 128:(r + 1) * 128], rhs=Vtm[:, kt, g, :], start=(idx == 0 and r == 0), stop=(idx == nk - 1 and r == 3))
                    accv = V(acc, None, acc.h[:, 0:260].rearrange("p (a b) -> p a b", a=4))
                    den = V(acc, None, accv.ap[:, :, 64])
                    if kind == "c":
                        P.dve("tensor_tensor", out=dn[:], in0=den, in1=esink[:, 4 * g:4 * g + 4], op=ALU.add)
                    else:
                        P.dve("tensor_copy", out=dn[:], in_=den)
                    P.dve("reciprocal", out=dn[:], in_=dn[:])
                    P.dve("tensor_tensor", out=r3(o_[:, g * 256:(g + 1) * 256], 4), in0=V(acc, None, accv.ap[:, :, 0:64]),
                          in1=V(dn, None, dn.h[:, :].unsqueeze(2).to_broadcast([128, 4, 64])), op=ALU.mult)
                P.dma(q(), O[qt * 128:(qt + 1) * 128, 0:512], o_[:])
            P.barrier()


    def ssd(i, j, b, need_ctx):
        TP = T_ + 8
        with ExitStack() as es:
            Xtm = sbt(es, "Xtm", [128, NT, 512]); BT = sbt(es, "BT", [128, 2, T_]); Btm = sbt(es, "Btm", [128, NT, 256])
            CTt = sbt(es, "CTt", [128, 2, T_]); Yacc = sbt(es, "Yacc", [128, NT, 512])
            buf = [sbt(es, "cbuf%d" % k_, [128, TP]) for k_ in range(1)]
            up = sbt(es, "up", [128, TP]); uo = sbt(es, "uo", [128, T_])
            cw = sbt(es, "cw", [128, 8, 5]); cb = sbt(es, "cb", [128, 8]); Abc = sbt(es, "Abc", [128, 16]); dtb = sbt(es, "dtb", [16, 1])
            Dsk = sbt(es, "Dsk", [128, 8]); nwd = sbt(es, "nwd", [128, 512])
            dt_tm = sbt(es, "dt_tm", [128, NT, 16]); a_tm = sbt(es, "a_tm", [128, NT, 16])
            ST = sbt(es, "ST", [128, 2, 256]); aTri = sbt(es, "aTri", [128, 8, 128]); acs = sbt(es, "acs", [128, 16])
            tmp = sbt(es, "stmp", [128, 8, 128]); CBs = sbt(es, "CBs", [128, 2, 128]); eacs = sbt(es, "eacs", [128, 8])
            dte = sbt(es, "dte", [128, 8]); cdec = sbt(es, "cdec", [128, 8]); xw = sbt(es, "xw", [128, 512]); tY = sbt(es, "tY", [128, 512])
            zt = sbt(es, "zt", [128, 4, 128]); u = sbt(es, "su", [128, 512]); ss2 = sbt(es, "ss2", [128, 2]); osb = [sbt(es, "sosb%d" % k_, [128, 512]) for k_ in range(2)]
            sqd = sbt(es, "sqd", [128, 256])
            P.dma(q(), cw[:], A["d_conv_wT"][j]); P.dma(q(), cb[:], A["d_conv_b_col"][j])
            P.dma(q(), Abc[:], A["d_A_log"][j:j + 1, :].partition_broadcast(128))
            P.act("activation", out=Abc[:], in_=Abc[:], func=AF.Exp)
            P.dve("tensor_scalar", out=Abc[:], in0=Abc[:], scalar1=-1.0, scalar2=None, op0=ALU.mult)
            P.dma(q(), dtb[:], A["d_dt_bias"][j:j + 1, :].rearrange("o d -> d o"))
            P.dma(q(), Dsk[:], A["d_D"][j:j + 1, :].partition_broadcast(128))
            P.dma(q(), nwd[:], A["d_norm_w"][j:j + 1, :].partition_broadcast(128))
            P.pool("memset", ap=buf[0][:], constant=0.0)
            P.pool("memset", ap=Yacc[:], constant=0.0)
            for c in range(8):
                bf = buf[0]
                r0 = 1280 + c * 128
                P.dma(q(), bf[:, 2:2 + TC], PT[r0:r0 + 128, 0:TC])
                P.dma(q(), bf[:, TC + 6:TC + 6 + TL], PT[r0:r0 + 128, TC:T_])
                W_ = T_ + 4
                P.dve("tensor_scalar", out=up[:, 2:2 + W_], in0=bf[:, 0:W_], scalar1=cw[:, c, 0:1], scalar2=None, op0=ALU.mult)
                for k_ in range(1, 5):
                    P.dve("scalar_tensor_tensor", out=up[:, 2:2 + W_], in0=bf[:, k_:k_ + W_], scalar=cw[:, c, k_:k_ + 1], in1=up[:, 2:2 + W_], op0=ALU.mult, op1=ALU.add)
                if c < 4:
                    dst = uo
                elif c < 6:
                    dst = V(BT, None, BT.h[:, c - 4, :])
                else:
                    dst = V(CTt, None, CTt.h[:, c - 6, :])
                dv = (lambda lo, hi: dst[:, lo:hi]) if c < 4 else (lambda lo, hi: V(dst.t, None, dst.ap[:, lo:hi]))
                P.act("activation", out=dv(0, TC), in_=up[:, 2:2 + TC], func=AF.Silu, bias=cb[:, c:c + 1])
                P.act("activation", out=dv(TC, T_), in_=up[:, TC + 6:TC + 6 + TL], func=AF.Silu, bias=cb[:, c:c + 1])
                if c < 6:
                    for t in range(NT):
                        pp = ps()
                        src = uo[:, t * 128:(t + 1) * 128] if c < 4 else BT[:, c - 4, t * 128:(t + 1) * 128]
                        P.pe("transpose", out=pp[:, 0:128], in_=src, identity=ident)
                        dd = Xtm[:, t, c * 128:(c + 1) * 128] if c < 4 else Btm[:, t, (c - 4) * 128:(c - 3) * 128]
                        evac(dd, pp[:, 0:128])
            dtT = _Sub(up, up.h[0:16, 0:T_])
            P.dma(q(), dtT[:, :], PT[2304:2320, :])
            P.act("activation", out=dtT[:, :], in_=dtT[:, :], func=AF.Exp, bias=dtb[:, 0:1])
            P.act("activation", out=dtT[:, :], in_=dtT[:, :], func=AF.Ln, bias=1.0)
            for t in range(NT):
                pp = ps()
                P.pe("transpose", out=pp[:, 0:16], in_=dtT[:, t * 128:(t + 1) * 128], identity=C("ident", slice(0, 16), 0, 16))
                evac(dt_tm[:, t, :], pp[:, 0:16])
                P.dve("tensor_tensor", out=a_tm[:, t, :], in0=dt_tm[:, t, :], in1=Abc[:], op=ALU.mult)
            for d in range(2):
                order = list(range(NT)) if d == 0 else (list(range(NTC - 1, -1, -1)) + list(range(NT - 1, NTC - 1, -1)))
                tri = C("m_iu") if d == 0 else C("m_il")
                neg = "neg_iu" if d == 0 else "neg_il"
                P.pool("memset", ap=ST[:], constant=0.0)
                for t in order:
                    a_ = a_tm[:, t, d * 8:(d + 1) * 8]; dt_ = dt_tm[:, t, d * 8:(d + 1) * 8]
                    pA = ps()
                    P.pe("matmul", out=pA[:, 0:8], lhsT=tri, rhs=a_, start=True, stop=False)
                    P.pe("matmul", out=pA[:, 8:16], lhsT=C("ones"), rhs=a_, start=False, stop=True)
                    P.dve("tensor_tensor", out=aTri[:], in0=bc(a_tm, a_tm.h[:, t, d * 8:(d + 1) * 8], [128, 8, 128], 2),
                          in1=bc(CT, tri.ap, [128, 8, 128], 1), op=ALU.mult)
                    evac(acs[:], pA[:, 0:16])
                    pC = ps()
                    for g in range(2):
                        P.pe("matmul", out=pC[:, g * 128:(g + 1) * 128], lhsT=BT[:, g, t * 128:(t + 1) * 128], rhs=CTt[:, g, t * 128:(t + 1) * 128], start=(g == 0), stop=(g == 1))
                    evac(CBs[:], r3(pC[:, 0:256], 2))
                    for g in range(2):
                        pR = ps()
                        P.pe("matmul", out=pR[:, :], lhsT=C("ones"), rhs=V(aTri, None, aTri.h[:, 4 * g:4 * g + 4, :].rearrange("p a b -> p (a b)")), start=True, stop=True)
                        tg = V(tmp, None, tmp.h[:, 4 * g:4 * g + 4, :])
                        P.dve("tensor_tensor", out=tg, in0=r3(pR[:, :], 4), in1=bc(CT, C(neg).ap, [128, 4, 128], 1), op=ALU.add)
                        P.dve("tensor_tensor", out=tg, in0=tg, in1=bc(acs, acs.h[:, 4 * g:4 * g + 4], [128, 4, 128], 2), op=ALU.subtract)
                        P.act("activation", out=tg, in_=tg, func=AF.Exp)
                        P.dve("tensor_tensor", out=tg, in0=tg, in1=bc(CBs, CBs.h[:, g, :], [128, 4, 128], 1), op=ALU.mult)
                        P.dve("tensor_tensor", out=tg, in0=tg, in1=bc(dt_tm, dt_tm.h[:, t, d * 8 + 4 * g:d * 8 + 4 * g + 4], [128, 4, 128], 2), op=ALU.mult)
                    pY = ps()
                    for hh in range(8):
                        P.pe("matmul", out=pY[:, hh * 64:(hh + 1) * 64], lhsT=tmp[:, hh, :], rhs=Xtm[:, t, hh * 64:(hh + 1) * 64], start=(hh == 0), stop=(hh == 7))
                    pO = ps()
                    for g in range(2):
                        P.pe("matmul", out=pO[:, g * 256:(g + 1) * 256], lhsT=CTt[:, g, t * 128:(t + 1) * 128], rhs=ST[:, g, :], start=(g == 0), stop=(g == 1))
                    P.act("activation", out=eacs[:], in_=acs[:, 0:8], func=AF.Exp)
                    P.dve("tensor_tensor", out=r3(tY[:], 8), in0=r3(pO[:, :], 8), in1=bc(eacs, eacs.h[:, :], [128, 8, 64], 2), op=ALU.mult)
                    P.dve("tensor_tensor", out=tY[:], in0=tY[:], in1=pY[:, :], op=ALU.add)
                    P.pool("tensor_tensor", out=Yacc[:, t, :], in0=Yacc[:, t, :], in1=tY[:], op=ALU.add)
                    P.dve("tensor_tensor", out=dte[:], in0=acs[:, 8:16], in1=acs[:, 0:8], op=ALU.subtract)
                    P.act("activation", out=dte[:], in_=dte[:], func=AF.Exp)
                    P.dve("tensor_tensor", out=dte[:], in0=dte[:], in1=dt_, op=ALU.mult)
                    P.dve("tensor_tensor", out=r3(xw[:], 8), in0=r3(Xtm[:, t, :], 8), in1=bc(dte, dte.h[:, :], [128, 8, 64], 2), op=ALU.mult)
                    pS = ps()
                    for g in range(2):
                        P.pe("matmul", out=pS[:, g * 256:(g + 1) * 256], lhsT=Btm[:, t, g * 128:(g + 1) * 128], rhs=xw[:, g * 256:(g + 1) * 256], start=(g == 0), stop=(g == 1))
                    P.act("activation", out=cdec[:], in_=acs[:, 8:16], func=AF.Exp)
                    st3 = V(ST, None, ST.h[:, :, :].rearrange("p g (a b) -> p (g a) b", a=4))
                    P.dve("tensor_tensor", out=st3, in0=st3, in1=bc(cdec, cdec.h[:, :], [128, 8, 64], 2), op=ALU.mult)
                    P.dve("tensor_tensor", out=V(ST, None, ST.h[:, :, :].rearrange("p g b -> p (g b)")), in0=V(ST, None, ST.h[:, :, :].rearrange("p g b -> p (g b)")), in1=pS[:, :], op=ALU.add)
            tiles = list(range(NTC, NT)) + (list(range(NTC)) if need_ctx else [])
            for kk_, t in enumerate(tiles):
                o_ = osb[kk_ % 2]
                P.dma(q(), zt[:], PT[768:1280, t * 128:(t + 1) * 128].rearrange("(c p) t -> p c t", p=128))
                P.act("activation", out=zt[:], in_=zt[:], func=AF.Silu)
                pZ = ps()
                for c in range(4):
                    P.pe("transpose", out=pZ[:, c * 128:(c + 1) * 128], in_=zt[:, c, :], identity=ident)
                P.dve("tensor_tensor", out=r3(u[:], 8), in0=r3(Xtm[:, t, :], 8), in1=bc(Dsk, Dsk.h[:, :], [128, 8, 64], 2), op=ALU.mult)
                P.dve("tensor_tensor", out=u[:], in0=u[:], in1=Yacc[:, t, :], op=ALU.add)
                P.dve("tensor_tensor", out=u[:], in0=u[:], in1=pZ[:, :], op=ALU.mult)
                P.pool("memset", ap=ss2[:], constant=0.0)
                for g in range(2):
                    P.act("activation", out=sqd[:], in_=u[:, g * 256:(g + 1) * 256], func=AF.Square, accum_out=ss2[:, g:g + 1])
                P.dve("tensor_scalar", out=ss2[:], in0=ss2[:], scalar1=1.0 / 256, scalar2=EPS, op0=ALU.mult, op1=ALU.add)
                P.act("activation", out=ss2[:], in_=ss2[:], func=AF.Sqrt)
                P.dve("reciprocal", out=ss2[:], in_=ss2[:])
                for g in range(2):
                    P.dve("scalar_tensor_tensor", out=o_[:, g * 256:(g + 1) * 256], in0=u[:, g * 256:(g + 1) * 256], scalar=ss2[:, g:g + 1], in1=nwd[:, g * 256:(g + 1) * 256], op0=ALU.mult, op1=ALU.mult)
                P.dma(q(), O[t * 128:(t + 1) * 128, 512:1024], o_[:])
            P.barrier()

    def rwkv(i, j, b, need_ctx):
        TP = T_ + 8
        W_ = T_ + 4
        with ExitStack() as es:
            big = lambda nm: sbt(es, nm, [128, T_])
            rT, kT, kkT, sgT, kdT, beT, twd, adT, sgd = [big(n_) for n_ in ("rT", "kT", "kkT", "sgT", "kdT", "beT", "twd", "adT", "sgd")]
            bf = sbt(es, "rbf", [128, TP]); up = sbt(es, "rup", [128, TP])
            Yp = sbt(es, "Yp", [128, NT, 128]); Vp = sbt(es, "Vp", [128, NT, 128]); bon = sbt(es, "bon", [128, NT, 2])
            mp = sbt(es, "mp", [128, 15]); mn = sbt(es, "mn", [128, 15]); m0 = sbt(es, "m0", [128, 15])
            w0c = sbt(es, "w0c", [128, 8]); a0c = sbt(es, "a0c", [128, 8]); kkc = sbt(es, "kkc", [128, 4]); kac = sbt(es, "kac", [128, 4])
            omka = sbt(es, "omka", [128, 4]); rkc = sbt(es, "rkc", [128, 4])
            w2s = sbt(es, "w2s", [128, 512]); a2s = sbt(es, "a2s", [128, 512]); g2s = sbt(es, "g2s", [128, 512])
            lnw = sbt(es, "lnw", [128, 512]); lnb = sbt(es, "lnb", [128, 512])
            sm = lambda nm, w=128: sbt(es, nm, [128, w])
            sgtm, cums, epos, eneg, eexc, etc_, kap, kti, bti, rti, K2T, B2T = [sm(n_) for n_ in ("sgtm", "cums", "epos", "eneg", "eexc", "etc", "kap", "kti", "bti", "rti", "K2T", "B2T")]
            cums = sm("cums2", 256); KB2 = sm("KB2", 256); gC = sm("gC", 1)
            Ns = [sm("Ns%d" % k_, 512) for k_ in range(2)]
            AukT = sm("AukT", 256); ArkT = sm("ArkT", 256); nArbT = sm("nArbT", 256)
            Wsb = sm("Wsb", 128); H = sm("Hst", 64)
            t512 = sm("t512", 512); t512b = sm("t512b", 512)
            ypost = sm("ypost", 128); cen = sm("cen", 128); mu = sm("mu", 2); var = sm("var", 2); ob = [sm("rob%d" % k_, 128) for k_ in range(2)]
            P.pool("memset", ap=bf[:], constant=0.0)
            P.dma(q(), mp[:], A["b_mu_prev_col"][j]); P.dma(q(), mn[:], A["b_mu_next_col"][j])
            P.dve("tensor_tensor", out=m0[:], in0=mp[:], in1=mn[:], op=ALU.add)
            P.dve("tensor_scalar", out=m0[:], in0=m0[:], scalar1=-1.0, scalar2=1.0, op0=ALU.mult, op1=ALU.add)
            P.dma(q(), w0c[:], A["b_w0_col"][j]); P.dma(q(), a0c[:], A["b_a0_col"][j])
            P.dma(q(), kkc[:], A["b_k_k_col"][j]); P.dma(q(), kac[:], A["b_k_a_col"][j]); P.dma(q(), rkc[:], A["b_r_k_col"][j])
            P.dve("tensor_scalar", out=omka[:], in0=kac[:], scalar1=-1.0, scalar2=1.0, op0=ALU.mult, op1=ALU.add)
            P.dma(q(), w2s[:], A["b_w2"][j]); P.dma(q(), a2s[:], A["b_a2"][j]); P.dma(q(), g2s[:], A["b_g2"][j])
            P.dma(q(), lnw[:], A["b_ln_w"][j:j + 1, :].partition_broadcast(128)); P.dma(q(), lnb[:], A["b_ln_b"][j:j + 1, :].partition_broadcast(128))

            def shift(fidx, dst, func):
                r0 = 768 + fidx * 128
                P.dma(q(), bf[:, 2:2 + TC], PT[r0:r0 + 128, 0:TC])
                P.dma(q(), bf[:, TC + 6:TC + 6 + TL], PT[r0:r0 + 128, TC:T_])
                P.dve("tensor_scalar", out=up[:, 2:2 + W_], in0=bf[:, 2:2 + W_], scalar1=m0[:, fidx:fidx + 1], scalar2=None, op0=ALU.mult)
                P.dve("scalar_tensor_tensor", out=up[:, 2:2 + W_], in0=bf[:, 1:1 + W_], scalar=mp[:, fidx:fidx + 1], in1=up[:, 2:2 + W_], op0=ALU.mult, op1=ALU.add)
                P.dve("scalar_tensor_tensor", out=up[:, 2:2 + W_], in0=bf[:, 3:3 + W_], scalar=mn[:, fidx:fidx + 1], in1=up[:, 2:2 + W_], op0=ALU.mult, op1=ALU.add)
                P.act("activation", out=dst[:, 0:TC], in_=up[:, 2:2 + TC], func=func)
                P.act("activation", out=dst[:, TC:T_], in_=up[:, TC + 6:TC + 6 + TL], func=func)

            shift(12, twd, AF.Tanh); shift(13, adT, AF.Copy); shift(14, sgd, AF.Sigmoid)
            out_tiles = list(range(NTC, NT)) + (list(range(NTC)) if need_ctx else [])
            for c in range(4):
                shift(c, rT, AF.Copy); shift(4 + c, kT, AF.Copy); shift(8 + c, sgT, AF.Copy)
                for t in range(NT):
                    pp = ps()
                    P.pe("transpose", out=pp[:, 0:128], in_=sgT[:, t * 128:(t + 1) * 128], identity=ident)
                    evac(Vp[:, t, :], pp[:, 0:128])
                P.dve("tensor_scalar", out=kkT[:], in0=kT[:], scalar1=kkc[:, c:c + 1], scalar2=None, op0=ALU.mult)
                for t0 in range(0, T_, 512):
                    n = min(512, T_ - t0)
                    P.act("activation", out=t512[:, 0:n], in_=kkT[:, t0:t0 + n], func=AF.Square)
                    pp = ps()
                    P.pe("matmul", out=pp[:, 0:n], lhsT=C("bones"), rhs=t512[:, 0:n], start=True, stop=True)
                    P.dve("tensor_scalar", out=t512[:, 0:n], in0=pp[:, 0:n], scalar1=1e-12, scalar2=None, op0=ALU.add)
                    P.act("activation", out=t512[:, 0:n], in_=t512[:, 0:n], func=AF.Sqrt)
                    P.dve("reciprocal", out=t512[:, 0:n], in_=t512[:, 0:n])
                    P.dve("tensor_tensor", out=kkT[:, t0:t0 + n], in0=kkT[:, t0:t0 + n], in1=t512[:, 0:n], op=ALU.mult)
                P.pool("memset", ap=Yp[:], constant=0.0)
                P.pool("memset", ap=bon[:], constant=0.0)
                for d in range(2):
                    aT = V(up, None, up.h[:, 0:T_])
                    for t0 in range(0, T_, 512):
                        n = min(512, T_ - t0)
                        pp = ps()
                        P.pe("matmul", out=pp[:, 0:n], lhsT=w2s[d * 64:(d + 1) * 64, c * 128:(c + 1) * 128], rhs=twd[d * 64:(d + 1) * 64, t0:t0 + n], start=True, stop=True)
                        P.act("activation", out=sgT[:, t0:t0 + n], in_=pp[:, 0:n], func=AF.Sigmoid, bias=w0c[:, d * 4 + c:d * 4 + c + 1])
                        pp = ps()
                        P.pe("matmul", out=pp[:, 0:n], lhsT=a2s[d * 64:(d + 1) * 64, c * 128:(c + 1) * 128], rhs=adT[d * 64:(d + 1) * 64, t0:t0 + n], start=True, stop=True)
                        P.act("activation", out=V(up, None, up.h[:, t0:t0 + n]), in_=pp[:, 0:n], func=AF.Sigmoid, bias=a0c[:, d * 4 + c:d * 4 + c + 1])
                    P.dve("tensor_tensor", out=beT[:], in0=aT, in1=kkT[:], op=ALU.mult)
                    P.dve("tensor_scalar", out=kdT[:], in0=aT, scalar1=kac[:, c:c + 1], scalar2=omka[:, c:c + 1], op0=ALU.mult, op1=ALU.add)
                    P.dve("tensor_tensor", out=kdT[:], in0=kdT[:], in1=kT[:], op=ALU.mult)
                    P.dve("scalar_tensor_tensor", out=aT, in0=rT[:], scalar=rkc[:, c:c + 1], in1=kdT[:], op0=ALU.mult, op1=ALU.mult)
                    pB = ps()
                    for t in range(NT):
                        P.pe("matmul", out=pB[:, t * 2:(t + 1) * 2], lhsT=V(up, None, up.h[:, t * 128:(t + 1) * 128]), rhs=C("hsel"), start=(t == 0), stop=(t == NT - 1))
                    P.dve("tensor_tensor", out=V(bon, None, bon.h[:, :, :].rearrange("p a b -> p (a b)")), in0=V(bon, None, bon.h[:, :, :].rearrange("p a b -> p (a b)")), in1=pB[:, 0:2 * NT], op=ALU.add)
                    order = list(range(NT)) if d == 0 else (list(range(NTC - 1, -1, -1)) + list(range(NT - 1, NTC - 1, -1)))
                    triS = C("triF") if d == 0 else C("triB")
                    nmA, nmB = ("nm_sl", "nm_su") if d == 0 else ("nm_su", "nm_sl")
                    mB = "m_su" if d == 0 else "m_sl"
                    iB, niB = ("m_iu", "nm_iu") if d == 0 else ("m_il", "nm_il")
                    P.pool("memset", ap=H[:], constant=0.0)
                    mk2 = lambda nm: bc(CT, C(nm).ap, [128, 2, 128], 1)
                    for t in order:
                        tl = slice(t * 128, (t + 1) * 128)
                        pp = ps()
                        P.pe("transpose", out=pp[:, 0:128], in_=sgT[:, tl], identity=ident)
                        evac(sgtm[:], pp[:, 0:128])
                        pc = ps()
                        P.pe("matmul", out=pc[:, 0:128], lhsT=sgtm[:], rhs=triS, start=True, stop=False)
                        P.pe("matmul", out=pc[:, 128:256], lhsT=sgtm[:], rhs=C("allS"), start=False, stop=True)
                        evac(cums[:], pc[:, 0:256])
                        P.act("activation", out=epos[:], in_=cums[:, 0:128], func=AF.Exp)
                        P.act("activation", out=eneg[:], in_=cums[:, 0:128], func=AF.Exp, scale=-1.0)
                        P.dve("scalar_tensor_tensor", out=eexc[:], in0=sgT[:, tl], scalar=WDEC, in1=cums[:, 0:128], op0=ALU.mult, op1=ALU.add)
                        P.act("activation", out=eexc[:], in_=eexc[:], func=AF.Exp)
                        P.dve("tensor_tensor", out=etc_[:], in0=cums[:, 128:256], in1=cums[:, 0:128], op=ALU.subtract)
                        P.act("activation", out=etc_[:], in_=etc_[:], func=AF.Exp)
                        P.act("activation", out=gC[:], in_=cums[:, 128:129], func=AF.Exp)
                        P.pool("tensor_tensor", out=kap[:], in0=kkT[:, tl], in1=eexc[:], op=ALU.mult)
                        P.dve("tensor_tensor", out=kti[:], in0=kdT[:, tl], in1=eneg[:], op=ALU.mult)
                        P.pool("tensor_tensor", out=bti[:], in0=beT[:, tl], in1=eneg[:], op=ALU.mult)
                        P.dve("tensor_tensor", out=rti[:], in0=rT[:, tl], in1=epos[:], op=ALU.mult)
                        P.pool("tensor_tensor", out=K2T[:], in0=kdT[:, tl], in1=etc_[:], op=ALU.mult)
                        P.dve("tensor_tensor", out=B2T[:], in0=beT[:, tl], in1=etc_[:], op=ALU.mult)
                        pT = ps()
                        P.pe("transpose", out=pT[:, 0:128], in_=K2T[:], identity=ident)
                        P.pe("transpose", out=pT[:, 128:256], in_=B2T[:], identity=ident)
                        P.act("copy", out=KB2[:, 0:128], in_=pT[:, 0:128])
                        P.dve("tensor_scalar", out=KB2[:, 128:256], in0=pT[:, 128:256], scalar1=-1.0, scalar2=None, op0=ALU.mult)
                        pN = ps(); pA = ps(); pB2 = ps()
                        for hp in range(2):
                            sl = slice(hp * 64, (hp + 1) * 64)
                            P.pe("matmul", out=pN[:, hp * 128:(hp + 1) * 128], lhsT=kap[sl, :], rhs=bti[sl, :], start=(hp == 0), stop=False)
                            P.pe("matmul", out=pN[:, 256 + hp * 128:256 + (hp + 1) * 128], lhsT=bti[sl, :], rhs=kap[sl, :], start=False, stop=(hp == 1))
                            P.pe("matmul", out=pA[:, hp * 128:(hp + 1) * 128], lhsT=kti[sl, :], rhs=kap[sl, :], start=(hp == 0), stop=False)
                            P.pe("matmul", out=pA[:, 256 + hp * 128:256 + (hp + 1) * 128], lhsT=kti[sl, :], rhs=rti[sl, :], start=False, stop=(hp == 1))
                            P.pe("matmul", out=pB2[:, hp * 128:(hp + 1) * 128], lhsT=bti[sl, :], rhs=rti[sl, :], start=(hp == 0), stop=(hp == 1))
                        N0 = Ns[0]
                        P.dve("tensor_tensor", out=r3(N0[:, 0:256], 2), in0=r3(pN[:, 0:256], 2), in1=mk2(nmA), op=ALU.mult)
                        P.dve("tensor_tensor", out=r3(N0[:, 256:512], 2), in0=r3(pN[:, 256:512], 2), in1=mk2(nmB), op=ALU.mult)
                        P.dve("tensor_tensor", out=r3(AukT[:], 2), in0=r3(pA[:, 0:256], 2), in1=mk2(mB), op=ALU.mult)
                        P.dve("tensor_tensor", out=r3(ArkT[:], 2), in0=r3(pA[:, 256:512], 2), in1=mk2(iB), op=ALU.mult)
                        P.dve("tensor_tensor", out=r3(nArbT[:], 2), in0=r3(pB2[:, 0:256], 2), in1=mk2(niB), op=ALU.mult)
                        pW = ps()
                        for hp in range(2):
                            sl = slice(hp * 64, (hp + 1) * 64)
                            P.pe("matmul", out=pW[:, hp * 64:(hp + 1) * 64], lhsT=kap[sl, :], rhs=H[sl, :], start=(hp == 0), stop=False)
                            P.pe("matmul", out=pW[:, hp * 64:(hp + 1) * 64], lhsT=AukT[:, hp * 128:(hp + 1) * 128], rhs=Vp[:, t, hp * 64:(hp + 1) * 64], start=False, stop=(hp == 1))
                        evac(Wsb[:], pW[:, 0:128])
                        cur = 0
                        for lv in range(7):
                            Nc = Ns[cur]
                            pU = ps()
                            for hp in range(2):
                                P.pe("matmul", out=pU[:, hp * 64:(hp + 1) * 64], lhsT=Nc[:, 256 + hp * 128:256 + (hp + 1) * 128], rhs=Wsb[:, hp * 64:(hp + 1) * 64], start=(hp == 0), stop=(hp == 1))
                            if lv < 6:
                                pQ = ps()
                                for hp in range(2):
                                    P.pe("matmul", out=pQ[:, hp * 128:(hp + 1) * 128], lhsT=Nc[:, 256 + hp * 128:256 + (hp + 1) * 128], rhs=Nc[:, hp * 128:(hp + 1) * 128], start=(hp == 0), stop=False)
                                    P.pe("matmul", out=pQ[:, 256 + hp * 128:256 + (hp + 1) * 128], lhsT=Nc[:, hp * 128:(hp + 1) * 128], rhs=Nc[:, 256 + hp * 128:256 + (hp + 1) * 128], start=False, stop=(hp == 1))
                            P.dve("tensor_tensor", out=Wsb[:], in0=Wsb[:], in1=pU[:, 0:128], op=ALU.add)
                            if lv < 6:
                                cur ^= 1
                                P.act("copy", out=Ns[cur][:], in_=pQ[:, :])
                        pYh = ps()
                        for hp in range(2):
                            sl = slice(hp * 64, (hp + 1) * 64)
                            o_ = pYh[:, hp * 64:(hp + 1) * 64]
                            P.pe("matmul", out=o_, lhsT=rti[sl, :], rhs=H[sl, :], start=(hp == 0), stop=False)
                            P.pe("matmul", out=o_, lhsT=ArkT[:, hp * 128:(hp + 1) * 128], rhs=Vp[:, t, hp * 64:(hp + 1) * 64], start=False, stop=False)
                            P.pe("matmul", out=o_, lhsT=nArbT[:, hp * 128:(hp + 1) * 128], rhs=Wsb[:, hp * 64:(hp + 1) * 64], start=False, stop=(hp == 1))
                        P.dve("tensor_tensor", out=Yp[:, t, :], in0=Yp[:, t, :], in1=pYh[:, 0:128], op=ALU.add)
                        pH = ps()
                        P.pe("matmul", out=pH[:, 0:128], lhsT=KB2[:, 0:128], rhs=Vp[:, t, :], start=True, stop=False)
                        P.pe("matmul", out=pH[:, 0:128], lhsT=KB2[:, 128:256], rhs=Wsb[:], start=False, stop=True)
                        for hp in range(2):
                            sl = slice(hp * 64, (hp + 1) * 64)
                            P.dve("scalar_tensor_tensor", out=H[sl, :], in0=H[sl, :], scalar=gC[sl, 0:1], in1=pH[sl, hp * 64:(hp + 1) * 64], op0=ALU.mult, op1=ALU.add)
                for kk_, t in enumerate(out_tiles):
                    o_ = ob[kk_ % 2]
                    y3 = r3(Yp[:, t, :], 2)
                    P.dve("tensor_reduce", out=mu[:], in_=y3, axis=AX.X, op=ALU.add)
                    P.dve("tensor_scalar", out=mu[:], in0=mu[:], scalar1=1.0 / 64, scalar2=None, op0=ALU.mult)
                    P.dve("tensor_tensor", out=r3(cen[:], 2), in0=y3, in1=bc(mu, mu.h[:, :], [128, 2, 64], 2), op=ALU.subtract)
                    P.dve("tensor_tensor", out=ypost[:], in0=cen[:], in1=cen[:], op=ALU.mult)
                    P.dve("tensor_reduce", out=var[:], in_=r3(ypost[:], 2), axis=AX.X, op=ALU.add)
                    P.dve("tensor_scalar", out=var[:], in0=var[:], scalar1=1.0 / 64, scalar2=64e-5, op0=ALU.mult, op1=ALU.add)
                    P.act("activation", out=var[:], in_=var[:], func=AF.Sqrt)
                    P.dve("reciprocal", out=var[:], in_=var[:])
                    P.dve("tensor_tensor", out=r3(cen[:], 2), in0=r3(cen[:], 2), in1=bc(var, var.h[:, :], [128, 2, 64], 2), op=ALU.mult)
                    P.dve("tensor_tensor", out=cen[:], in0=cen[:], in1=lnw[:, c * 128:(c + 1) * 128], op=ALU.mult)
                    P.dve("tensor_tensor", out=cen[:], in0=cen[:], in1=lnb[:, c * 128:(c + 1) * 128], op=ALU.add)
                    P.dve("tensor_tensor", out=r3(ypost[:], 2), in0=r3(Vp[:, t, :], 2), in1=bc(bon, bon.h[:, t, :], [128, 2, 64], 2), op=ALU.mult)
                    P.dve("tensor_tensor", out=cen[:], in0=cen[:], in1=ypost[:], op=ALU.add)
                    pG = ps()
                    P.pe("matmul", out=pG[:, 0:128], lhsT=sgd[:, t * 128:(t + 1) * 128], rhs=g2s[:, c * 128:(c + 1) * 128], start=True, stop=True)
                    P.dve("tensor_tensor", out=o_[:], in0=cen[:], in1=pG[:, 0:128], op=ALU.mult)
                    P.dma(q(), O[t * 128:(t + 1) * 128, 512 + c * 128:512 + (c + 1) * 128], o_[:])
            P.barrier()

    def mix_ab(i, j, b, need_ctx):
        if "attn" not in cfg.dbg.get("skip", ()):
            attn(i, j, b, need_ctx, "a")
        if "rwkv" not in cfg.dbg.get("skip", ()):
            rwkv(i, j, b, need_ctx)

    def mix_cd(i, j, b, need_ctx):
        if "attn" not in cfg.dbg.get("skip", ()):
            attn(i, j, b, need_ctx, "c")
        if "ssd" not in cfg.dbg.get("skip", ()):
            ssd(i, j, b, need_ctx)

    def dcopy(dst, src, rows):
        for r0 in range(0, rows, 128):
            P.dma(q(), dst[r0:r0 + 128, :], src[r0:r0 + 128, :])

    phase_mod()
    final_evs = []
    for b in range(NB):
        dcopy(X[0:TC, :], A["ctx"][b], TC)
        dcopy(X[TC:T_, :], A["x"][b], TL)
        P.barrier()
        for i in range(DEPTH):
            need_ctx = i < DEPTH - 1
            j = i // 2
            if i % 2 == 0:
                phase_inproj(i, b, A["ab_w_in"][j], 2688, need_ctx)
            else:
                phase_inproj(i, b, A["cd_w_in"][j], 2320, need_ctx)
            if ("PT%d" % i) in dbg_out and b == 0:
                dcopy(dbg_out["PT%d" % i], PT, 2688)
                P.barrier()
            if "O_in" in cfg.dbg:
                dcopy(O, A["O_in"][i, b], T_)
                P.barrier()
            if i % 2 == 0:
                mix_ab(i, j, b, need_ctx)
            else:
                mix_cd(i, j, b, need_ctx)
            if ("O%d" % i) in dbg_out and b == 0:
                dcopy(dbg_out["O%d" % i], O, T_)
                P.barrier()
            phase_mlp(i, b, need_ctx)
            if ("X%d" % i) in dbg_out and b == 0:
                dcopy(dbg_out["X%d" % i], X, T_)
                P.barrier()
        final_evs += phase_final(b)
    for nm in dbg_out:
        pass
    P.barrier()
    P.emit(final_evs)
    P.close()
    return nc


def make_in_maps(cfg, inputs, n_cores):
    carr, _, rope = make_consts(cfg)
    NB = cfg.NB
    f = lambda a: np.ascontiguousarray(np.asarray(a, dtype=np.float32))
    shared = {}
    for k_ in ("w_mod", "b_mod", "norm_mix", "norm_mlp", "w_out", "mlp_w1", "mlp_w2", "ab_w_in", "a_q_norm", "a_k_norm",
               "b_mu_prev", "b_mu_next", "b_g2", "b_k_k", "b_k_a", "b_ln_w", "b_ln_b"):
        shared[k_] = f(inputs[k_])
    NE = shared["ab_w_in"].shape[0]
    shared["final_norm"] = f(inputs["final_norm"]).reshape(1, D)
    shared["b_w0"] = f(inputs["b_w0"]).reshape(NE, 1024); shared["b_a0"] = f(inputs["b_a0"]).reshape(NE, 1024)
    shared["b_w2"] = f(inputs["b_w2"]).reshape(NE, 128, 512); shared["b_a2"] = f(inputs["b_a2"]).reshape(NE, 128, 512)
    shared["b_r_k"] = f(inputs["b_r_k"]).reshape(NE, 512)
    col = lambda a, n: f(f(a).reshape(NE, n, 128).transpose(0, 2, 1))
    shared["b_mu_prev_col"] = col(inputs["b_mu_prev"], 15); shared["b_mu_next_col"] = col(inputs["b_mu_next"], 15)
    shared["b_w0_col"] = col(inputs["b_w0"], 8); shared["b_a0_col"] = col(inputs["b_a0"], 8)
    shared["b_k_k_col"] = col(inputs["b_k_k"], 4); shared["b_k_a_col"] = col(inputs["b_k_a"], 4); shared["b_r_k_col"] = col(inputs["b_r_k"], 4)
    if cfg.DEPTH // 2:
        NO = cfg.DEPTH // 2
        for k_ in ("cd_w_in", "c_sink", "d_conv_w", "d_conv_b", "d_D", "d_norm_w"):
            shared[k_] = f(inputs[k_])
        shared["d_conv_wT"] = f(f(inputs["d_conv_w"]).reshape(NO, 5, 8, 128).transpose(0, 3, 2, 1))
        shared["d_conv_b_col"] = f(f(inputs["d_conv_b"]).reshape(NO, 8, 128).transpose(0, 2, 1))
        shared["d_dt_bias"] = f(inputs["d_dt_bias"]).reshape(NO, 16); shared["d_A_log"] = f(inputs["d_A_log"]).reshape(NO, 16)
    shared["consts"] = carr; shared["rope"] = rope
    maps = []
    x, c, ctx, c_ctx = f(inputs["x"]), f(inputs["c"]), f(inputs["ctx"]), f(inputs["c_ctx"])
    for k_ in range(n_cores):
        m = dict(shared)
        sl = slice(k_ * NB, (k_ + 1) * NB)
        m["x"] = x[sl]; m["ctx"] = ctx[sl]
        m["c5T"] = np.ascontiguousarray(np.concatenate([c[sl], c_ctx[None, :]], axis=0).T)
        maps.append(m)
    return maps


_CACHE = {}


def kernel(**inputs):
    cfg = Cfg()
    n_cores = 8
    if "nc" not in _CACHE:
        _CACHE["nc"] = build(cfg)
    nc = _CACHE["nc"]
    maps = make_in_maps(cfg, inputs, n_cores)
    res = run_bass_kernel_spmd(nc, maps, core_ids=list(range(n_cores)))
    return np.concatenate([np.asarray(r["out"]) for r in res.results], axis=0).astype(np.float32)
```

```python
import numpy as np
from contextlib import ExitStack
import concourse.bass as bass
import concourse.mybir as mybir
from concourse.bass_utils import run_bass_kernel_spmd

F32 = mybir.dt.float32
BF16 = mybir.dt.bfloat16
ALU = mybir.AluOpType
AF = mybir.ActivationFunctionType
AX = mybir.AxisListType

WRITE_KW = ("out", "ap", "accum_out")
NDMASEM = 6


class V:
    __slots__ = ("t", "key", "ap")

    def __init__(self, t, key, ap):
        self.t, self.key, self.ap = t, key, ap


class T:
    def __init__(self, name, handle, is_ap=False):
        self.name = name
        self.h = handle
        self.is_ap = is_ap
        self.w = {}
        self.r = {}

    def __getitem__(self, idx):
        return V(self, None, self.h[idx])

    def k(self, key):
        return _Keyed(self, key)

    def v(self, ap, key=None):
        return V(self, key, ap)


class _Sub:
    def __init__(self, t, ap):
        self.t, self.ap = t, ap

    def __getitem__(self, idx):
        return V(self.t, None, self.ap[idx])


class _Keyed:
    def __init__(self, t, key):
        self.t, self.key = t, key

    def __getitem__(self, idx):
        return V(self.t, self.key, self.t.h[idx])


class Prog:
    ENG = ("pe", "act", "dve", "pool", "sp")

    def __init__(self, nc):
        self.nc = nc
        self.es = ExitStack()
        self.ops = {e: [] for e in self.ENG}
        self.cnt = {e: 0 for e in self.ENG}
        self.dcnt = {e: 0 for e in self.ENG}
        self.sems = {}
        self.known = {e: {} for e in self.ENG}
        for e in self.ENG:
            self.sems[e] = self.es.enter_context(nc.semaphore("s_" + e))
            for j in range(NDMASEM):
                self.sems[(e, j)] = self.es.enter_context(nc.semaphore("d_%s%d" % (e, j)))
        self.n_psum = 0

    def sb(self, name, shape, dtype=F32):
        h = self.es.enter_context(self.nc.sbuf_tensor(name, list(shape), dtype))
        return T(name, h)

    def ps(self, name, shape, dtype=F32):
        h = self.es.enter_context(self.nc.psum_tensor(name, list(shape), dtype))
        return T(name, h)

    def dram(self, name, shape, dtype=F32, kind="Internal"):
        h = self.nc.dram_tensor(name, list(shape), dtype, kind=kind).ap()
        return T(name, h, True)

    def _conflicts(self, d, key):
        if key is None:
            for kk, ev in d.items():
                yield kk, ev
        else:
            if key in d:
                yield key, d[key]
            if None in d:
                yield None, d[None]

    def barrier(self):
        evs = []
        for e in self.ENG:
            if self.cnt[e]:
                evs.append((e, self.cnt[e]))
            for j in range(NDMASEM):
                n = self.dcnt[e]
                k = (n - j + NDMASEM - 1) // NDMASEM if n > j else 0
                if k:
                    evs.append(((e, j), 16 * k))
        for e in self.ENG:
            kn = self.known[e]
            wl = []
            for s, v in evs:
                if s == e and e == "pe":
                    continue
                if kn.get(s, 0) >= v:
                    continue
                kn[s] = v
                wl.append((s, v))
            if wl:
                self.ops[e].append((wl, None, None, None, None, None))

    def _issue(self, eng, name, args, kw, is_dma):
        kw = dict(kw)
        reads, writes = list(kw.pop("rd", [])), list(kw.pop("wr", []))
        nargs = []
        for i, a in enumerate(args):
            if isinstance(a, V):
                (writes if i == 0 else reads).append(a)
                nargs.append(a.ap)
            else:
                nargs.append(a)
        nkw = {}
        for k_, a in kw.items():
            if isinstance(a, V):
                (writes if k_ in WRITE_KW else reads).append(a)
                nkw[k_] = a.ap
            else:
                nkw[k_] = a
        waits = {}

        def need(ev):
            s, val = ev
            if waits.get(s, 0) < val:
                waits[s] = val

        for v in reads:
            for _, ev in self._conflicts(v.t.w, v.key):
                need(ev)
        for v in writes:
            for _, ev in self._conflicts(v.t.w, v.key):
                need(ev)
            for _, evs in self._conflicts(v.t.r, v.key):
                for ev in evs:
                    need(ev)
        if is_dma:
            n = self.dcnt[eng]
            self.dcnt[eng] += 1
            j = n % NDMASEM
            semk = (eng, j)
            val = 16 * (n // NDMASEM + 1)
            inc = 16
            if n >= NDMASEM:
                need((semk, val - 16))
        else:
            self.cnt[eng] += 1
            semk = eng
            val = self.cnt[eng]
            inc = 1
        ev = (semk, val)
        kn = self.known[eng]
        wl = []
        for s, val_ in waits.items():
            if s == eng and eng == "pe":
                continue
            if kn.get(s, 0) >= val_:
                continue
            kn[s] = val_
            wl.append((s, val_))
        for v in writes:
            t, key = v.t, v.key
            if key is None:
                t.w = {None: ev}
                t.r = {}
            else:
                t.w[key] = ev
                t.r[key] = []
        for v in reads:
            t, key = v.t, v.key
            t.r.setdefault(key, []).append(ev)
            if len(t.r[key]) > 12:
                d = {}
                for s, vv in t.r[key]:
                    if d.get(s, 0) < vv:
                        d[s] = vv
                t.r[key] = list(d.items())
        self.ops[eng].append((wl, name, nargs, nkw, semk, inc))
        return ev

    def I(self, eng, name, *args, **kw):
        return self._issue(eng, name, args, kw, False)

    def dma(self, eng, out, in_, **kw):
        return self._issue(eng, "dma_start", (), dict(out=out, in_=in_, **kw), True)

    def pe(self, name, *a, **k):
        return self.I("pe", name, *a, **k)

    def act(self, name, *a, **k):
        return self.I("act", name, *a, **k)

    def dve(self, name, *a, **k):
        return self.I("dve", name, *a, **k)

    def pool(self, name, *a, **k):
        return self.I("pool", name, *a, **k)

    def emit(self, final_events):
        nc = self.nc
        engobj = {"pe": "tensor", "act": "scalar", "dve": "vector", "pool": "gpsimd", "sp": "sync"}
        with nc.Block() as block:
            for e in self.ENG:
                ops = self.ops[e]
                if e == "sp":
                    ops = list(ops)
                    fin = {}
                    for (s, v) in final_events:
                        fin[s] = max(fin.get(s, 0), v)
                    ops.append(([(s, v) for s, v in fin.items()], None, None, None, None, None))
                if not ops:
                    continue

                def body(eng, ops=ops):
                    for (wl, name, nargs, nkw, semk, inc) in ops:
                        for s, v in wl:
                            eng.wait_ge(self.sems[s], v)
                        if name is None:
                            continue
                        ins = getattr(eng, name)(*nargs, **nkw)
                        ins.then_inc(self.sems[semk], inc)
                getattr(block, engobj[e])(body)

    def close(self):
        self.es.close()

D = 1024
EPS = 1e-6
WDEC = 0.6065306597126334


class Cfg:
    def __init__(self, NB=4, TC=256, TL=2048, DEPTH=4, GRID_W=64, dbg=None):
        self.NB, self.TC, self.TL, self.DEPTH, self.GRID_W = NB, TC, TL, DEPTH, GRID_W
        self.T = TC + TL
        self.NT = self.T // 128
        self.NTC = TC // 128
        self.R = NB + 1
        self.dbg = dbg or {}


def make_consts(cfg):
    c = {}
    idx = np.arange(128)
    s, t = idx[:, None], idx[None, :]
    c["ident"] = np.eye(128, dtype=np.float32)
    c["bones"] = ((s // 64) == (t // 64)).astype(np.float32)
    c["ones"] = np.ones((128, 128), np.float32)
    selI = np.zeros((128, 128), np.float32)
    selR = np.zeros((128, 128), np.float32)
    for hp in range(2):
        for d in range(64):
            selI[hp * 64 + d, hp * 64 + d] = 1.0
            q = d % 32
            if q < 16:
                selR[hp * 64 + d + 16, hp * 64 + d] = -1.0
            else:
                selR[hp * 64 + d - 16, hp * 64 + d] = 1.0
    c["selI"] = selI
    c["selR"] = selR
    su = (s < t).astype(np.float32); iu = (s <= t).astype(np.float32)
    sl = (s > t).astype(np.float32); il = (s >= t).astype(np.float32)
    c["m_su"] = su; c["m_iu"] = iu
    c["m_sl"] = sl; c["m_il"] = il
    c["nm_su"] = -c["m_su"]; c["nm_sl"] = -c["m_sl"]
    c["nm_iu"] = -c["m_iu"]; c["nm_il"] = -c["m_il"]
    c["triF"] = (-WDEC) * iu; c["triB"] = (-WDEC) * il; c["allS"] = (-WDEC) * np.ones((128, 128), np.float32)
    c["neg_iu"] = ((1.0 - iu) * (-30000.0)).astype(np.float32)
    c["neg_il"] = ((1.0 - il) * (-30000.0)).astype(np.float32)
    c["hsel"] = np.zeros((128, 2), np.float32); c["hsel"][:64, 0] = 1; c["hsel"][64:, 1] = 1
    offs = {}
    cols = []
    o = 0
    for k_, v in c.items():
        offs[k_] = (o, v.shape[1]); cols.append(v.astype(np.float32)); o += v.shape[1]
    arr = np.concatenate(cols, axis=1)
    TL, TC, GW = cfg.TL, cfg.TC, cfg.GRID_W
    tl = np.arange(TL)
    row = (tl // GW).astype(np.float32); col = (tl % GW).astype(np.float32)
    inv = (10000.0 ** (-np.arange(16, dtype=np.float32) / 16)).astype(np.float32)
    ang_r = row[:, None] * inv[None, :]; ang_c = col[:, None] * inv[None, :]
    ang = np.concatenate([ang_r, ang_r, ang_c, ang_c], axis=1)
    rope = np.zeros((64, 2, cfg.T), np.float32)
    rope[:, 0, :TC] = 1.0
    rope[:, 0, TC:] = np.cos(ang).T
    rope[:, 1, TC:] = np.sin(ang).T
    return arr, offs, rope


def build(cfg):
    NB, TC, TL, DEPTH, T_, NT, NTC, R = cfg.NB, cfg.TC, cfg.TL, cfg.DEPTH, cfg.T, cfg.NT, cfg.NTC, cfg.R
    NE, NO = (DEPTH + 1) // 2, DEPTH // 2
    nc = bass.Bass("TRN2", target_bir_lowering=False)
    P = Prog(nc)
    carr, coffs, _ = make_consts(cfg)
    NCC = carr.shape[1]
    A = {}

    def inp(name, shape):
        A[name] = nc.dram_tensor(name, list(shape), F32, kind="ExternalInput").ap()
        return A[name]

    inp("x", [NB, TL, D]); inp("ctx", [NB, TC, D]); inp("c5T", [D, R])
    inp("w_mod", [DEPTH, D, 6 * D]); inp("b_mod", [DEPTH, 6 * D]); inp("norm_mix", [DEPTH, D]); inp("norm_mlp", [DEPTH, D])
    inp("w_out", [DEPTH, D, D]); inp("mlp_w1", [DEPTH, D, 4 * D]); inp("mlp_w2", [DEPTH, 4 * D, D]); inp("final_norm", [1, D])
    inp("ab_w_in", [NE, D, 2688]); inp("a_q_norm", [NE, 64]); inp("a_k_norm", [NE, 64])
    inp("b_mu_prev", [NE, 1920]); inp("b_mu_next", [NE, 1920]); inp("b_w0", [NE, 1024]); inp("b_w2", [NE, 128, 512])
    inp("b_a0", [NE, 1024]); inp("b_a2", [NE, 128, 512]); inp("b_g2", [NE, 128, 512]); inp("b_k_k", [NE, 512]); inp("b_k_a", [NE, 512])
    inp("b_r_k", [NE, 512]); inp("b_ln_w", [NE, 512]); inp("b_ln_b", [NE, 512])
    inp("b_mu_prev_col", [NE, 128, 15]); inp("b_mu_next_col", [NE, 128, 15]); inp("b_w0_col", [NE, 128, 8]); inp("b_a0_col", [NE, 128, 8])
    inp("b_k_k_col", [NE, 128, 4]); inp("b_k_a_col", [NE, 128, 4]); inp("b_r_k_col", [NE, 128, 4])
    if NO:
        inp("cd_w_in", [NO, D, 2320]); inp("c_sink", [NO, 8]); inp("d_conv_w", [NO, 5, 1024]); inp("d_conv_b", [NO, 1024])
        inp("d_dt_bias", [NO, 16]); inp("d_A_log", [NO, 16]); inp("d_D", [NO, 8]); inp("d_norm_w", [NO, 512])
        inp("d_conv_wT", [NO, 128, 8, 5]); inp("d_conv_b_col", [NO, 128, 8])
    inp("consts", [128, NCC]); inp("rope", [64, 2, T_])
    if "O_in" in cfg.dbg:
        inp("O_in", [DEPTH, NB, T_, D])
    out_ap = nc.dram_tensor("out", [NB, TL, D], F32, kind="ExternalOutput").ap()
    dbg_out = {}
    for nm, shp in cfg.dbg.get("outs", {}).items():
        dbg_out[nm] = nc.dram_tensor(nm, list(shp), F32, kind="ExternalOutput").ap()

    def scratch(name, shape):
        return nc.dram_tensor(name, list(shape), F32, kind="Internal").ap()

    MOD = scratch("MOD", [DEPTH, R, 6 * D])
    X = scratch("X", [T_, D])
    PT = scratch("PT", [2688, T_])
    O = scratch("O", [T_, D])

    CT = P.sb("consts_sb", [128, NCC])
    PS = [P.ps("ps%d" % j, [128, 512]) for j in range(8)]
    st = {"ps": 0, "q": 0, "ev": 0}

    def ps():
        st["ps"] = (st["ps"] + 1) % 8
        return PS[st["ps"]]

    def q():
        st["q"] ^= 1
        return "sp" if st["q"] else "pool"

    def C(name, rows=slice(0, 128), lo=0, n=None):
        o, w = coffs[name]
        n = w - lo if n is None else n
        return CT[rows, o + lo:o + lo + n]

    def evac(dst, src):
        st["ev"] ^= 1
        if st["ev"]:
            P.act("copy", out=dst, in_=src)
        else:
            P.dve("tensor_copy", out=dst, in_=src)

    def sbt(es, name, shape, dt=F32):
        st["uid"] = st.get("uid", 0) + 1
        name = "%s_%d" % (name, st["uid"])
        h = es.enter_context(nc.sbuf_tensor(name, list(shape), dt))
        return T(name, h)

    def r3(v, a):
        return V(v.t, v.key, v.ap.rearrange("p (a b) -> p a b", a=a))

    def bc(t_, ap, shape, axis):
        return V(t_, None, ap.unsqueeze(axis).to_broadcast(list(shape)))

    def bcast_row(ap2d, n=128):
        return ap2d.partition_broadcast(n)

    def colvec(ap1d_row):
        return ap1d_row.rearrange("o (c p) -> p (o c)", p=128)

    P.dma("sp", CT[:], A["consts"][:, :])
    ident = C("ident")

    def transpose(dst, src, npart_in, nfree_in, pp=None, off=0):
        pp = pp or ps()
        P.pe("transpose", out=pp[0:nfree_in, off:off + npart_in], in_=src, identity=C("ident", slice(0, npart_in), 0, npart_in))
        if dst is not None:
            evac(dst, pp[0:nfree_in, off:off + npart_in])
        return pp

    def rms(es_tiles, xt, Gt, St, outv):
        sq, ss, rs = es_tiles
        P.pool("memset", ap=ss[:], constant=0.0)
        P.act("activation", out=sq[:], in_=xt, func=AF.Square, accum_out=ss[:])
        P.dve("tensor_scalar", out=rs[:], in0=ss[:], scalar1=1.0 / D, scalar2=EPS, op0=ALU.mult, op1=ALU.add)
        P.act("activation", out=rs[:], in_=rs[:], func=AF.Sqrt)
        P.dve("reciprocal", out=rs[:], in_=rs[:])
        P.dve("scalar_tensor_tensor", out=outv, in0=xt, scalar=rs[:, 0:1], in1=Gt, op0=ALU.mult, op1=ALU.mult)
        if St is not None:
            P.dve("tensor_tensor", out=outv, in0=outv, in1=St, op=ALU.add)

    def phase_mod():
        with ExitStack() as es:
            c5 = sbt(es, "c5", [128, 8, R]); sc5 = sbt(es, "sc5", [128, 8, R])
            wb = [sbt(es, "wmb%d" % j, [128, 8, 512]) for j in range(2)]
            bt = [sbt(es, "bmb%d" % j, [R, 512]) for j in range(2)]
            res = [sbt(es, "mres%d" % j, [R, 512]) for j in range(2)]
            P.dma("sp", c5[:], A["c5T"].rearrange("(kc p) r -> p kc r", p=128))
            P.act("activation", out=sc5[:], in_=c5[:], func=AF.Silu)
            k = 0
            for i in range(DEPTH):
                for n in range(12):
                    w_, b_, r_ = wb[k % 2], bt[k % 2], res[k % 2]
                    P.dma(q(), w_[:], A["w_mod"][i][:, n * 512:(n + 1) * 512].rearrange("(kc p) n -> p kc n", p=128))
                    P.dma(q(), b_[:], A["b_mod"][i:i + 1, n * 512:(n + 1) * 512].partition_broadcast(R))
                    pp = ps()
                    for kc in range(8):
                        P.pe("matmul", out=pp[0:R, :], lhsT=sc5[:, kc, :], rhs=w_[:, kc, :], start=(kc == 0), stop=(kc == 7))
                    P.dve("tensor_tensor", out=r_[:], in0=pp[0:R, :], in1=b_[:], op=ALU.add)
                    P.dma(q(), MOD[i, :, n * 512:(n + 1) * 512], r_[:])
                    k += 1
            P.barrier()

    def phase_inproj(i, b, w_in, NF, need_ctx):
        with ExitStack() as es:
            hT = sbt(es, "hT", [128, 8, T_], BF16)
            Wbb = [sbt(es, "Wbb%d" % j, [128, 8, 512], BF16) for j in range(2)]
            xt = [sbt(es, "xt%d" % j, [128, D]) for j in range(2)]
            h = [sbt(es, "h%d" % j, [128, D]) for j in range(2)]
            sq = sbt(es, "sq", [128, D]); ss = sbt(es, "ss", [128, 1]); rs = sbt(es, "rs", [128, 1])
            G = {s_: sbt(es, "G" + s_, [128, D]) for s_ in "cl"}
            S = {s_: sbt(es, "S" + s_, [128, D]) for s_ in "cl"}
            nw = sbt(es, "nw", [128, D])
            Wb = [sbt(es, "Wb%d" % j, [128, 8, 512]) for j in range(2)]
            stg = [sbt(es, "stg%d" % j, [128, 512]) for j in range(2)]
            P.dma(q(), nw[:], A["norm_mix"][i:i + 1, :].partition_broadcast(128))
            for s_, r_ in (("c", NB), ("l", b)):
                P.dma(q(), sq[:], MOD[i, r_:r_ + 1, D:2 * D].partition_broadcast(128))
                P.dma(q(), S[s_][:], MOD[i, r_:r_ + 1, 0:D].partition_broadcast(128))
                P.dve("scalar_tensor_tensor", out=G[s_][:], in0=sq[:], scalar=1.0, in1=nw[:], op0=ALU.add, op1=ALU.mult)
            for t in range(NT):
                s_ = "c" if t < NTC else "l"
                x_ = xt[t % 2]; h_ = h[t % 2]
                P.dma(q(), x_[:], X[t * 128:(t + 1) * 128, :])
                rms((sq, ss, rs), x_[:], G[s_][:], S[s_][:], h_[:])
                for half in range(2):
                    pp = ps()
                    for j in range(4):
                        kc = half * 4 + j
                        P.pe("transpose", out=pp[:, j * 128:(j + 1) * 128], in_=h_[:, kc * 128:(kc + 1) * 128], identity=ident)
                    evac(hT[:, half * 4:(half + 1) * 4, t * 128:(t + 1) * 128], r3(pp[:, :], 4))
            k = 0
            for n0 in range(0, NF, 512):
                nn = min(512, NF - n0)
                W_ = Wb[(n0 // 512) % 2]
                P.dma(q(), W_[:, :, 0:nn], w_in[:, n0:n0 + nn].rearrange("(kc p) n -> p kc n", p=128))
                Wf_ = W_
                W_ = Wbb[(n0 // 512) % 2]
                P.pool("tensor_copy", out=W_[:, :, 0:nn], in_=Wf_[:, :, 0:nn])
                for f0 in range(0, nn, 128):
                    m = min(128, nn - f0)
                    for t0 in range(0, T_, 512):
                        n = min(512, T_ - t0)
                        pp = ps()
                        for kc in range(8):
                            P.pe("matmul", out=pp[0:m, 0:n], lhsT=W_[:, kc, f0:f0 + m], rhs=hT[:, kc, t0:t0 + n], start=(kc == 0), stop=(kc == 7))
                        s2 = stg[k % 2]; k += 1
                        evac(s2[0:m, 0:n], pp[0:m, 0:n])
                        P.dma(q(), PT[n0 + f0:n0 + f0 + m, t0:t0 + n], s2[0:m, 0:n])
            P.barrier()

    def phase_mlp(i, b, need_ctx):
        with ExitStack() as es:
            hid = sbt(es, "hid", [128, 32, 512], BF16)
            Wout = sbt(es, "Woutb", [128, 8, D], BF16)
            xb = sbt(es, "xb", [128, 4, D]); ot = sbt(es, "ot", [128, D]); oT = sbt(es, "oT", [128, 8, 128], BF16)
            h2 = sbt(es, "h2", [128, D]); sq = sbt(es, "sq2", [128, D]); ss = sbt(es, "ss2", [128, 1]); rs = sbt(es, "rs2", [128, 1])
            h2T = sbt(es, "h2T", [128, 8, 512], BF16)
            W1b = [sbt(es, "W1b%d" % j, [128, 8, 512]) for j in range(2)]
            W1bb = [sbt(es, "W1bb%d" % j, [128, 8, 512], BF16) for j in range(2)]
            W2r = [sbt(es, "W2r%d" % j, [128, D]) for j in range(2)]
            W2rb = [sbt(es, "W2rb%d" % j, [128, D], BF16) for j in range(2)]
            g1 = sbt(es, "g1", [128, D]); G2 = sbt(es, "G2", [128, D]); S2 = sbt(es, "S2", [128, D]); g2 = sbt(es, "g2", [128, D])
            blocks = []
            if need_ctx:
                for t0 in range(0, NTC, 4):
                    blocks.append(("c", list(range(t0, min(NTC, t0 + 4)))))
            for t0 in range(NTC, NT, 4):
                blocks.append(("l", list(range(t0, min(NT, t0 + 4)))))
            cur = None
            for s_, tiles in blocks:
                if s_ != cur:
                    cur = s_
                    r_ = NB if s_ == "c" else b
                    P.dma(q(), g1[:], MOD[i, r_:r_ + 1, 2 * D:3 * D].partition_broadcast(128))
                    P.dma(q(), S2[:], MOD[i, r_:r_ + 1, 3 * D:4 * D].partition_broadcast(128))
                    P.dma(q(), sq[:], MOD[i, r_:r_ + 1, 4 * D:5 * D].partition_broadcast(128))
                    P.dma(q(), g2[:], MOD[i, r_:r_ + 1, 5 * D:6 * D].partition_broadcast(128))
                    P.dma(q(), h2[:], A["norm_mlp"][i:i + 1, :].partition_broadcast(128))
                    P.dve("scalar_tensor_tensor", out=G2[:], in0=sq[:], scalar=1.0, in1=h2[:], op0=ALU.add, op1=ALU.mult)
                ntl = len(tiles)
                for nh in range(2):
                    P.dma(q(), W1b[nh][:], A["w_out"][i][:, nh * 512:(nh + 1) * 512].rearrange("(kc p) n -> p kc n", p=128))
                    P.pool("tensor_copy", out=Wout[:, :, nh * 512:(nh + 1) * 512], in_=W1b[nh][:])
                for j, t in enumerate(tiles):
                    P.dma(q(), xb[:, j, :], X[t * 128:(t + 1) * 128, :])
                    P.dma(q(), ot[:], O[t * 128:(t + 1) * 128, :])
                    for half in range(2):
                        pp = ps()
                        for jj in range(4):
                            kc = half * 4 + jj
                            P.pe("transpose", out=pp[:, jj * 128:(jj + 1) * 128], in_=ot[:, kc * 128:(kc + 1) * 128], identity=ident)
                        evac(oT[:, half * 4:(half + 1) * 4, :], r3(pp[:, :], 4))
                    for nh in range(2):
                        pp = ps()
                        for kc in range(8):
                            P.pe("matmul", out=pp[:, :], lhsT=oT[:, kc, :], rhs=Wout[:, kc, nh * 512:(nh + 1) * 512], start=(kc == 0), stop=(kc == 7))
                        P.dve("tensor_tensor", out=sq[:, 0:512], in0=pp[:, :], in1=g1[:, nh * 512:(nh + 1) * 512], op=ALU.mult)
                        P.dve("tensor_tensor", out=xb[:, j, nh * 512:(nh + 1) * 512], in0=xb[:, j, nh * 512:(nh + 1) * 512], in1=sq[:, 0:512], op=ALU.add)
                    rms((sq, ss, rs), xb[:, j, :], G2[:], S2[:], h2[:])
                    for half in range(2):
                        pp = ps()
                        for jj in range(4):
                            kc = half * 4 + jj
                            P.pe("transpose", out=pp[:, jj * 128:(jj + 1) * 128], in_=h2[:, kc * 128:(kc + 1) * 128], identity=ident)
                        evac(h2T[:, half * 4:(half + 1) * 4, j * 128:(j + 1) * 128], r3(pp[:, :], 4))
                ntok = ntl * 128
                for n8 in range(8):
                    Wf_ = W1b[n8 % 2]
                    P.dma(q(), Wf_[:], A["mlp_w1"][i][:, n8 * 512:(n8 + 1) * 512].rearrange("(kc p) n -> p kc n", p=128))
                    W_ = W1bb[n8 % 2]
                    P.pool("tensor_copy", out=W_[:], in_=Wf_[:])
                    for f4 in range(4):
                        fc = n8 * 4 + f4
                        pp = ps()
                        for kc in range(8):
                            P.pe("matmul", out=pp[:, 0:ntok], lhsT=W_[:, kc, f4 * 128:(f4 + 1) * 128], rhs=h2T[:, kc, 0:ntok], start=(kc == 0), stop=(kc == 7))
                        P.act("activation", out=sq[:, 0:ntok], in_=pp[:, 0:ntok], func=AF.Relu)
                        P.dve("tensor_tensor", out=hid[:, fc, 0:ntok], in0=sq[:, 0:ntok], in1=sq[:, 0:ntok], op=ALU.mult)
                for fc in range(32):
                    Wf_ = W2r[fc % 2]
                    P.dma(q(), Wf_[:], A["mlp_w2"][i][fc * 128:(fc + 1) * 128, :])
                    W_ = W2rb[fc % 2]
                    P.pool("tensor_copy", out=W_[:], in_=Wf_[:])
                    for j in range(ntl):
                        for nh in range(2):
                            P.pe("matmul", out=PS[j * 2 + nh][:, :], lhsT=hid[:, fc, j * 128:(j + 1) * 128], rhs=W_[:, nh * 512:(nh + 1) * 512], start=(fc == 0), stop=(fc == 31))
                for j, t in enumerate(tiles):
                    for nh in range(2):
                        P.dve("tensor_tensor", out=sq[:, nh * 512:(nh + 1) * 512], in0=PS[j * 2 + nh][:, :], in1=g2[:, nh * 512:(nh + 1) * 512], op=ALU.mult)
                    P.pool("tensor_tensor", out=xb[:, j, :], in0=xb[:, j, :], in1=sq[:], op=ALU.add)
                    P.dma(q(), X[t * 128:(t + 1) * 128, :], xb[:, j, :])
            P.barrier()

    def phase_final(b):
        with ExitStack() as es:
            xt = [sbt(es, "fx%d" % j, [128, D]) for j in range(2)]
            yo = [sbt(es, "fy%d" % j, [128, D]) for j in range(2)]
            sq = sbt(es, "fsq", [128, D]); ss = sbt(es, "fss", [128, 1]); rs = sbt(es, "frs", [128, 1])
            Gf = sbt(es, "Gf", [128, D])
            P.dma(q(), Gf[:], A["final_norm"][0:1, :].partition_broadcast(128))
            evs = []
            for t in range(NTC, NT):
                x_, y_ = xt[t % 2], yo[t % 2]
                P.dma(q(), x_[:], X[t * 128:(t + 1) * 128, :])
                rms((sq, ss, rs), x_[:], Gf[:], None, y_[:])
                evs.append(P.dma(q(), out_ap[b, (t - NTC) * 128:(t - NTC + 1) * 128, :], y_[:]))
            P.barrier()
            return evs

    def attn(i, j, b, need_ctx, kind):
        with ExitStack() as es:
            QT = sbt(es, "QT", [64, 8, T_], BF16); KT = sbt(es, "KT", [64, 2, T_], BF16)
            Vtm = sbt(es, "Vtm", [128, NT, 2, 65], BF16)
            Cc = sbt(es, "Cc", [64, T_]); Ss = sbt(es, "Ss", [64, T_])
            ch = [sbt(es, "ch%d" % k_, [128, 512]) for k_ in range(2)]
            sq = sbt(es, "asq", [128, 512]); rstd = sbt(es, "arstd", [128, 512]); qn = sbt(es, "aqn", [128, 512])
            t1 = sbt(es, "at1", [64, 512]); t2 = sbt(es, "at2", [64, 512])
            ptb = [sbt(es, "ptb%d" % k_, [128, 512], BF16) for k_ in range(3)]
            osb = [sbt(es, "osb%d" % k_, [128, 512]) for k_ in range(2)]
            dn = sbt(es, "dn", [128, 4]); nwq = sbt(es, "nwq", [128, 1]); nwk = sbt(es, "nwk", [128, 1])
            esink = sbt(es, "esink", [128, 8]); vch = sbt(es, "vch", [128, T_])
            P.dma(q(), Cc[:], A["rope"][:, 0, :]); P.dma(q(), Ss[:], A["rope"][:, 1, :])
            P.pool("memset", ap=Vtm[:], constant=1.0)
            if kind == "a":
                for hp in range(2):
                    P.dma(q(), nwq[hp * 64:(hp + 1) * 64, :], A["a_q_norm"][j:j + 1, :].rearrange("o d -> d o"))
                    P.dma(q(), nwk[hp * 64:(hp + 1) * 64, :], A["a_k_norm"][j:j + 1, :].rearrange("o d -> d o"))
            else:
                P.dma(q(), esink[:], A["c_sink"][j:j + 1, :].partition_broadcast(128))
                P.act("activation", out=esink[:], in_=esink[:], func=AF.Exp)
            k_ = 0
            for c in range(5):
                for t0 in range(0, T_, 512):
                    n = min(512, T_ - t0)
                    ch_ = ch[k_ % 2]; k_ += 1
                    P.dma(q(), ch_[:, 0:n], PT[c * 128:(c + 1) * 128, t0:t0 + n])
                    if kind == "a":
                        P.act("activation", out=sq[:, 0:n], in_=ch_[:, 0:n], func=AF.Square)
                        pp = ps()
                        P.pe("matmul", out=pp[:, 0:n], lhsT=C("bones"), rhs=sq[:, 0:n], start=True, stop=True)
                        P.dve("tensor_scalar", out=rstd[:, 0:n], in0=pp[:, 0:n], scalar1=1.0 / 64, scalar2=EPS, op0=ALU.mult, op1=ALU.add)
                        P.act("activation", out=rstd[:, 0:n], in_=rstd[:, 0:n], func=AF.Sqrt)
                        P.dve("reciprocal", out=rstd[:, 0:n], in_=rstd[:, 0:n])
                        nw_ = nwq if c < 4 else nwk
                        P.dve("scalar_tensor_tensor", out=qn[:, 0:n], in0=ch_[:, 0:n], scalar=nw_[:, 0:1], in1=rstd[:, 0:n], op0=ALU.mult, op1=ALU.mult)
                        src = qn
                    else:
                        src = ch_
                    for hp in range(2):
                        p1 = ps(); p2 = ps()
                        P.pe("matmul", out=p1[0:64, 0:n], lhsT=C("selI", lo=hp * 64, n=64), rhs=src[:, 0:n], start=True, stop=True)
                        P.pe("matmul", out=p2[0:64, 0:n], lhsT=C("selR", lo=hp * 64, n=64), rhs=src[:, 0:n], start=True, stop=True)
                        P.dve("tensor_tensor", out=t1[:, 0:n], in0=p1[0:64, 0:n], in1=Cc[:, t0:t0 + n], op=ALU.mult)
                        P.dve("tensor_tensor", out=t2[:, 0:n], in0=p2[0:64, 0:n], in1=Ss[:, t0:t0 + n], op=ALU.mult)
                        dst = QT[:, 2 * c + hp, t0:t0 + n] if c < 4 else KT[:, hp, t0:t0 + n]
                        P.pool("tensor_tensor", out=dst, in0=t1[:, 0:n], in1=t2[:, 0:n], op=ALU.add)
            P.dma(q(), vch[:], PT[640:768, :])
            for t in range(NT):
                pp = ps()
                P.pe("transpose", out=pp[:, 0:128], in_=vch[:, t * 128:(t + 1) * 128], identity=ident)
                evac(Vtm[:, t, :, 0:64], r3(pp[:, 0:128], 2))
            qblocks = list(range(NTC, NT)) + (list(range(NTC)) if need_ctx else [])
            kk_ = 0
            for qt in qblocks:
                if qt < NTC:
                    keys = [(kt, None) for kt in range(NTC)]
                elif kind == "a":
                    keys = [(kt, None) for kt in range(NT)]
                else:
                    keys = [(kt, None) for kt in range(NTC)]
                    if qt - 1 >= NTC:
                        keys.append((qt - 1, "m_il"))
                    keys.append((qt, None))
                    if qt + 1 < NT:
                        keys.append((qt + 1, "m_iu"))
                o_ = osb[kk_ % 2]; kk_ += 1
                for g in range(2):
                    acc = PS[g]
                    nk = len(keys)

                    def smm(idx):
                        kt, m = keys[idx]
                        st["aps"] = st.get("aps", 0) + 1
                        sp_ = PS[2 + st["aps"] % 6]
                        P.pe("matmul", out=r3(sp_[:, :], 4), lhsT=KT[:, g, kt * 128:(kt + 1) * 128], rhs=QT[:, 4 * g:4 * g + 4, qt * 128:(qt + 1) * 128], start=True, stop=True)
                        return sp_

                    pend = smm(0)
                    for idx, (kt, m) in enumerate(keys):
                        sp_ = pend
                        if idx + 1 < nk:
                            pend = smm(idx + 1)
                        pt_ = ptb[idx % 3]
                        P.act("activation", out=pt_[:], in_=sp_[:, :], func=AF.Exp, scale=0.125)
                        if m:
                            P.dve("tensor_tensor", out=r3(pt_[:], 4), in0=r3(pt_[:], 4), in1=bc(CT, C(m).ap, [128, 4, 128], 1), op=ALU.mult)
                        for r in range(4):
                            P.pe("matmul", out=acc[:, r * 65:(r + 1) * 65], lhsT=pt_[:, r * 128:(r + 1) * 128], rhs=Vtm[:, kt, g, :], start=(idx == 0 and r == 0), stop=(idx == nk - 1 and r == 3))
                    accv = V(acc, None, acc.h[:, 0:260].rearrange("p (a b) -> p a b", a=4))
                    den = V(acc, None, accv.ap[:, :, 64])
                    if kind == "c":
                        P.dve("tensor_tensor", out=dn[:], in0=den, in1=esink[:, 4 * g:4 * g + 4], op=ALU.add)
                    else:
                        P.dve("tensor_copy", out=dn[:], in_=den)
                    P.dve("reciprocal", out=dn[:], in_=dn[:])
                    P.dve("tensor_tensor", out=r3(o_[:, g * 256:(g + 1) * 256], 4), in0=V(acc, None, accv.ap[:, :, 0:64]),
                          in1=V(dn, None, dn.h[:, :].unsqueeze(2).to_broadcast([128, 4, 64])), op=ALU.mult)
                P.dma(q(), O[qt * 128:(qt + 1) * 128, 0:512], o_[:])
            P.barrier()


    def ssd(i, j, b, need_ctx):
        TP = T_ + 8
        with ExitStack() as es:
            Xtm = sbt(es, "Xtm", [128, NT, 512]); BT = sbt(es, "BT", [128, 2, T_]); Btm = sbt(es, "Btm", [128, NT, 256])
            CTt = sbt(es, "CTt", [128, 2, T_]); Yacc = sbt(es, "Yacc", [128, NT, 512])
            buf = [sbt(es, "cbuf%d" % k_, [128, TP]) for k_ in range(1)]
            up = sbt(es, "up", [128, TP]); uo = sbt(es, "uo", [128, T_])
            cw = sbt(es, "cw", [128, 8, 5]); cb = sbt(es, "cb", [128, 8]); Abc = sbt(es, "Abc", [128, 16]); dtb = sbt(es, "dtb", [16, 1])
            Dsk = sbt(es, "Dsk", [128, 8]); nwd = sbt(es, "nwd", [128, 512])
            dt_tm = sbt(es, "dt_tm", [128, NT, 16]); a_tm = sbt(es, "a_tm", [128, NT, 16])
            ST = sbt(es, "ST", [128, 2, 256]); aTri = sbt(es, "aTri", [128, 8, 128]); acs = sbt(es, "acs", [128, 16])
            tmp = sbt(es, "stmp", [128, 8, 128]); CBs = sbt(es, "CBs", [128, 2, 128]); eacs = sbt(es, "eacs", [128, 8])
            dte = sbt(es, "dte", [128, 8]); cdec = sbt(es, "cdec", [128, 8]); xw = sbt(es, "xw", [128, 512]); tY = sbt(es, "tY", [128, 512])
            zt = sbt(es, "zt", [128, 4, 128]); u = sbt(es, "su", [128, 512]); ss2 = sbt(es, "ss2", [128, 2]); osb = [sbt(es, "sosb%d" % k_, [128, 512]) for k_ in range(2)]
            sqd = sbt(es, "sqd", [128, 256])
            P.dma(q(), cw[:], A["d_conv_wT"][j]); P.dma(q(), cb[:], A["d_conv_b_col"][j])
            P.dma(q(), Abc[:], A["d_A_log"][j:j + 1, :].partition_broadcast(128))
            P.act("activation", out=Abc[:], in_=Abc[:], func=AF.Exp)
            P.dve("tensor_scalar", out=Abc[:], in0=Abc[:], scalar1=-1.0, scalar2=None, op0=ALU.mult)
            P.dma(q(), dtb[:], A["d_dt_bias"][j:j + 1, :].rearrange("o d -> d o"))
            P.dma(q(), Dsk[:], A["d_D"][j:j + 1, :].partition_broadcast(128))
            P.dma(q(), nwd[:], A["d_norm_w"][j:j + 1, :].partition_broadcast(128))
            P.pool("memset", ap=buf[0][:], constant=0.0)
            P.pool("memset", ap=Yacc[:], constant=0.0)
            for c in range(8):
                bf = buf[0]
                r0 = 1280 + c * 128
                P.dma(q(), bf[:, 2:2 + TC], PT[r0:r0 + 128, 0:TC])
                P.dma(q(), bf[:, TC + 6:TC + 6 + TL], PT[r0:r0 + 128, TC:T_])
                W_ = T_ + 4
                P.dve("tensor_scalar", out=up[:, 2:2 + W_], in0=bf[:, 0:W_], scalar1=cw[:, c, 0:1], scalar2=None, op0=ALU.mult)
                for k_ in range(1, 5):
                    P.dve("scalar_tensor_tensor", out=up[:, 2:2 + W_], in0=bf[:, k_:k_ + W_], scalar=cw[:, c, k_:k_ + 1], in1=up[:, 2:2 + W_], op0=ALU.mult, op1=ALU.add)
                if c < 4:
                    dst = uo
                elif c < 6:
                    dst = V(BT, None, BT.h[:, c - 4, :])
                else:
                    dst = V(CTt, None, CTt.h[:, c - 6, :])
                dv = (lambda lo, hi: dst[:, lo:hi]) if c < 4 else (lambda lo, hi: V(dst.t, None, dst.ap[:, lo:hi]))
                P.act("activation", out=dv(0, TC), in_=up[:, 2:2 + TC], func=AF.Silu, bias=cb[:, c:c + 1])
                P.act("activation", out=dv(TC, T_), in_=up[:, TC + 6:TC + 6 + TL], func=AF.Silu, bias=cb[:, c:c + 1])
                if c < 6:
                    for t in range(NT):
                        pp = ps()
                        src = uo[:, t * 128:(t + 1) * 128] if c < 4 else BT[:, c - 4, t * 128:(t + 1) * 128]
                        P.pe("transpose", out=pp[:, 0:128], in_=src, identity=ident)
                        dd = Xtm[:, t, c * 128:(c + 1) * 128] if c < 4 else Btm[:, t, (c - 4) * 128:(c - 3) * 128]
                        evac(dd, pp[:, 0:128])
            dtT = _Sub(up, up.h[0:16, 0:T_])
            P.dma(q(), dtT[:, :], PT[2304:2320, :])
            P.act("activation", out=dtT[:, :], in_=dtT[:, :], func=AF.Exp, bias=dtb[:, 0:1])
            P.act("activation", out=dtT[:, :], in_=dtT[:, :], func=AF.Ln, bias=1.0)
            for t in range(NT):
                pp = ps()
                P.pe("transpose", out=pp[:, 0:16], in_=dtT[:, t * 128:(t + 1) * 128], identity=C("ident", slice(0, 16), 0, 16))
                evac(dt_tm[:, t, :], pp[:, 0:16])
                P.dve("tensor_tensor", out=a_tm[:, t, :], in0=dt_tm[:, t, :], in1=Abc[:], op=ALU.mult)
            for d in range(2):
                order = list(range(NT)) if d == 0 else (list(range(NTC - 1, -1, -1)) + list(range(NT - 1, NTC - 1, -1)))
                tri = C("m_iu") if d == 0 else C("m_il")
                neg = "neg_iu" if d == 0 else "neg_il"
                P.pool("memset", ap=ST[:], constant=0.0)
                for t in order:
                    a_ = a_tm[:, t, d * 8:(d + 1) * 8]; dt_ = dt_tm[:, t, d * 8:(d + 1) * 8]
                    pA = ps()
                    P.pe("matmul", out=pA[:, 0:8], lhsT=tri, rhs=a_, start=True, stop=False)
                    P.pe("matmul", out=pA[:, 8:16], lhsT=C("ones"), rhs=a_, start=False, stop=True)
                    P.dve("tensor_tensor", out=aTri[:], in0=bc(a_tm, a_tm.h[:, t, d * 8:(d + 1) * 8], [128, 8, 128], 2),
                          in1=bc(CT, tri.ap, [128, 8, 128], 1), op=ALU.mult)
                    evac(acs[:], pA[:, 0:16])
                    pC = ps()
                    for g in range(2):
                        P.pe("matmul", out=pC[:, g * 128:(g + 1) * 128], lhsT=BT[:, g, t * 128:(t + 1) * 128], rhs=CTt[:, g, t * 128:(t + 1) * 128], start=(g == 0), stop=(g == 1))
                    evac(CBs[:], r3(pC[:, 0:256], 2))
                    for g in range(2):
                        pR = ps()
                        P.pe("matmul", out=pR[:, :], lhsT=C("ones"), rhs=V(aTri, None, aTri.h[:, 4 * g:4 * g + 4, :].rearrange("p a b -> p (a b)")), start=True, stop=True)
                        tg = V(tmp, None, tmp.h[:, 4 * g:4 * g + 4, :])
                        P.dve("tensor_tensor", out=tg, in0=r3(pR[:, :], 4), in1=bc(CT, C(neg).ap, [128, 4, 128], 1), op=ALU.add)
                        P.dve("tensor_tensor", out=tg, in0=tg, in1=bc(acs, acs.h[:, 4 * g:4 * g + 4], [128, 4, 128], 2), op=ALU.subtract)
                        P.act("activation", out=tg, in_=tg, func=AF.Exp)
                        P.dve("tensor_tensor", out=tg, in0=tg, in1=bc(CBs, CBs.h[:, g, :], [128, 4, 128], 1), op=ALU.mult)
                        P.dve("tensor_tensor", out=tg, in0=tg, in1=bc(dt_tm, dt_tm.h[:, t, d * 8 + 4 * g:d * 8 + 4 * g + 4], [128, 4, 128], 2), op=ALU.mult)
                    pY = ps()
                    for hh in range(8):
                        P.pe("matmul", out=pY[:, hh * 64:(hh + 1) * 64], lhsT=tmp[:, hh, :], rhs=Xtm[:, t, hh * 64:(hh + 1) * 64], start=(hh == 0), stop=(hh == 7))
                    pO = ps()
                    for g in range(2):
                        P.pe("matmul", out=pO[:, g * 256:(g + 1) * 256], lhsT=CTt[:, g, t * 128:(t + 1) * 128], rhs=ST[:, g, :], start=(g == 0), stop=(g == 1))
                    P.act("activation", out=eacs[:], in_=acs[:, 0:8], func=AF.Exp)
                    P.dve("tensor_tensor", out=r3(tY[:], 8), in0=r3(pO[:, :], 8), in1=bc(eacs, eacs.h[:, :], [128, 8, 64], 2), op=ALU.mult)
                    P.dve("tensor_tensor", out=tY[:], in0=tY[:], in1=pY[:, :], op=ALU.add)
                    P.pool("tensor_tensor", out=Yacc[:, t, :], in0=Yacc[:, t, :], in1=tY[:], op=ALU.add)
                    P.dve("tensor_tensor", out=dte[:], in0=acs[:, 8:16], in1=acs[:, 0:8], op=ALU.subtract)
                    P.act("activation", out=dte[:], in_=dte[:], func=AF.Exp)
                    P.dve("tensor_tensor", out=dte[:], in0=dte[:], in1=dt_, op=ALU.mult)
                    P.dve("tensor_tensor", out=r3(xw[:], 8), in0=r3(Xtm[:, t, :], 8), in1=bc(dte, dte.h[:, :], [128, 8, 64], 2), op=ALU.mult)
                    pS = ps()
                    for g in range(2):
                        P.pe("matmul", out=pS[:, g * 256:(g + 1) * 256], lhsT=Btm[:, t, g * 128:(g + 1) * 128], rhs=xw[:, g * 256:(g + 1) * 256], start=(g == 0), stop=(g == 1))
                    P.act("activation", out=cdec[:], in_=acs[:, 8:16], func=AF.Exp)
                    st3 = V(ST, None, ST.h[:, :, :].rearrange("p g (a b) -> p (g a) b", a=4))
                    P.dve("tensor_tensor", out=st3, in0=st3, in1=bc(cdec, cdec.h[:, :], [128, 8, 64], 2), op=ALU.mult)
                    P.dve("tensor_tensor", out=V(ST, None, ST.h[:, :, :].rearrange("p g b -> p (g b)")), in0=V(ST, None, ST.h[:, :, :].rearrange("p g b -> p (g b)")), in1=pS[:, :], op=ALU.add)
            tiles = list(range(NTC, NT)) + (list(range(NTC)) if need_ctx else [])
            for kk_, t in enumerate(tiles):
                o_ = osb[kk_ % 2]
                P.dma(q(), zt[:], PT[768:1280, t * 128:(t + 1) * 128].rearrange("(c p) t -> p c t", p=128))
                P.act("activation", out=zt[:], in_=zt[:], func=AF.Silu)
                pZ = ps()
                for c in range(4):
                    P.pe("transpose", out=pZ[:, c * 128:(c + 1) * 128], in_=zt[:, c, :], identity=ident)
                P.dve("tensor_tensor", out=r3(u[:], 8), in0=r3(Xtm[:, t, :], 8), in1=bc(Dsk, Dsk.h[:, :], [128, 8, 64], 2), op=ALU.mult)
                P.dve("tensor_tensor", out=u[:], in0=u[:], in1=Yacc[:, t, :], op=ALU.add)
                P.dve("tensor_tensor", out=u[:], in0=u[:], in1=pZ[:, :], op=ALU.mult)
                P.pool("memset", ap=ss2[:], constant=0.0)
                for g in range(2):
                    P.act("activation", out=sqd[:], in_=u[:, g * 256:(g + 1) * 256], func=AF.Square, accum_out=ss2[:, g:g + 1])
                P.dve("tensor_scalar", out=ss2[:], in0=ss2[:], scalar1=1.0 / 256, scalar2=EPS, op0=ALU.mult, op1=ALU.add)
                P.act("activation", out=ss2[:], in_=ss2[:], func=AF.Sqrt)
                P.dve("reciprocal", out=ss2[:], in_=ss2[:])
                for g in range(2):
                    P.dve("scalar_tensor_tensor", out=o_[:, g * 256:(g + 1) * 256], in0=u[:, g * 256:(g + 1) * 256], scalar=ss2[:, g:g + 1], in1=nwd[:, g * 256:(g + 1) * 256], op0=ALU.mult, op1=ALU.mult)
                P.dma(q(), O[t * 128:(t + 1) * 128, 512:1024], o_[:])
            P.barrier()

    def rwkv(i, j, b, need_ctx):
        TP = T_ + 8
        W_ = T_ + 4
        RWD = F32 if cfg.dbg.get("rw32", True) else BF16
        with ExitStack() as es:
            big = lambda nm: sbt(es, nm, [128, T_])
            rT, kT, kkT, twd, adT, sgd = [big(n_) for n_ in ("rT", "kT", "kkT", "twd", "adT", "sgd")]
            sgT2 = [big("sgT%d" % d_) for d_ in range(2)]; kdT2 = [big("kdT%d" % d_) for d_ in range(2)]; beT2 = [big("beT%d" % d_) for d_ in range(2)]
            sgT = sgT2[0]
            bf = sbt(es, "rbf", [128, TP]); up = sbt(es, "rup", [128, TP])
            Yp = sbt(es, "Yp", [128, NT, 128]); Vp = sbt(es, "Vp", [128, NT, 128], RWD); bon = sbt(es, "bon", [128, NT, 2])
            mp = sbt(es, "mp", [128, 15]); mn = sbt(es, "mn", [128, 15]); m0 = sbt(es, "m0", [128, 15])
            w0c = sbt(es, "w0c", [128, 8]); a0c = sbt(es, "a0c", [128, 8]); kkc = sbt(es, "kkc", [128, 4]); kac = sbt(es, "kac", [128, 4])
            omka = sbt(es, "omka", [128, 4]); rkc = sbt(es, "rkc", [128, 4])
            w2s = sbt(es, "w2s", [128, 512]); a2s = sbt(es, "a2s", [128, 512]); g2s = sbt(es, "g2s", [128, 512])
            lnw = sbt(es, "lnw", [128, 512]); lnb = sbt(es, "lnb", [128, 512])
            sm = lambda nm, w=128, dt=F32: sbt(es, nm, [128, w], dt)
            WK = []
            for d_ in range(2):
                w_ = {}
                for n_ in ("sgtm", "epos", "eneg", "eexc", "etc", "K2T", "B2T"):
                    w_[n_] = sm(n_ + str(d_))
                for n_ in ("kap", "kti", "bti", "rti"):
                    w_[n_] = sm(n_ + str(d_), 128, F32)
                w_["Wsb"] = sm("Wsb" + str(d_), 128, RWD)
                w_["cums"] = sm("cums" + str(d_), 256); w_["KB2"] = sm("KB2" + str(d_), 256, RWD); w_["gC"] = sm("gC" + str(d_), 1)
                w_["Ns"] = [sm("Ns%d_%d" % (k_, d_), 512, RWD) for k_ in range(2)]
                for n_ in ("AukT", "ArkT", "nArbT"):
                    w_[n_] = sm(n_ + str(d_), 256, RWD)
                w_["H"] = sm("Hst" + str(d_), 64); w_["Hb"] = sm("Hb" + str(d_), 64, F32)
                WK.append(w_)
            t512 = sm("t512", 512); t512b = sm("t512b", 512)
            ypost = sm("ypost", 128); cen = sm("cen", 128); mu = sm("mu", 2); var = sm("var", 2); ob = [sm("rob%d" % k_, 128) for k_ in range(2)]
            P.pool("memset", ap=bf[:], constant=0.0)
            P.dma(q(), mp[:], A["b_mu_prev_col"][j]); P.dma(q(), mn[:], A["b_mu_next_col"][j])
            P.dve("tensor_tensor", out=m0[:], in0=mp[:], in1=mn[:], op=ALU.add)
            P.dve("tensor_scalar", out=m0[:], in0=m0[:], scalar1=-1.0, scalar2=1.0, op0=ALU.mult, op1=ALU.add)
            P.dma(q(), w0c[:], A["b_w0_col"][j]); P.dma(q(), a0c[:], A["b_a0_col"][j])
            P.dma(q(), kkc[:], A["b_k_k_col"][j]); P.dma(q(), kac[:], A["b_k_a_col"][j]); P.dma(q(), rkc[:], A["b_r_k_col"][j])
            P.dve("tensor_scalar", out=omka[:], in0=kac[:], scalar1=-1.0, scalar2=1.0, op0=ALU.mult, op1=ALU.add)
            P.dma(q(), w2s[:], A["b_w2"][j]); P.dma(q(), a2s[:], A["b_a2"][j]); P.dma(q(), g2s[:], A["b_g2"][j])
            P.dma(q(), lnw[:], A["b_ln_w"][j:j + 1, :].partition_broadcast(128)); P.dma(q(), lnb[:], A["b_ln_b"][j:j + 1, :].partition_broadcast(128))

            def shift(fidx, dst, func):
                r0 = 768 + fidx * 128
                P.dma(q(), bf[:, 2:2 + TC], PT[r0:r0 + 128, 0:TC])
                P.dma(q(), bf[:, TC + 6:TC + 6 + TL], PT[r0:r0 + 128, TC:T_])
                P.dve("tensor_scalar", out=up[:, 2:2 + W_], in0=bf[:, 2:2 + W_], scalar1=m0[:, fidx:fidx + 1], scalar2=None, op0=ALU.mult)
                P.dve("scalar_tensor_tensor", out=up[:, 2:2 + W_], in0=bf[:, 1:1 + W_], scalar=mp[:, fidx:fidx + 1], in1=up[:, 2:2 + W_], op0=ALU.mult, op1=ALU.add)
                P.dve("scalar_tensor_tensor", out=up[:, 2:2 + W_], in0=bf[:, 3:3 + W_], scalar=mn[:, fidx:fidx + 1], in1=up[:, 2:2 + W_], op0=ALU.mult, op1=ALU.add)
                P.act("activation", out=dst[:, 0:TC], in_=up[:, 2:2 + TC], func=func)
                P.act("activation", out=dst[:, TC:T_], in_=up[:, TC + 6:TC + 6 + TL], func=func)

            shift(12, twd, AF.Tanh); shift(13, adT, AF.Copy); shift(14, sgd, AF.Sigmoid)
            out_tiles = list(range(NTC, NT)) + (list(range(NTC)) if need_ctx else [])
            for c in range(4):
                shift(c, rT, AF.Copy); shift(4 + c, kT, AF.Copy); shift(8 + c, sgT2[0], AF.Copy)
                for t in range(NT):
                    pp = ps()
                    P.pe("transpose", out=pp[:, 0:128], in_=sgT2[0][:, t * 128:(t + 1) * 128], identity=ident)
                    evac(Vp[:, t, :], pp[:, 0:128])
                P.dve("tensor_scalar", out=kkT[:], in0=kT[:], scalar1=kkc[:, c:c + 1], scalar2=None, op0=ALU.mult)
                for t0 in range(0, T_, 512):
                    n = min(512, T_ - t0)
                    P.act("activation", out=t512[:, 0:n], in_=kkT[:, t0:t0 + n], func=AF.Square)
                    pp = ps()
                    P.pe("matmul", out=pp[:, 0:n], lhsT=C("bones"), rhs=t512[:, 0:n], start=True, stop=True)
                    P.dve("tensor_scalar", out=t512[:, 0:n], in0=pp[:, 0:n], scalar1=1e-12, scalar2=None, op0=ALU.add)
                    P.act("activation", out=t512[:, 0:n], in_=t512[:, 0:n], func=AF.Sqrt)
                    P.dve("reciprocal", out=t512[:, 0:n], in_=t512[:, 0:n])
                    P.dve("tensor_tensor", out=kkT[:, t0:t0 + n], in0=kkT[:, t0:t0 + n], in1=t512[:, 0:n], op=ALU.mult)
                P.pool("memset", ap=Yp[:], constant=0.0)
                P.pool("memset", ap=bon[:], constant=0.0)
                for d in range(2):
                    sgT, kdT, beT = sgT2[d], kdT2[d], beT2[d]
                    aT = V(up, None, up.h[:, 0:T_])
                    for t0 in range(0, T_, 512):
                        n = min(512, T_ - t0)
                        pp = ps()
                        P.pe("matmul", out=pp[:, 0:n], lhsT=w2s[d * 64:(d + 1) * 64, c * 128:(c + 1) * 128], rhs=twd[d * 64:(d + 1) * 64, t0:t0 + n], start=True, stop=True)
                        P.act("activation", out=sgT[:, t0:t0 + n], in_=pp[:, 0:n], func=AF.Sigmoid, bias=w0c[:, d * 4 + c:d * 4 + c + 1])
                        pp = ps()
                        P.pe("matmul", out=pp[:, 0:n], lhsT=a2s[d * 64:(d + 1) * 64, c * 128:(c + 1) * 128], rhs=adT[d * 64:(d + 1) * 64, t0:t0 + n], start=True, stop=True)
                        P.act("activation", out=V(up, None, up.h[:, t0:t0 + n]), in_=pp[:, 0:n], func=AF.Sigmoid, bias=a0c[:, d * 4 + c:d * 4 + c + 1])
                    P.dve("tensor_tensor", out=beT[:], in0=aT, in1=kkT[:], op=ALU.mult)
                    P.dve("tensor_scalar", out=kdT[:], in0=aT, scalar1=kac[:, c:c + 1], scalar2=omka[:, c:c + 1], op0=ALU.mult, op1=ALU.add)
                    P.dve("tensor_tensor", out=kdT[:], in0=kdT[:], in1=kT[:], op=ALU.mult)
                    P.dve("scalar_tensor_tensor", out=aT, in0=rT[:], scalar=rkc[:, c:c + 1], in1=kdT[:], op0=ALU.mult, op1=ALU.mult)
                    pB = ps()
                    for t in range(NT):
                        P.pe("matmul", out=pB[:, t * 2:(t + 1) * 2], lhsT=V(up, None, up.h[:, t * 128:(t + 1) * 128]), rhs=C("hsel"), start=(t == 0), stop=(t == NT - 1))
                    P.dve("tensor_tensor", out=V(bon, None, bon.h[:, :, :].rearrange("p a b -> p (a b)")), in0=V(bon, None, bon.h[:, :, :].rearrange("p a b -> p (a b)")), in1=pB[:, 0:2 * NT], op=ALU.add)
                    P.pool("memset", ap=WK[d]["H"][:], constant=0.0)
                    P.pool("memset", ap=WK[d]["Hb"][:], constant=0.0)

                def group(d, t):
                    w_ = WK[d]
                    sgT, kdT, beT = sgT2[d], kdT2[d], beT2[d]
                    sgtm, cums, epos, eneg, eexc, etc_ = w_["sgtm"], w_["cums"], w_["epos"], w_["eneg"], w_["eexc"], w_["etc"]
                    kap, kti, bti, rti, K2T, B2T, KB2, gC = w_["kap"], w_["kti"], w_["bti"], w_["rti"], w_["K2T"], w_["B2T"], w_["KB2"], w_["gC"]
                    Ns, AukT, ArkT, nArbT, Wsb, H, Hb = w_["Ns"], w_["AukT"], w_["ArkT"], w_["nArbT"], w_["Wsb"], w_["H"], w_["Hb"]

                    def psd():
                        st["psd%d" % d] = st.get("psd%d" % d, 0) + 1
                        return PS[4 * d + st["psd%d" % d] % 4]

                    triS = C("triF") if d == 0 else C("triB")
                    nmA, nmB = ("nm_sl", "nm_su") if d == 0 else ("nm_su", "nm_sl")
                    mB = "m_su" if d == 0 else "m_sl"
                    iB, niB = ("m_iu", "nm_iu") if d == 0 else ("m_il", "nm_il")
                    mk2 = lambda nm: bc(CT, C(nm).ap, [128, 2, 128], 1)
                    tl = slice(t * 128, (t + 1) * 128)
                    pp = psd()
                    P.pe("matmul", out=pp[:, 0:128], lhsT=sgT[:, tl], rhs=ident, start=True, stop=True)
                    P.act("copy", out=sgtm[:], in_=pp[:, 0:128])
                    yield
                    pc = psd()
                    P.pe("matmul", out=pc[:, 0:128], lhsT=sgtm[:], rhs=triS, start=True, stop=False)
                    P.pe("matmul", out=pc[:, 128:256], lhsT=sgtm[:], rhs=C("allS"), start=False, stop=True)
                    P.dve("tensor_copy", out=cums[:], in_=pc[:, 0:256])
                    yield
                    P.act("activation", out=epos[:], in_=cums[:, 0:128], func=AF.Exp)
                    P.act("activation", out=eneg[:], in_=cums[:, 0:128], func=AF.Exp, scale=-1.0)
                    P.dve("scalar_tensor_tensor", out=eexc[:], in0=sgT[:, tl], scalar=WDEC, in1=cums[:, 0:128], op0=ALU.mult, op1=ALU.add)
                    P.act("activation", out=eexc[:], in_=eexc[:], func=AF.Exp)
                    P.dve("tensor_tensor", out=etc_[:], in0=cums[:, 128:256], in1=cums[:, 0:128], op=ALU.subtract)
                    P.act("activation", out=etc_[:], in_=etc_[:], func=AF.Exp)
                    P.act("activation", out=gC[:], in_=cums[:, 128:129], func=AF.Exp)
                    P.pool("tensor_tensor", out=kap[:], in0=kkT[:, tl], in1=eexc[:], op=ALU.mult)
                    P.dve("tensor_tensor", out=kti[:], in0=kdT[:, tl], in1=eneg[:], op=ALU.mult)
                    P.pool("tensor_tensor", out=bti[:], in0=beT[:, tl], in1=eneg[:], op=ALU.mult)
                    P.dve("tensor_tensor", out=rti[:], in0=rT[:, tl], in1=epos[:], op=ALU.mult)
                    P.pool("tensor_tensor", out=K2T[:], in0=kdT[:, tl], in1=etc_[:], op=ALU.mult)
                    P.dve("tensor_tensor", out=B2T[:], in0=beT[:, tl], in1=etc_[:], op=ALU.mult)
                    yield
                    pT = psd()
                    P.pe("matmul", out=pT[:, 0:128], lhsT=K2T[:], rhs=ident, start=True, stop=False)
                    P.pe("matmul", out=pT[:, 128:256], lhsT=B2T[:], rhs=ident, start=False, stop=True)
                    P.act("copy", out=KB2[:, 0:128], in_=pT[:, 0:128])
                    P.dve("tensor_scalar", out=KB2[:, 128:256], in0=pT[:, 128:256], scalar1=-1.0, scalar2=None, op0=ALU.mult)
                    pN = psd(); pA = psd(); pB2 = psd()
                    for hp in range(2):
                        sl = slice(hp * 64, (hp + 1) * 64)
                        P.pe("matmul", out=pN[:, hp * 128:(hp + 1) * 128], lhsT=kap[sl, :], rhs=bti[sl, :], start=(hp == 0), stop=False)
                        P.pe("matmul", out=pN[:, 256 + hp * 128:256 + (hp + 1) * 128], lhsT=bti[sl, :], rhs=kap[sl, :], start=False, stop=(hp == 1))
                        P.pe("matmul", out=pA[:, hp * 128:(hp + 1) * 128], lhsT=kti[sl, :], rhs=kap[sl, :], start=(hp == 0), stop=False)
                        P.pe("matmul", out=pA[:, 256 + hp * 128:256 + (hp + 1) * 128], lhsT=kti[sl, :], rhs=rti[sl, :], start=False, stop=(hp == 1))
                        P.pe("matmul", out=pB2[:, hp * 128:(hp + 1) * 128], lhsT=bti[sl, :], rhs=rti[sl, :], start=(hp == 0), stop=(hp == 1))
                    N0 = Ns[0]
                    P.dve("tensor_tensor", out=r3(N0[:, 0:256], 2), in0=r3(pN[:, 0:256], 2), in1=mk2(nmA), op=ALU.mult)
                    P.dve("tensor_tensor", out=r3(N0[:, 256:512], 2), in0=r3(pN[:, 256:512], 2), in1=mk2(nmB), op=ALU.mult)
                    P.dve("tensor_tensor", out=r3(AukT[:], 2), in0=r3(pA[:, 0:256], 2), in1=mk2(mB), op=ALU.mult)
                    P.dve("tensor_tensor", out=r3(ArkT[:], 2), in0=r3(pA[:, 256:512], 2), in1=mk2(iB), op=ALU.mult)
                    P.dve("tensor_tensor", out=r3(nArbT[:], 2), in0=r3(pB2[:, 0:256], 2), in1=mk2(niB), op=ALU.mult)
                    yield
                    pW = psd()
                    for hp in range(2):
                        sl = slice(hp * 64, (hp + 1) * 64)
                        P.pe("matmul", out=pW[:, hp * 64:(hp + 1) * 64], lhsT=kap[sl, :], rhs=Hb[sl, :], start=(hp == 0), stop=False)
                        P.pe("matmul", out=pW[:, hp * 64:(hp + 1) * 64], lhsT=AukT[:, hp * 128:(hp + 1) * 128], rhs=Vp[:, t, hp * 64:(hp + 1) * 64], start=False, stop=(hp == 1))
                    P.act("copy", out=Wsb[:], in_=pW[:, 0:128])
                    yield
                    cur = 0
                    for lv in range(7):
                        Nc = Ns[cur]
                        pU = psd()
                        for hp in range(2):
                            P.pe("matmul", out=pU[:, hp * 64:(hp + 1) * 64], lhsT=Nc[:, 256 + hp * 128:256 + (hp + 1) * 128], rhs=Wsb[:, hp * 64:(hp + 1) * 64], start=(hp == 0), stop=(hp == 1))
                        if lv < 6:
                            pQ = psd()
                            for hp in range(2):
                                P.pe("matmul", out=pQ[:, hp * 128:(hp + 1) * 128], lhsT=Nc[:, 256 + hp * 128:256 + (hp + 1) * 128], rhs=Nc[:, hp * 128:(hp + 1) * 128], start=(hp == 0), stop=False)
                                P.pe("matmul", out=pQ[:, 256 + hp * 128:256 + (hp + 1) * 128], lhsT=Nc[:, hp * 128:(hp + 1) * 128], rhs=Nc[:, 256 + hp * 128:256 + (hp + 1) * 128], start=False, stop=(hp == 1))
                        P.dve("tensor_tensor", out=Wsb[:], in0=Wsb[:], in1=pU[:, 0:128], op=ALU.add)
                        if lv < 6:
                            cur ^= 1
                            P.act("copy", out=Ns[cur][:], in_=pQ[:, :])
                        yield
                    pYh = psd()
                    for hp in range(2):
                        sl = slice(hp * 64, (hp + 1) * 64)
                        o_ = pYh[:, hp * 64:(hp + 1) * 64]
                        P.pe("matmul", out=o_, lhsT=rti[sl, :], rhs=Hb[sl, :], start=(hp == 0), stop=False)
                        P.pe("matmul", out=o_, lhsT=ArkT[:, hp * 128:(hp + 1) * 128], rhs=Vp[:, t, hp * 64:(hp + 1) * 64], start=False, stop=False)
                        P.pe("matmul", out=o_, lhsT=nArbT[:, hp * 128:(hp + 1) * 128], rhs=Wsb[:, hp * 64:(hp + 1) * 64], start=False, stop=(hp == 1))
                    P.dve("tensor_tensor", out=Yp[:, t, :], in0=Yp[:, t, :], in1=pYh[:, 0:128], op=ALU.add)
                    pH = psd()
                    P.pe("matmul", out=pH[:, 0:128], lhsT=KB2[:, 0:128], rhs=Vp[:, t, :], start=True, stop=False)
                    P.pe("matmul", out=pH[:, 0:128], lhsT=KB2[:, 128:256], rhs=Wsb[:], start=False, stop=True)
                    for hp in range(2):
                        sl = slice(hp * 64, (hp + 1) * 64)
                        P.dve("scalar_tensor_tensor", out=H[sl, :], in0=H[sl, :], scalar=gC[sl, 0:1], in1=pH[sl, hp * 64:(hp + 1) * 64], op0=ALU.mult, op1=ALU.add)
                    P.act("copy", out=Hb[:], in_=H[:])
                    yield

                orders = [list(range(NT)), list(range(NTC - 1, -1, -1)) + list(range(NT - 1, NTC - 1, -1))]
                for s_ in range(NT):
                    gens = [group(0, orders[0][s_]), group(1, orders[1][s_])]
                    if cfg.dbg.get("rwseq", False):
                        for g_ in gens:
                            for _ in g_:
                                pass
                        continue
                    live = [True, True]
                    while any(live):
                        for d_ in range(2):
                            if live[d_]:
                                try:
                                    next(gens[d_])
                                except StopIteration:
                                    live[d_] = False
                for kk_, t in enumerate(out_tiles):
                    o_ = ob[kk_ % 2]
                    y3 = r3(Yp[:, t, :], 2)
                    P.dve("tensor_reduce", out=mu[:], in_=y3, axis=AX.X, op=ALU.add)
                    P.dve("tensor_scalar", out=mu[:], in0=mu[:], scalar1=1.0 / 64, scalar2=None, op0=ALU.mult)
                    P.dve("tensor_tensor", out=r3(cen[:], 2), in0=y3, in1=bc(mu, mu.h[:, :], [128, 2, 64], 2), op=ALU.subtract)
                    P.dve("tensor_tensor", out=ypost[:], in0=cen[:], in1=cen[:], op=ALU.mult)
                    P.dve("tensor_reduce", out=var[:], in_=r3(ypost[:], 2), axis=AX.X, op=ALU.add)
                    P.dve("tensor_scalar", out=var[:], in0=var[:], scalar1=1.0 / 64, scalar2=64e-5, op0=ALU.mult, op1=ALU.add)
                    P.act("activation", out=var[:], in_=var[:], func=AF.Sqrt)
                    P.dve("reciprocal", out=var[:], in_=var[:])
                    P.dve("tensor_tensor", out=r3(cen[:], 2), in0=r3(cen[:], 2), in1=bc(var, var.h[:, :], [128, 2, 64], 2), op=ALU.mult)
                    P.dve("tensor_tensor", out=cen[:], in0=cen[:], in1=lnw[:, c * 128:(c + 1) * 128], op=ALU.mult)
                    P.dve("tensor_tensor", out=cen[:], in0=cen[:], in1=lnb[:, c * 128:(c + 1) * 128], op=ALU.add)
                    P.dve("tensor_tensor", out=r3(ypost[:], 2), in0=r3(Vp[:, t, :], 2), in1=bc(bon, bon.h[:, t, :], [128, 2, 64], 2), op=ALU.mult)
                    P.dve("tensor_tensor", out=cen[:], in0=cen[:], in1=ypost[:], op=ALU.add)
                    pG = ps()
                    P.pe("matmul", out=pG[:, 0:128], lhsT=sgd[:, t * 128:(t + 1) * 128], rhs=g2s[:, c * 128:(c + 1) * 128], start=True, stop=True)
                    P.dve("tensor_tensor", out=o_[:], in0=cen[:], in1=pG[:, 0:128], op=ALU.mult)
                    P.dma(q(), O[t * 128:(t + 1) * 128, 512 + c * 128:512 + (c + 1) * 128], o_[:])
            P.barrier()

    def mix_ab(i, j, b, need_ctx):
        if "attn" not in cfg.dbg.get("skip", ()):
            attn(i, j, b, need_ctx, "a")
        if "rwkv" not in cfg.dbg.get("skip", ()):
            rwkv(i, j, b, need_ctx)

    def mix_cd(i, j, b, need_ctx):
        if "attn" not in cfg.dbg.get("skip", ()):
            attn(i, j, b, need_ctx, "c")
        if "ssd" not in cfg.dbg.get("skip", ()):
            ssd(i, j, b, need_ctx)

    def dcopy(dst, src, rows):
        for r0 in range(0, rows, 128):
            P.dma(q(), dst[r0:r0 + 128, :], src[r0:r0 + 128, :])

    phase_mod()
    final_evs = []
    for b in range(NB):
        dcopy(X[0:TC, :], A["ctx"][b], TC)
        dcopy(X[TC:T_, :], A["x"][b], TL)
        P.barrier()
        for i in range(DEPTH):
            need_ctx = i < DEPTH - 1
            j = i // 2
            if i % 2 == 0:
                phase_inproj(i, b, A["ab_w_in"][j], 2688, need_ctx)
            else:
                phase_inproj(i, b, A["cd_w_in"][j], 2320, need_ctx)
            if ("PT%d" % i) in dbg_out and b == 0:
                dcopy(dbg_out["PT%d" % i], PT, 2688)
                P.barrier()
            if "O_in" in cfg.dbg:
                dcopy(O, A["O_in"][i, b], T_)
                P.barrier()
            if i % 2 == 0:
                mix_ab(i, j, b, need_ctx)
            else:
                mix_cd(i, j, b, need_ctx)
            if ("O%d" % i) in dbg_out and b == 0:
                dcopy(dbg_out["O%d" % i], O, T_)
                P.barrier()
            phase_mlp(i, b, need_ctx)
            if ("X%d" % i) in dbg_out and b == 0:
                dcopy(dbg_out["X%d" % i], X, T_)
                P.barrier()
        final_evs += phase_final(b)
    for nm in dbg_out:
        pass
    P.barrier()
    P.emit(final_evs)
    P.close()
    return nc


def make_in_maps(cfg, inputs, n_cores):
    carr, _, rope = make_consts(cfg)
    NB = cfg.NB
    f = lambda a: np.ascontiguousarray(np.asarray(a, dtype=np.float32))
    shared = {}
    for k_ in ("w_mod", "b_mod", "norm_mix", "norm_mlp", "w_out", "mlp_w1", "mlp_w2", "ab_w_in", "a_q_norm", "a_k_norm",
               "b_mu_prev", "b_mu_next", "b_g2", "b_k_k", "b_k_a", "b_ln_w", "b_ln_b"):
        shared[k_] = f(inputs[k_])
    NE = shared["ab_w_in"].shape[0]
    shared["final_norm"] = f(inputs["final_norm"]).reshape(1, D)
    shared["b_w0"] = f(inputs["b_w0"]).reshape(NE, 1024); shared["b_a0"] = f(inputs["b_a0"]).reshape(NE, 1024)
    shared["b_w2"] = f(inputs["b_w2"]).reshape(NE, 128, 512); shared["b_a2"] = f(inputs["b_a2"]).reshape(NE, 128, 512)
    shared["b_r_k"] = f(inputs["b_r_k"]).reshape(NE, 512)
    col = lambda a, n: f(f(a).reshape(NE, n, 128).transpose(0, 2, 1))
    shared["b_mu_prev_col"] = col(inputs["b_mu_prev"], 15); shared["b_mu_next_col"] = col(inputs["b_mu_next"], 15)
    shared["b_w0_col"] = col(inputs["b_w0"], 8); shared["b_a0_col"] = col(inputs["b_a0"], 8)
    shared["b_k_k_col"] = col(inputs["b_k_k"], 4); shared["b_k_a_col"] = col(inputs["b_k_a"], 4); shared["b_r_k_col"] = col(inputs["b_r_k"], 4)
    if cfg.DEPTH // 2:
        NO = cfg.DEPTH // 2
        for k_ in ("cd_w_in", "c_sink", "d_conv_w", "d_conv_b", "d_D", "d_norm_w"):
            shared[k_] = f(inputs[k_])
        shared["d_conv_wT"] = f(f(inputs["d_conv_w"]).reshape(NO, 5, 8, 128).transpose(0, 3, 2, 1))
        shared["d_conv_b_col"] = f(f(inputs["d_conv_b"]).reshape(NO, 8, 128).transpose(0, 2, 1))
        shared["d_dt_bias"] = f(inputs["d_dt_bias"]).reshape(NO, 16); shared["d_A_log"] = f(inputs["d_A_log"]).reshape(NO, 16)
    shared["consts"] = carr; shared["rope"] = rope
    maps = []
    x, c, ctx, c_ctx = f(inputs["x"]), f(inputs["c"]), f(inputs["ctx"]), f(inputs["c_ctx"])
    for k_ in range(n_cores):
        m = dict(shared)
        sl = slice(k_ * NB, (k_ + 1) * NB)
        m["x"] = x[sl]; m["ctx"] = ctx[sl]
        m["c5T"] = np.ascontiguousarray(np.concatenate([c[sl], c_ctx[None, :]], axis=0).T)
        maps.append(m)
    return maps


_CACHE = {}


def kernel(**inputs):
    cfg = Cfg()
    n_cores = 8
    if "nc" not in _CACHE:
        _CACHE["nc"] = build(cfg)
    nc = _CACHE["nc"]
    maps = make_in_maps(cfg, inputs, n_cores)
    res = run_bass_kernel_spmd(nc, maps, core_ids=list(range(n_cores)))
    return np.concatenate([np.asarray(r["out"]) for r in res.results], axis=0).astype(np.float32)
```

```python
import numpy as np
from contextlib import ExitStack
import concourse.bass as bass
import concourse.mybir as mybir
from concourse.bass_utils import run_bass_kernel_spmd

F32 = mybir.dt.float32
BF16 = mybir.dt.bfloat16
ALU = mybir.AluOpType
AF = mybir.ActivationFunctionType
AX = mybir.AxisListType

WRITE_KW = ("out", "ap", "accum_out")
NDMASEM = 12


class V:
    __slots__ = ("t", "key", "ap")

    def __init__(self, t, key, ap):
        self.t, self.key, self.ap = t, key, ap


class T:
    def __init__(self, name, handle, is_ap=False):
        self.name = name
        self.h = handle
        self.is_ap = is_ap
        self.w = {}
        self.r = {}

    def __getitem__(self, idx):
        return V(self, None, self.h[idx])

    def k(self, key):
        return _Keyed(self, key)

    def v(self, ap, key=None):
        return V(self, key, ap)


class _Sub:
    def __init__(self, t, ap):
        self.t, self.ap = t, ap

    def __getitem__(self, idx):
        return V(self.t, None, self.ap[idx])


class _Keyed:
    def __init__(self, t, key):
        self.t, self.key = t, key

    def __getitem__(self, idx):
        return V(self.t, self.key, self.t.h[idx])


class Prog:
    ENG = ("pe", "act", "dve", "pool", "sp")

    def __init__(self, nc):
        self.nc = nc
        self.es = ExitStack()
        self.ops = {e: [] for e in self.ENG}
        self.cnt = {e: 0 for e in self.ENG}
        self.dcnt = {e: 0 for e in self.ENG}
        self.sems = {}
        self.known = {e: {} for e in self.ENG}
        for e in self.ENG:
            self.sems[e] = self.es.enter_context(nc.semaphore("s_" + e))
            for j in range(NDMASEM):
                self.sems[(e, j)] = self.es.enter_context(nc.semaphore("d_%s%d" % (e, j)))
        self.n_psum = 0

    def sb(self, name, shape, dtype=F32):
        h = self.es.enter_context(self.nc.sbuf_tensor(name, list(shape), dtype))
        return T(name, h)

    def ps(self, name, shape, dtype=F32):
        h = self.es.enter_context(self.nc.psum_tensor(name, list(shape), dtype))
        return T(name, h)

    def dram(self, name, shape, dtype=F32, kind="Internal"):
        h = self.nc.dram_tensor(name, list(shape), dtype, kind=kind).ap()
        return T(name, h, True)

    def _conflicts(self, d, key):
        if key is None:
            for kk, ev in d.items():
                yield kk, ev
        else:
            if key in d:
                yield key, d[key]
            if None in d:
                yield None, d[None]

    def barrier(self):
        evs = []
        for e in self.ENG:
            if self.cnt[e]:
                evs.append((e, self.cnt[e]))
            for j in range(NDMASEM):
                n = self.dcnt[e]
                k = (n - j + NDMASEM - 1) // NDMASEM if n > j else 0
                if k:
                    evs.append(((e, j), 16 * k))
        for e in self.ENG:
            kn = self.known[e]
            wl = []
            for s, v in evs:
                if s == e and e == "pe":
                    continue
                if kn.get(s, 0) >= v:
                    continue
                kn[s] = v
                wl.append((s, v))
            if wl:
                self.ops[e].append((wl, None, None, None, None, None))

    def _issue(self, eng, name, args, kw, is_dma):
        kw = dict(kw)
        reads, writes = list(kw.pop("rd", [])), list(kw.pop("wr", []))
        nargs = []
        for i, a in enumerate(args):
            if isinstance(a, V):
                (writes if i == 0 else reads).append(a)
                nargs.append(a.ap)
            else:
                nargs.append(a)
        nkw = {}
        for k_, a in kw.items():
            if isinstance(a, V):
                (writes if k_ in WRITE_KW else reads).append(a)
                nkw[k_] = a.ap
            else:
                nkw[k_] = a
        waits = {}

        def need(ev):
            s, val = ev
            if waits.get(s, 0) < val:
                waits[s] = val

        for v in reads:
            for _, ev in self._conflicts(v.t.w, v.key):
                need(ev)
        for v in writes:
            for _, ev in self._conflicts(v.t.w, v.key):
                need(ev)
            for _, evs in self._conflicts(v.t.r, v.key):
                for ev in evs:
                    need(ev)
        if is_dma:
            n = self.dcnt[eng]
            self.dcnt[eng] += 1
            j = n % NDMASEM
            semk = (eng, j)
            val = 16 * (n // NDMASEM + 1)
            inc = 16
            if n >= NDMASEM:
                need((semk, val - 16))
        else:
            self.cnt[eng] += 1
            semk = eng
            val = self.cnt[eng]
            inc = 1
        ev = (semk, val)
        kn = self.known[eng]
        wl = []
        for s, val_ in waits.items():
            if s == eng and eng == "pe":
                continue
            if kn.get(s, 0) >= val_:
                continue
            kn[s] = val_
            wl.append((s, val_))
        for v in writes:
            t, key = v.t, v.key
            if key is None:
                t.w = {None: ev}
                t.r = {}
            else:
                t.w[key] = ev
                t.r[key] = []
        for v in reads:
            t, key = v.t, v.key
            t.r.setdefault(key, []).append(ev)
            if len(t.r[key]) > 12:
                d = {}
                for s, vv in t.r[key]:
                    if d.get(s, 0) < vv:
                        d[s] = vv
                t.r[key] = list(d.items())
        self.ops[eng].append((wl, name, nargs, nkw, semk, inc))
        return ev

    def I(self, eng, name, *args, **kw):
        return self._issue(eng, name, args, kw, False)

    def dma(self, eng, out, in_, **kw):
        return self._issue(eng, "dma_start", (), dict(out=out, in_=in_, **kw), True)

    def pe(self, name, *a, **k):
        return self.I("pe", name, *a, **k)

    def act(self, name, *a, **k):
        return self.I("act", name, *a, **k)

    def dve(self, name, *a, **k):
        return self.I("dve", name, *a, **k)

    def pool(self, name, *a, **k):
        return self.I("pool", name, *a, **k)

    def emit(self, final_events):
        nc = self.nc
        engobj = {"pe": "tensor", "act": "scalar", "dve": "vector", "pool": "gpsimd", "sp": "sync"}
        with nc.Block() as block:
            for e in self.ENG:
                ops = self.ops[e]
                if e == "sp":
                    ops = list(ops)
                    fin = {}
                    for (s, v) in final_events:
                        fin[s] = max(fin.get(s, 0), v)
                    ops.append(([(s, v) for s, v in fin.items()], None, None, None, None, None))
                if not ops:
                    continue

                def body(eng, ops=ops):
                    for (wl, name, nargs, nkw, semk, inc) in ops:
                        for s, v in wl:
                            eng.wait_ge(self.sems[s], v)
                        if name is None:
                            continue
                        ins = getattr(eng, name)(*nargs, **nkw)
                        ins.then_inc(self.sems[semk], inc)
                getattr(block, engobj[e])(body)

    def close(self):
        self.es.close()

D = 1024
EPS = 1e-6
WDEC = 0.6065306597126334


class Cfg:
    def __init__(self, NB=4, TC=256, TL=2048, DEPTH=4, GRID_W=64, dbg=None):
        self.NB, self.TC, self.TL, self.DEPTH, self.GRID_W = NB, TC, TL, DEPTH, GRID_W
        self.T = TC + TL
        self.NT = self.T // 128
        self.NTC = TC // 128
        self.R = NB + 1
        self.dbg = dbg or {}


def make_consts(cfg):
    c = {}
    idx = np.arange(128)
    s, t = idx[:, None], idx[None, :]
    c["ident"] = np.eye(128, dtype=np.float32)
    c["bones"] = ((s // 64) == (t // 64)).astype(np.float32)
    c["ones"] = np.ones((128, 128), np.float32)
    selI = np.zeros((128, 128), np.float32)
    selR = np.zeros((128, 128), np.float32)
    for hp in range(2):
        for d in range(64):
            selI[hp * 64 + d, hp * 64 + d] = 1.0
            q = d % 32
            if q < 16:
                selR[hp * 64 + d + 16, hp * 64 + d] = -1.0
            else:
                selR[hp * 64 + d - 16, hp * 64 + d] = 1.0
    c["selI"] = selI
    c["selR"] = selR
    su = (s < t).astype(np.float32); iu = (s <= t).astype(np.float32)
    sl = (s > t).astype(np.float32); il = (s >= t).astype(np.float32)
    c["m_su"] = su; c["m_iu"] = iu
    c["m_sl"] = sl; c["m_il"] = il
    c["nm_su"] = -c["m_su"]; c["nm_sl"] = -c["m_sl"]
    c["nm_iu"] = -c["m_iu"]; c["nm_il"] = -c["m_il"]
    c["triF"] = (-WDEC) * iu; c["triB"] = (-WDEC) * il; c["allS"] = (-WDEC) * np.ones((128, 128), np.float32)
    c["neg_iu"] = ((1.0 - iu) * (-30000.0)).astype(np.float32)
    c["neg_il"] = ((1.0 - il) * (-30000.0)).astype(np.float32)
    c["hsel"] = np.zeros((128, 2), np.float32); c["hsel"][:64, 0] = 1; c["hsel"][64:, 1] = 1
    offs = {}
    cols = []
    o = 0
    for k_, v in c.items():
        offs[k_] = (o, v.shape[1]); cols.append(v.astype(np.float32)); o += v.shape[1]
    arr = np.concatenate(cols, axis=1)
    TL, TC, GW = cfg.TL, cfg.TC, cfg.GRID_W
    tl = np.arange(TL)
    row = (tl // GW).astype(np.float32); col = (tl % GW).astype(np.float32)
    inv = (10000.0 ** (-np.arange(16, dtype=np.float32) / 16)).astype(np.float32)
    ang_r = row[:, None] * inv[None, :]; ang_c = col[:, None] * inv[None, :]
    ang = np.concatenate([ang_r, ang_r, ang_c, ang_c], axis=1)
    rope = np.zeros((64, 2, cfg.T), np.float32)
    rope[:, 0, :TC] = 1.0
    rope[:, 0, TC:] = np.cos(ang).T
    rope[:, 1, TC:] = np.sin(ang).T
    return arr, offs, rope


def build(cfg):
    NB, TC, TL, DEPTH, T_, NT, NTC, R = cfg.NB, cfg.TC, cfg.TL, cfg.DEPTH, cfg.T, cfg.NT, cfg.NTC, cfg.R
    NE, NO = (DEPTH + 1) // 2, DEPTH // 2
    nc = bass.Bass("TRN2", target_bir_lowering=False)
    P = Prog(nc)
    carr, coffs, _ = make_consts(cfg)
    NCC = carr.shape[1]
    A = {}

    def inp(name, shape):
        A[name] = nc.dram_tensor(name, list(shape), F32, kind="ExternalInput").ap()
        return A[name]

    inp("x", [NB, TL, D]); inp("ctx", [NB, TC, D]); inp("c5T", [D, R])
    inp("w_mod", [DEPTH, D, 6 * D]); inp("b_mod", [DEPTH, 6 * D]); inp("norm_mix", [DEPTH, D]); inp("norm_mlp", [DEPTH, D])
    inp("w_out", [DEPTH, D, D]); inp("mlp_w1", [DEPTH, D, 4 * D]); inp("mlp_w2", [DEPTH, 4 * D, D]); inp("final_norm", [1, D])
    inp("ab_w_in", [NE, D, 2688]); inp("a_q_norm", [NE, 64]); inp("a_k_norm", [NE, 64])
    inp("b_mu_prev", [NE, 1920]); inp("b_mu_next", [NE, 1920]); inp("b_w0", [NE, 1024]); inp("b_w2", [NE, 128, 512])
    inp("b_a0", [NE, 1024]); inp("b_a2", [NE, 128, 512]); inp("b_g2", [NE, 128, 512]); inp("b_k_k", [NE, 512]); inp("b_k_a", [NE, 512])
    inp("b_r_k", [NE, 512]); inp("b_ln_w", [NE, 512]); inp("b_ln_b", [NE, 512])
    inp("b_mu_prev_col", [NE, 128, 15]); inp("b_mu_next_col", [NE, 128, 15]); inp("b_w0_col", [NE, 128, 8]); inp("b_a0_col", [NE, 128, 8])
    inp("b_k_k_col", [NE, 128, 4]); inp("b_k_a_col", [NE, 128, 4]); inp("b_r_k_col", [NE, 128, 4])
    if NO:
        inp("cd_w_in", [NO, D, 2320]); inp("c_sink", [NO, 8]); inp("d_conv_w", [NO, 5, 1024]); inp("d_conv_b", [NO, 1024])
        inp("d_dt_bias", [NO, 16]); inp("d_A_log", [NO, 16]); inp("d_D", [NO, 8]); inp("d_norm_w", [NO, 512])
        inp("d_conv_wT", [NO, 128, 8, 5]); inp("d_conv_b_col", [NO, 128, 8])
    inp("consts", [128, NCC]); inp("rope", [64, 2, T_])
    if "O_in" in cfg.dbg:
        inp("O_in", [DEPTH, NB, T_, D])
    out_ap = nc.dram_tensor("out", [NB, TL, D], F32, kind="ExternalOutput").ap()
    dbg_out = {}
    for nm, shp in cfg.dbg.get("outs", {}).items():
        dbg_out[nm] = nc.dram_tensor(nm, list(shp), F32, kind="ExternalOutput").ap()

    def scratch(name, shape):
        return nc.dram_tensor(name, list(shape), F32, kind="Internal").ap()

    MOD = scratch("MOD", [DEPTH, R, 6 * D])
    X = scratch("X", [T_, D])
    PT = scratch("PT", [2688, T_])
    O = scratch("O", [T_, D])

    CT = P.sb("consts_sb", [128, NCC])
    PS = [P.ps("ps%d" % j, [128, 512]) for j in range(8)]
    st = {"ps": 0, "q": 0, "ev": 0}

    def ps():
        st["ps"] = (st["ps"] + 1) % 8
        return PS[st["ps"]]

    def q():
        return "sp"

    def C(name, rows=slice(0, 128), lo=0, n=None):
        o, w = coffs[name]
        n = w - lo if n is None else n
        return CT[rows, o + lo:o + lo + n]

    def evac(dst, src):
        st["ev"] ^= 1
        if st["ev"]:
            P.act("copy", out=dst, in_=src)
        else:
            P.dve("tensor_copy", out=dst, in_=src)

    def sbt(es, name, shape, dt=F32):
        st["uid"] = st.get("uid", 0) + 1
        name = "%s_%d" % (name, st["uid"])
        h = es.enter_context(nc.sbuf_tensor(name, list(shape), dt))
        return T(name, h)

    def r3(v, a):
        return V(v.t, v.key, v.ap.rearrange("p (a b) -> p a b", a=a))

    def bc(t_, ap, shape, axis):
        return V(t_, None, ap.unsqueeze(axis).to_broadcast(list(shape)))

    def bcast_row(ap2d, n=128):
        return ap2d.partition_broadcast(n)

    def colvec(ap1d_row):
        return ap1d_row.rearrange("o (c p) -> p (o c)", p=128)

    P.dma("sp", CT[:], A["consts"][:, :])
    ident = C("ident")

    def transpose(dst, src, npart_in, nfree_in, pp=None, off=0):
        pp = pp or ps()
        P.pe("transpose", out=pp[0:nfree_in, off:off + npart_in], in_=src, identity=C("ident", slice(0, npart_in), 0, npart_in))
        if dst is not None:
            evac(dst, pp[0:nfree_in, off:off + npart_in])
        return pp

    def rms(es_tiles, xt, Gt, St, outv):
        sq, ss, rs = es_tiles
        P.pool("memset", ap=ss[:], constant=0.0)
        P.act("activation", out=sq[:], in_=xt, func=AF.Square, accum_out=ss[:])
        P.dve("tensor_scalar", out=rs[:], in0=ss[:], scalar1=1.0 / D, scalar2=EPS, op0=ALU.mult, op1=ALU.add)
        P.act("activation", out=rs[:], in_=rs[:], func=AF.Sqrt)
        P.dve("reciprocal", out=rs[:], in_=rs[:])
        P.dve("scalar_tensor_tensor", out=outv, in0=xt, scalar=rs[:, 0:1], in1=Gt, op0=ALU.mult, op1=ALU.mult)
        if St is not None:
            P.dve("tensor_tensor", out=outv, in0=outv, in1=St, op=ALU.add)

    def phase_mod():
        with ExitStack() as es:
            c5 = sbt(es, "c5", [128, 8, R]); sc5 = sbt(es, "sc5", [128, 8, R])
            wb = [sbt(es, "wmb%d" % j, [128, 8, 512]) for j in range(2)]
            bt = [sbt(es, "bmb%d" % j, [R, 512]) for j in range(2)]
            res = [sbt(es, "mres%d" % j, [R, 512]) for j in range(2)]
            P.dma("sp", c5[:], A["c5T"].rearrange("(kc p) r -> p kc r", p=128))
            P.act("activation", out=sc5[:], in_=c5[:], func=AF.Silu)
            k = 0
            for i in range(DEPTH):
                for n in range(12):
                    w_, b_, r_ = wb[k % 2], bt[k % 2], res[k % 2]
                    P.dma(q(), w_[:], A["w_mod"][i][:, n * 512:(n + 1) * 512].rearrange("(kc p) n -> p kc n", p=128))
                    P.dma(q(), b_[:], A["b_mod"][i:i + 1, n * 512:(n + 1) * 512].partition_broadcast(R))
                    pp = ps()
                    for kc in range(8):
                        P.pe("matmul", out=pp[0:R, :], lhsT=sc5[:, kc, :], rhs=w_[:, kc, :], start=(kc == 0), stop=(kc == 7))
                    P.dve("tensor_tensor", out=r_[:], in0=pp[0:R, :], in1=b_[:], op=ALU.add)
                    P.dma(q(), MOD[i, :, n * 512:(n + 1) * 512], r_[:])
                    k += 1
            P.barrier()

    def phase_inproj(i, b, w_in, NF, need_ctx):
        with ExitStack() as es:
            hT = sbt(es, "hT", [128, 8, T_], BF16)
            Wbb = [sbt(es, "Wbb%d" % j, [128, 8, 512], BF16) for j in range(2)]
            xt = [sbt(es, "xt%d" % j, [128, D]) for j in range(2)]
            h = [sbt(es, "h%d" % j, [128, D]) for j in range(2)]
            sq = sbt(es, "sq", [128, D]); ss = sbt(es, "ss", [128, 1]); rs = sbt(es, "rs", [128, 1])
            G = {s_: sbt(es, "G" + s_, [128, D]) for s_ in "cl"}
            S = {s_: sbt(es, "S" + s_, [128, D]) for s_ in "cl"}
            nw = sbt(es, "nw", [128, D])
            Wb = [sbt(es, "Wb%d" % j, [128, 8, 512]) for j in range(2)]
            stg = [sbt(es, "stg%d" % j, [128, 512]) for j in range(2)]
            P.dma(q(), nw[:], A["norm_mix"][i:i + 1, :].partition_broadcast(128))
            for s_, r_ in (("c", NB), ("l", b)):
                P.dma(q(), sq[:], MOD[i, r_:r_ + 1, D:2 * D].partition_broadcast(128))
                P.dma(q(), S[s_][:], MOD[i, r_:r_ + 1, 0:D].partition_broadcast(128))
                P.dve("scalar_tensor_tensor", out=G[s_][:], in0=sq[:], scalar=1.0, in1=nw[:], op0=ALU.add, op1=ALU.mult)
            for t in range(NT):
                s_ = "c" if t < NTC else "l"
                x_ = xt[t % 2]; h_ = h[t % 2]
                P.dma(q(), x_[:], X[t * 128:(t + 1) * 128, :])
                rms((sq, ss, rs), x_[:], G[s_][:], S[s_][:], h_[:])
                for half in range(2):
                    pp = ps()
                    for j in range(4):
                        kc = half * 4 + j
                        P.pe("transpose", out=pp[:, j * 128:(j + 1) * 128], in_=h_[:, kc * 128:(kc + 1) * 128], identity=ident)
                    evac(hT[:, half * 4:(half + 1) * 4, t * 128:(t + 1) * 128], r3(pp[:, :], 4))
            k = 0
            for n0 in range(0, NF, 512):
                nn = min(512, NF - n0)
                W_ = Wb[(n0 // 512) % 2]
                P.dma(q(), W_[:, :, 0:nn], w_in[:, n0:n0 + nn].rearrange("(kc p) n -> p kc n", p=128))
                Wf_ = W_
                W_ = Wbb[(n0 // 512) % 2]
                P.pool("tensor_copy", out=W_[:, :, 0:nn], in_=Wf_[:, :, 0:nn])
                for f0 in range(0, nn, 128):
                    m = min(128, nn - f0)
                    for t0 in range(0, T_, 512):
                        n = min(512, T_ - t0)
                        pp = ps()
                        for kc in range(8):
                            P.pe("matmul", out=pp[0:m, 0:n], lhsT=W_[:, kc, f0:f0 + m], rhs=hT[:, kc, t0:t0 + n], start=(kc == 0), stop=(kc == 7))
                        s2 = stg[k % 2]; k += 1
                        evac(s2[0:m, 0:n], pp[0:m, 0:n])
                        P.dma(q(), PT[n0 + f0:n0 + f0 + m, t0:t0 + n], s2[0:m, 0:n])
            P.barrier()

    def phase_mlp(i, b, need_ctx):
        with ExitStack() as es:
            hid = sbt(es, "hid", [128, 32, 512], BF16)
            Wout = sbt(es, "Woutb", [128, 8, D], BF16)
            xb = sbt(es, "xb", [128, 4, D]); ot = sbt(es, "ot", [128, D]); oT = sbt(es, "oT", [128, 8, 128], BF16)
            h2 = sbt(es, "h2", [128, D]); sq = sbt(es, "sq2", [128, D]); ss = sbt(es, "ss2", [128, 1]); rs = sbt(es, "rs2", [128, 1])
            h2T = sbt(es, "h2T", [128, 8, 512], BF16)
            W1b = [sbt(es, "W1b%d" % j, [128, 8, 512]) for j in range(2)]
            W1bb = [sbt(es, "W1bb%d" % j, [128, 8, 512], BF16) for j in range(2)]
            W2r = [sbt(es, "W2r%d" % j, [128, D]) for j in range(2)]
            W2rb = [sbt(es, "W2rb%d" % j, [128, D], BF16) for j in range(2)]
            g1 = sbt(es, "g1", [128, D]); G2 = sbt(es, "G2", [128, D]); S2 = sbt(es, "S2", [128, D]); g2 = sbt(es, "g2", [128, D])
            blocks = []
            if need_ctx:
                for t0 in range(0, NTC, 4):
                    blocks.append(("c", list(range(t0, min(NTC, t0 + 4)))))
            for t0 in range(NTC, NT, 4):
                blocks.append(("l", list(range(t0, min(NT, t0 + 4)))))
            cur = None
            for s_, tiles in blocks:
                if s_ != cur:
                    cur = s_
                    r_ = NB if s_ == "c" else b
                    P.dma(q(), g1[:], MOD[i, r_:r_ + 1, 2 * D:3 * D].partition_broadcast(128))
                    P.dma(q(), S2[:], MOD[i, r_:r_ + 1, 3 * D:4 * D].partition_broadcast(128))
                    P.dma(q(), sq[:], MOD[i, r_:r_ + 1, 4 * D:5 * D].partition_broadcast(128))
                    P.dma(q(), g2[:], MOD[i, r_:r_ + 1, 5 * D:6 * D].partition_broadcast(128))
                    P.dma(q(), h2[:], A["norm_mlp"][i:i + 1, :].partition_broadcast(128))
                    P.dve("scalar_tensor_tensor", out=G2[:], in0=sq[:], scalar=1.0, in1=h2[:], op0=ALU.add, op1=ALU.mult)
                ntl = len(tiles)
                for nh in range(2):
                    P.dma(q(), W1b[nh][:], A["w_out"][i][:, nh * 512:(nh + 1) * 512].rearrange("(kc p) n -> p kc n", p=128))
                    P.pool("tensor_copy", out=Wout[:, :, nh * 512:(nh + 1) * 512], in_=W1b[nh][:])
                for j, t in enumerate(tiles):
                    P.dma(q(), xb[:, j, :], X[t * 128:(t + 1) * 128, :])
                    P.dma(q(), ot[:], O[t * 128:(t + 1) * 128, :])
                    for half in range(2):
                        pp = ps()
                        for jj in range(4):
                            kc = half * 4 + jj
                            P.pe("transpose", out=pp[:, jj * 128:(jj + 1) * 128], in_=ot[:, kc * 128:(kc + 1) * 128], identity=ident)
                        evac(oT[:, half * 4:(half + 1) * 4, :], r3(pp[:, :], 4))
                    for nh in range(2):
                        pp = ps()
                        for kc in range(8):
                            P.pe("matmul", out=pp[:, :], lhsT=oT[:, kc, :], rhs=Wout[:, kc, nh * 512:(nh + 1) * 512], start=(kc == 0), stop=(kc == 7))
                        P.dve("tensor_tensor", out=sq[:, 0:512], in0=pp[:, :], in1=g1[:, nh * 512:(nh + 1) * 512], op=ALU.mult)
                        P.dve("tensor_tensor", out=xb[:, j, nh * 512:(nh + 1) * 512], in0=xb[:, j, nh * 512:(nh + 1) * 512], in1=sq[:, 0:512], op=ALU.add)
                    rms((sq, ss, rs), xb[:, j, :], G2[:], S2[:], h2[:])
                    for half in range(2):
                        pp = ps()
                        for jj in range(4):
                            kc = half * 4 + jj
                            P.pe("transpose", out=pp[:, jj * 128:(jj + 1) * 128], in_=h2[:, kc * 128:(kc + 1) * 128], identity=ident)
                        evac(h2T[:, half * 4:(half + 1) * 4, j * 128:(j + 1) * 128], r3(pp[:, :], 4))
                ntok = ntl * 128
                for n8 in range(8):
                    Wf_ = W1b[n8 % 2]
                    P.dma(q(), Wf_[:], A["mlp_w1"][i][:, n8 * 512:(n8 + 1) * 512].rearrange("(kc p) n -> p kc n", p=128))
                    W_ = W1bb[n8 % 2]
                    P.pool("tensor_copy", out=W_[:], in_=Wf_[:])
                    for f4 in range(4):
                        fc = n8 * 4 + f4
                        pp = ps()
                        for kc in range(8):
                            P.pe("matmul", out=pp[:, 0:ntok], lhsT=W_[:, kc, f4 * 128:(f4 + 1) * 128], rhs=h2T[:, kc, 0:ntok], start=(kc == 0), stop=(kc == 7))
                        P.act("activation", out=sq[:, 0:ntok], in_=pp[:, 0:ntok], func=AF.Relu)
                        P.dve("tensor_tensor", out=hid[:, fc, 0:ntok], in0=sq[:, 0:ntok], in1=sq[:, 0:ntok], op=ALU.mult)
                for fc in range(32):
                    Wf_ = W2r[fc % 2]
                    P.dma(q(), Wf_[:], A["mlp_w2"][i][fc * 128:(fc + 1) * 128, :])
                    W_ = W2rb[fc % 2]
                    P.pool("tensor_copy", out=W_[:], in_=Wf_[:])
                    for j in range(ntl):
                        for nh in range(2):
                            P.pe("matmul", out=PS[j * 2 + nh][:, :], lhsT=hid[:, fc, j * 128:(j + 1) * 128], rhs=W_[:, nh * 512:(nh + 1) * 512], start=(fc == 0), stop=(fc == 31))
                for j, t in enumerate(tiles):
                    for nh in range(2):
                        P.dve("tensor_tensor", out=sq[:, nh * 512:(nh + 1) * 512], in0=PS[j * 2 + nh][:, :], in1=g2[:, nh * 512:(nh + 1) * 512], op=ALU.mult)
                    P.pool("tensor_tensor", out=xb[:, j, :], in0=xb[:, j, :], in1=sq[:], op=ALU.add)
                    P.dma(q(), X[t * 128:(t + 1) * 128, :], xb[:, j, :])
            P.barrier()

    def phase_final(b):
        with ExitStack() as es:
            xt = [sbt(es, "fx%d" % j, [128, D]) for j in range(2)]
            yo = [sbt(es, "fy%d" % j, [128, D]) for j in range(2)]
            sq = sbt(es, "fsq", [128, D]); ss = sbt(es, "fss", [128, 1]); rs = sbt(es, "frs", [128, 1])
            Gf = sbt(es, "Gf", [128, D])
            P.dma(q(), Gf[:], A["final_norm"][0:1, :].partition_broadcast(128))
            evs = []
            for t in range(NTC, NT):
                x_, y_ = xt[t % 2], yo[t % 2]
                P.dma(q(), x_[:], X[t * 128:(t + 1) * 128, :])
                rms((sq, ss, rs), x_[:], Gf[:], None, y_[:])
                evs.append(P.dma(q(), out_ap[b, (t - NTC) * 128:(t - NTC + 1) * 128, :], y_[:]))
            P.barrier()
            return evs

    def attn(i, j, b, need_ctx, kind):
        with ExitStack() as es:
            QT = sbt(es, "QT", [64, 8, T_], BF16); KT = sbt(es, "KT", [64, 2, T_], BF16)
            Vtm = sbt(es, "Vtm", [128, NT, 2, 65], BF16)
            Cc = sbt(es, "Cc", [64, T_]); Ss = sbt(es, "Ss", [64, T_])
            ch = [sbt(es, "ch%d" % k_, [128, 512]) for k_ in range(2)]
            sq = sbt(es, "asq", [128, 512]); rstd = sbt(es, "arstd", [128, 512]); qn = sbt(es, "aqn", [128, 512])
            t1 = sbt(es, "at1", [64, 512]); t2 = sbt(es, "at2", [64, 512])
            ptb = [sbt(es, "ptb%d" % k_, [128, 512], BF16) for k_ in range(3)]
            osb = [sbt(es, "osb%d" % k_, [128, 512]) for k_ in range(2)]
            dn = sbt(es, "dn", [128, 4]); nwq = sbt(es, "nwq", [128, 1]); nwk = sbt(es, "nwk", [128, 1])
            esink = sbt(es, "esink", [128, 8]); vch = sbt(es, "vch", [128, T_])
            P.dma(q(), Cc[:], A["rope"][:, 0, :]); P.dma(q(), Ss[:], A["rope"][:, 1, :])
            P.pool("memset", ap=Vtm[:], constant=1.0)
            if kind == "a":
                for hp in range(2):
                    P.dma(q(), nwq[hp * 64:(hp + 1) * 64, :], A["a_q_norm"][j:j + 1, :].rearrange("o d -> d o"))
                    P.dma(q(), nwk[hp * 64:(hp + 1) * 64, :], A["a_k_norm"][j:j + 1, :].rearrange("o d -> d o"))
            else:
                P.dma(q(), esink[:], A["c_sink"][j:j + 1, :].partition_broadcast(128))
                P.act("activation", out=esink[:], in_=esink[:], func=AF.Exp)
            k_ = 0
            for c in range(5):
                for t0 in range(0, T_, 512):
                    n = min(512, T_ - t0)
                    ch_ = ch[k_ % 2]; k_ += 1
                    P.dma(q(), ch_[:, 0:n], PT[c * 128:(c + 1) * 128, t0:t0 + n])
                    if kind == "a":
                        P.act("activation", out=sq[:, 0:n], in_=ch_[:, 0:n], func=AF.Square)
                        pp = ps()
                        P.pe("matmul", out=pp[:, 0:n], lhsT=C("bones"), rhs=sq[:, 0:n], start=True, stop=True)
                        P.dve("tensor_scalar", out=rstd[:, 0:n], in0=pp[:, 0:n], scalar1=1.0 / 64, scalar2=EPS, op0=ALU.mult, op1=ALU.add)
                        P.act("activation", out=rstd[:, 0:n], in_=rstd[:, 0:n], func=AF.Sqrt)
                        P.dve("reciprocal", out=rstd[:, 0:n], in_=rstd[:, 0:n])
                        nw_ = nwq if c < 4 else nwk
                        P.dve("scalar_tensor_tensor", out=qn[:, 0:n], in0=ch_[:, 0:n], scalar=nw_[:, 0:1], in1=rstd[:, 0:n], op0=ALU.mult, op1=ALU.mult)
                        src = qn
                    else:
                        src = ch_
                    for hp in range(2):
                        p1 = ps(); p2 = ps()
                        P.pe("matmul", out=p1[0:64, 0:n], lhsT=C("selI", lo=hp * 64, n=64), rhs=src[:, 0:n], start=True, stop=True)
                        P.pe("matmul", out=p2[0:64, 0:n], lhsT=C("selR", lo=hp * 64, n=64), rhs=src[:, 0:n], start=True, stop=True)
                        P.dve("tensor_tensor", out=t1[:, 0:n], in0=p1[0:64, 0:n], in1=Cc[:, t0:t0 + n], op=ALU.mult)
                        P.dve("tensor_tensor", out=t2[:, 0:n], in0=p2[0:64, 0:n], in1=Ss[:, t0:t0 + n], op=ALU.mult)
                        dst = QT[:, 2 * c + hp, t0:t0 + n] if c < 4 else KT[:, hp, t0:t0 + n]
                        P.pool("tensor_tensor", out=dst, in0=t1[:, 0:n], in1=t2[:, 0:n], op=ALU.add)
            P.dma(q(), vch[:], PT[640:768, :])
            for t in range(NT):
                pp = ps()
                P.pe("transpose", out=pp[:, 0:128], in_=vch[:, t * 128:(t + 1) * 128], identity=ident)
                evac(Vtm[:, t, :, 0:64], r3(pp[:, 0:128], 2))
            qblocks = list(range(NTC, NT)) + (list(range(NTC)) if need_ctx else [])
            kk_ = 0
            for qt in qblocks:
                if qt < NTC:
                    keys = [(kt, None) for kt in range(NTC)]
                elif kind == "a":
                    keys = [(kt, None) for kt in range(NT)]
                else:
                    keys = [(kt, None) for kt in range(NTC)]
                    if qt - 1 >= NTC:
                        keys.append((qt - 1, "m_il"))
                    keys.append((qt, None))
                    if qt + 1 < NT:
                        keys.append((qt + 1, "m_iu"))
                o_ = osb[kk_ % 2]; kk_ += 1
                for g in range(2):
                    acc = PS[g]
                    nk = len(keys)

                    def smm(idx):
                        kt, m = keys[idx]
                        st["aps"] = st.get("aps", 0) + 1
                        sp_ = PS[2 + st["aps"] % 6]
                        P.pe("matmul", out=r3(sp_[:, :], 4), lhsT=KT[:, g, kt * 128:(kt + 1) * 128], rhs=QT[:, 4 * g:4 * g + 4, qt * 128:(qt + 1) * 128], start=True, stop=True)
                        return sp_

                    pend = smm(0)
                    for idx, (kt, m) in enumerate(keys):
                        sp_ = pend
                        if idx + 1 < nk:
                            pend = smm(idx + 1)
                        pt_ = ptb[idx % 3]
                        P.act("activation", out=pt_[:], in_=sp_[:, :], func=AF.Exp, scale=0.125)
                        if m:
                            P.dve("tensor_tensor", out=r3(pt_[:], 4), in0=r3(pt_[:], 4), in1=bc(CT, C(m).ap, [128, 4, 128], 1), op=ALU.mult)
                        for r in range(4):
                            P.pe("matmul", out=acc[:, r * 65:(r + 1) * 65], lhsT=pt_[:, r * 128:(r + 1) * 128], rhs=Vtm[:, kt, g, :], start=(idx == 0 and r == 0), stop=(idx == nk - 1 and r == 3))
                    accv = V(acc, None, acc.h[:, 0:260].rearrange("p (a b) -> p a b", a=4))
                    den = V(acc, None, accv.ap[:, :, 64])
                    if kind == "c":
                        P.dve("tensor_tensor", out=dn[:], in0=den, in1=esink[:, 4 * g:4 * g + 4], op=ALU.add)
                    else:
                        P.dve("tensor_copy", out=dn[:], in_=den)
                    P.dve("reciprocal", out=dn[:], in_=dn[:])
                    P.dve("tensor_tensor", out=r3(o_[:, g * 256:(g + 1) * 256], 4), in0=V(acc, None, accv.ap[:, :, 0:64]),
                          in1=V(dn, None, dn.h[:, :].unsqueeze(2).to_broadcast([128, 4, 64])), op=ALU.mult)
                P.dma(q(), O[qt * 128:(qt + 1) * 128, 0:512], o_[:])
            P.barrier()


    def ssd(i, j, b, need_ctx):
        TP = T_ + 8
        with ExitStack() as es:
            Xtm = sbt(es, "Xtm", [128, NT, 512]); BT = sbt(es, "BT", [128, 2, T_]); Btm = sbt(es, "Btm", [128, NT, 256])
            CTt = sbt(es, "CTt", [128, 2, T_]); Yacc = sbt(es, "Yacc", [128, NT, 512])
            buf = [sbt(es, "cbuf%d" % k_, [128, TP]) for k_ in range(1)]
            up = sbt(es, "up", [128, TP]); uo = sbt(es, "uo", [128, T_])
            cw = sbt(es, "cw", [128, 8, 5]); cb = sbt(es, "cb", [128, 8]); Abc = sbt(es, "Abc", [128, 16]); dtb = sbt(es, "dtb", [16, 1])
            Dsk = sbt(es, "Dsk", [128, 8]); nwd = sbt(es, "nwd", [128, 512])
            dt_tm = sbt(es, "dt_tm", [128, NT, 16]); a_tm = sbt(es, "a_tm", [128, NT, 16])
            ST = sbt(es, "ST", [128, 2, 256]); aTri = sbt(es, "aTri", [128, 8, 128]); acs = sbt(es, "acs", [128, 16])
            tmp = sbt(es, "stmp", [128, 8, 128]); CBs = sbt(es, "CBs", [128, 2, 128]); eacs = sbt(es, "eacs", [128, 8])
            dte = sbt(es, "dte", [128, 8]); cdec = sbt(es, "cdec", [128, 8]); xw = sbt(es, "xw", [128, 512]); tY = sbt(es, "tY", [128, 512])
            zt = sbt(es, "zt", [128, 4, 128]); u = sbt(es, "su", [128, 512]); ss2 = sbt(es, "ss2", [128, 2]); osb = [sbt(es, "sosb%d" % k_, [128, 512]) for k_ in range(2)]
            sqd = sbt(es, "sqd", [128, 256])
            P.dma(q(), cw[:], A["d_conv_wT"][j]); P.dma(q(), cb[:], A["d_conv_b_col"][j])
            P.dma(q(), Abc[:], A["d_A_log"][j:j + 1, :].partition_broadcast(128))
            P.act("activation", out=Abc[:], in_=Abc[:], func=AF.Exp)
            P.dve("tensor_scalar", out=Abc[:], in0=Abc[:], scalar1=-1.0, scalar2=None, op0=ALU.mult)
            P.dma(q(), dtb[:], A["d_dt_bias"][j:j + 1, :].rearrange("o d -> d o"))
            P.dma(q(), Dsk[:], A["d_D"][j:j + 1, :].partition_broadcast(128))
            P.dma(q(), nwd[:], A["d_norm_w"][j:j + 1, :].partition_broadcast(128))
            P.pool("memset", ap=buf[0][:], constant=0.0)
            P.pool("memset", ap=Yacc[:], constant=0.0)
            for c in range(8):
                bf = buf[0]
                r0 = 1280 + c * 128
                P.dma(q(), bf[:, 2:2 + TC], PT[r0:r0 + 128, 0:TC])
                P.dma(q(), bf[:, TC + 6:TC + 6 + TL], PT[r0:r0 + 128, TC:T_])
                W_ = T_ + 4
                P.dve("tensor_scalar", out=up[:, 2:2 + W_], in0=bf[:, 0:W_], scalar1=cw[:, c, 0:1], scalar2=None, op0=ALU.mult)
                for k_ in range(1, 5):
                    P.dve("scalar_tensor_tensor", out=up[:, 2:2 + W_], in0=bf[:, k_:k_ + W_], scalar=cw[:, c, k_:k_ + 1], in1=up[:, 2:2 + W_], op0=ALU.mult, op1=ALU.add)
                if c < 4:
                    dst = uo
                elif c < 6:
                    dst = V(BT, None, BT.h[:, c - 4, :])
                else:
                    dst = V(CTt, None, CTt.h[:, c - 6, :])
                dv = (lambda lo, hi: dst[:, lo:hi]) if c < 4 else (lambda lo, hi: V(dst.t, None, dst.ap[:, lo:hi]))
                P.act("activation", out=dv(0, TC), in_=up[:, 2:2 + TC], func=AF.Silu, bias=cb[:, c:c + 1])
                P.act("activation", out=dv(TC, T_), in_=up[:, TC + 6:TC + 6 + TL], func=AF.Silu, bias=cb[:, c:c + 1])
                if c < 6:
                    for t in range(NT):
                        pp = ps()
                        src = uo[:, t * 128:(t + 1) * 128] if c < 4 else BT[:, c - 4, t * 128:(t + 1) * 128]
                        P.pe("transpose", out=pp[:, 0:128], in_=src, identity=ident)
                        dd = Xtm[:, t, c * 128:(c + 1) * 128] if c < 4 else Btm[:, t, (c - 4) * 128:(c - 3) * 128]
                        evac(dd, pp[:, 0:128])
            dtT = _Sub(up, up.h[0:16, 0:T_])
            P.dma(q(), dtT[:, :], PT[2304:2320, :])
            P.act("activation", out=dtT[:, :], in_=dtT[:, :], func=AF.Exp, bias=dtb[:, 0:1])
            P.act("activation", out=dtT[:, :], in_=dtT[:, :], func=AF.Ln, bias=1.0)
            for t in range(NT):
                pp = ps()
                P.pe("transpose", out=pp[:, 0:16], in_=dtT[:, t * 128:(t + 1) * 128], identity=C("ident", slice(0, 16), 0, 16))
                evac(dt_tm[:, t, :], pp[:, 0:16])
                P.dve("tensor_tensor", out=a_tm[:, t, :], in0=dt_tm[:, t, :], in1=Abc[:], op=ALU.mult)
            for d in range(2):
                order = list(range(NT)) if d == 0 else (list(range(NTC - 1, -1, -1)) + list(range(NT - 1, NTC - 1, -1)))
                tri = C("m_iu") if d == 0 else C("m_il")
                neg = "neg_iu" if d == 0 else "neg_il"
                P.pool("memset", ap=ST[:], constant=0.0)
                for t in order:
                    a_ = a_tm[:, t, d * 8:(d + 1) * 8]; dt_ = dt_tm[:, t, d * 8:(d + 1) * 8]
                    pA = ps()
                    P.pe("matmul", out=pA[:, 0:8], lhsT=tri, rhs=a_, start=True, stop=False)
                    P.pe("matmul", out=pA[:, 8:16], lhsT=C("ones"), rhs=a_, start=False, stop=True)
                    P.dve("tensor_tensor", out=aTri[:], in0=bc(a_tm, a_tm.h[:, t, d * 8:(d + 1) * 8], [128, 8, 128], 2),
                          in1=bc(CT, tri.ap, [128, 8, 128], 1), op=ALU.mult)
                    evac(acs[:], pA[:, 0:16])
                    pC = ps()
                    for g in range(2):
                        P.pe("matmul", out=pC[:, g * 128:(g + 1) * 128], lhsT=BT[:, g, t * 128:(t + 1) * 128], rhs=CTt[:, g, t * 128:(t + 1) * 128], start=(g == 0), stop=(g == 1))
                    evac(CBs[:], r3(pC[:, 0:256], 2))
                    for g in range(2):
                        pR = ps()
                        P.pe("matmul", out=pR[:, :], lhsT=C("ones"), rhs=V(aTri, None, aTri.h[:, 4 * g:4 * g + 4, :].rearrange("p a b -> p (a b)")), start=True, stop=True)
                        tg = V(tmp, None, tmp.h[:, 4 * g:4 * g + 4, :])
                        P.dve("tensor_tensor", out=tg, in0=r3(pR[:, :], 4), in1=bc(CT, C(neg).ap, [128, 4, 128], 1), op=ALU.add)
                        P.dve("tensor_tensor", out=tg, in0=tg, in1=bc(acs, acs.h[:, 4 * g:4 * g + 4], [128, 4, 128], 2), op=ALU.subtract)
                        P.act("activation", out=tg, in_=tg, func=AF.Exp)
                        P.dve("tensor_tensor", out=tg, in0=tg, in1=bc(CBs, CBs.h[:, g, :], [128, 4, 128], 1), op=ALU.mult)
                        P.dve("tensor_tensor", out=tg, in0=tg, in1=bc(dt_tm, dt_tm.h[:, t, d * 8 + 4 * g:d * 8 + 4 * g + 4], [128, 4, 128], 2), op=ALU.mult)
                    pY = ps()
                    for hh in range(8):
                        P.pe("matmul", out=pY[:, hh * 64:(hh + 1) * 64], lhsT=tmp[:, hh, :], rhs=Xtm[:, t, hh * 64:(hh + 1) * 64], start=(hh == 0), stop=(hh == 7))
                    pO = ps()
                    for g in range(2):
                        P.pe("matmul", out=pO[:, g * 256:(g + 1) * 256], lhsT=CTt[:, g, t * 128:(t + 1) * 128], rhs=ST[:, g, :], start=(g == 0), stop=(g == 1))
                    P.act("activation", out=eacs[:], in_=acs[:, 0:8], func=AF.Exp)
                    P.dve("tensor_tensor", out=r3(tY[:], 8), in0=r3(pO[:, :], 8), in1=bc(eacs, eacs.h[:, :], [128, 8, 64], 2), op=ALU.mult)
                    P.dve("tensor_tensor", out=tY[:], in0=tY[:], in1=pY[:, :], op=ALU.add)
                    P.pool("tensor_tensor", out=Yacc[:, t, :], in0=Yacc[:, t, :], in1=tY[:], op=ALU.add)
                    P.dve("tensor_tensor", out=dte[:], in0=acs[:, 8:16], in1=acs[:, 0:8], op=ALU.subtract)
                    P.act("activation", out=dte[:], in_=dte[:], func=AF.Exp)
                    P.dve("tensor_tensor", out=dte[:], in0=dte[:], in1=dt_, op=ALU.mult)
                    P.dve("tensor_tensor", out=r3(xw[:], 8), in0=r3(Xtm[:, t, :], 8), in1=bc(dte, dte.h[:, :], [128, 8, 64], 2), op=ALU.mult)
                    pS = ps()
                    for g in range(2):
                        P.pe("matmul", out=pS[:, g * 256:(g + 1) * 256], lhsT=Btm[:, t, g * 128:(g + 1) * 128], rhs=xw[:, g * 256:(g + 1) * 256], start=(g == 0), stop=(g == 1))
                    P.act("activation", out=cdec[:], in_=acs[:, 8:16], func=AF.Exp)
                    st3 = V(ST, None, ST.h[:, :, :].rearrange("p g (a b) -> p (g a) b", a=4))
                    P.dve("tensor_tensor", out=st3, in0=st3, in1=bc(cdec, cdec.h[:, :], [128, 8, 64], 2), op=ALU.mult)
                    P.dve("tensor_tensor", out=V(ST, None, ST.h[:, :, :].rearrange("p g b -> p (g b)")), in0=V(ST, None, ST.h[:, :, :].rearrange("p g b -> p (g b)")), in1=pS[:, :], op=ALU.add)
            tiles = list(range(NTC, NT)) + (list(range(NTC)) if need_ctx else [])
            for kk_, t in enumerate(tiles):
                o_ = osb[kk_ % 2]
                P.dma(q(), zt[:], PT[768:1280, t * 128:(t + 1) * 128].rearrange("(c p) t -> p c t", p=128))
                P.act("activation", out=zt[:], in_=zt[:], func=AF.Silu)
                pZ = ps()
                for c in range(4):
                    P.pe("transpose", out=pZ[:, c * 128:(c + 1) * 128], in_=zt[:, c, :], identity=ident)
                P.dve("tensor_tensor", out=r3(u[:], 8), in0=r3(Xtm[:, t, :], 8), in1=bc(Dsk, Dsk.h[:, :], [128, 8, 64], 2), op=ALU.mult)
                P.dve("tensor_tensor", out=u[:], in0=u[:], in1=Yacc[:, t, :], op=ALU.add)
                P.dve("tensor_tensor", out=u[:], in0=u[:], in1=pZ[:, :], op=ALU.mult)
                P.pool("memset", ap=ss2[:], constant=0.0)
                for g in range(2):
                    P.act("activation", out=sqd[:], in_=u[:, g * 256:(g + 1) * 256], func=AF.Square, accum_out=ss2[:, g:g + 1])
                P.dve("tensor_scalar", out=ss2[:], in0=ss2[:], scalar1=1.0 / 256, scalar2=EPS, op0=ALU.mult, op1=ALU.add)
                P.act("activation", out=ss2[:], in_=ss2[:], func=AF.Sqrt)
                P.dve("reciprocal", out=ss2[:], in_=ss2[:])
                for g in range(2):
                    P.dve("scalar_tensor_tensor", out=o_[:, g * 256:(g + 1) * 256], in0=u[:, g * 256:(g + 1) * 256], scalar=ss2[:, g:g + 1], in1=nwd[:, g * 256:(g + 1) * 256], op0=ALU.mult, op1=ALU.mult)
                P.dma(q(), O[t * 128:(t + 1) * 128, 512:1024], o_[:])
            P.barrier()

    def rwkv(i, j, b, need_ctx):
        TP = T_ + 8
        W_ = T_ + 4
        RWD = F32 if cfg.dbg.get("rw32", True) else BF16
        with ExitStack() as es:
            big = lambda nm: sbt(es, nm, [128, T_])
            rT, kT, kkT, twd, adT, sgd = [big(n_) for n_ in ("rT", "kT", "kkT", "twd", "adT", "sgd")]
            sgT2 = [big("sgT%d" % d_) for d_ in range(2)]; kdT2 = [big("kdT%d" % d_) for d_ in range(2)]; beT2 = [big("beT%d" % d_) for d_ in range(2)]
            sgT = sgT2[0]
            bf = sbt(es, "rbf", [128, TP]); up = sbt(es, "rup", [128, TP])
            Yp = sbt(es, "Yp", [128, NT, 128]); Vp = sbt(es, "Vp", [128, NT, 128], RWD); bon = sbt(es, "bon", [128, NT, 2])
            mp = sbt(es, "mp", [128, 15]); mn = sbt(es, "mn", [128, 15]); m0 = sbt(es, "m0", [128, 15])
            w0c = sbt(es, "w0c", [128, 8]); a0c = sbt(es, "a0c", [128, 8]); kkc = sbt(es, "kkc", [128, 4]); kac = sbt(es, "kac", [128, 4])
            omka = sbt(es, "omka", [128, 4]); rkc = sbt(es, "rkc", [128, 4])
            w2s = sbt(es, "w2s", [128, 512]); a2s = sbt(es, "a2s", [128, 512]); g2s = sbt(es, "g2s", [128, 512])
            lnw = sbt(es, "lnw", [128, 512]); lnb = sbt(es, "lnb", [128, 512])
            sm = lambda nm, w=128, dt=F32: sbt(es, nm, [128, w], dt)
            WK = []
            for d_ in range(2):
                w_ = {}
                for n_ in ("sgtm", "epos", "eneg", "eexc", "etc", "K2T", "B2T"):
                    w_[n_] = sm(n_ + str(d_))
                for n_ in ("kap", "kti", "bti", "rti"):
                    w_[n_] = sm(n_ + str(d_), 128, F32)
                w_["Wsb"] = sm("Wsb" + str(d_), 128, RWD)
                w_["cums"] = sm("cums" + str(d_), 256); w_["KB2"] = sm("KB2" + str(d_), 256, RWD); w_["gC"] = sm("gC" + str(d_), 1)
                w_["Ns"] = [sm("Ns%d_%d" % (k_, d_), 512, RWD) for k_ in range(2)]
                for n_ in ("AukT", "ArkT", "nArbT"):
                    w_[n_] = sm(n_ + str(d_), 256, RWD)
                w_["H"] = sm("Hst" + str(d_), 64); w_["Hb"] = sm("Hb" + str(d_), 64, F32)
                WK.append(w_)
            t512 = sm("t512", 512); t512b = sm("t512b", 512)
            ypost = sm("ypost", 128); cen = sm("cen", 128); mu = sm("mu", 2); var = sm("var", 2); ob = [sm("rob%d" % k_, 128) for k_ in range(2)]
            P.pool("memset", ap=bf[:], constant=0.0)
            P.dma(q(), mp[:], A["b_mu_prev_col"][j]); P.dma(q(), mn[:], A["b_mu_next_col"][j])
            P.dve("tensor_tensor", out=m0[:], in0=mp[:], in1=mn[:], op=ALU.add)
            P.dve("tensor_scalar", out=m0[:], in0=m0[:], scalar1=-1.0, scalar2=1.0, op0=ALU.mult, op1=ALU.add)
            P.dma(q(), w0c[:], A["b_w0_col"][j]); P.dma(q(), a0c[:], A["b_a0_col"][j])
            P.dma(q(), kkc[:], A["b_k_k_col"][j]); P.dma(q(), kac[:], A["b_k_a_col"][j]); P.dma(q(), rkc[:], A["b_r_k_col"][j])
            P.dve("tensor_scalar", out=omka[:], in0=kac[:], scalar1=-1.0, scalar2=1.0, op0=ALU.mult, op1=ALU.add)
            P.dma(q(), w2s[:], A["b_w2"][j]); P.dma(q(), a2s[:], A["b_a2"][j]); P.dma(q(), g2s[:], A["b_g2"][j])
            P.dma(q(), lnw[:], A["b_ln_w"][j:j + 1, :].partition_broadcast(128)); P.dma(q(), lnb[:], A["b_ln_b"][j:j + 1, :].partition_broadcast(128))

            def shift(fidx, dst, func):
                r0 = 768 + fidx * 128
                P.dma(q(), bf[:, 2:2 + TC], PT[r0:r0 + 128, 0:TC])
                P.dma(q(), bf[:, TC + 6:TC + 6 + TL], PT[r0:r0 + 128, TC:T_])
                P.dve("tensor_scalar", out=up[:, 2:2 + W_], in0=bf[:, 2:2 + W_], scalar1=m0[:, fidx:fidx + 1], scalar2=None, op0=ALU.mult)
                P.dve("scalar_tensor_tensor", out=up[:, 2:2 + W_], in0=bf[:, 1:1 + W_], scalar=mp[:, fidx:fidx + 1], in1=up[:, 2:2 + W_], op0=ALU.mult, op1=ALU.add)
                P.dve("scalar_tensor_tensor", out=up[:, 2:2 + W_], in0=bf[:, 3:3 + W_], scalar=mn[:, fidx:fidx + 1], in1=up[:, 2:2 + W_], op0=ALU.mult, op1=ALU.add)
                P.act("activation", out=dst[:, 0:TC], in_=up[:, 2:2 + TC], func=func)
                P.act("activation", out=dst[:, TC:T_], in_=up[:, TC + 6:TC + 6 + TL], func=func)

            shift(12, twd, AF.Tanh); shift(13, adT, AF.Copy); shift(14, sgd, AF.Sigmoid)
            out_tiles = list(range(NTC, NT)) + (list(range(NTC)) if need_ctx else [])
            for c in range(4):
                shift(c, rT, AF.Copy); shift(4 + c, kT, AF.Copy); shift(8 + c, sgT2[0], AF.Copy)
                for t in range(NT):
                    pp = ps()
                    P.pe("transpose", out=pp[:, 0:128], in_=sgT2[0][:, t * 128:(t + 1) * 128], identity=ident)
                    evac(Vp[:, t, :], pp[:, 0:128])
                P.dve("tensor_scalar", out=kkT[:], in0=kT[:], scalar1=kkc[:, c:c + 1], scalar2=None, op0=ALU.mult)
                for t0 in range(0, T_, 512):
                    n = min(512, T_ - t0)
                    P.act("activation", out=t512[:, 0:n], in_=kkT[:, t0:t0 + n], func=AF.Square)
                    pp = ps()
                    P.pe("matmul", out=pp[:, 0:n], lhsT=C("bones"), rhs=t512[:, 0:n], start=True, stop=True)
                    P.dve("tensor_scalar", out=t512[:, 0:n], in0=pp[:, 0:n], scalar1=1e-12, scalar2=None, op0=ALU.add)
                    P.act("activation", out=t512[:, 0:n], in_=t512[:, 0:n], func=AF.Sqrt)
                    P.dve("reciprocal", out=t512[:, 0:n], in_=t512[:, 0:n])
                    P.dve("tensor_tensor", out=kkT[:, t0:t0 + n], in0=kkT[:, t0:t0 + n], in1=t512[:, 0:n], op=ALU.mult)
                P.pool("memset", ap=Yp[:], constant=0.0)
                P.pool("memset", ap=bon[:], constant=0.0)
                for d in range(2):
                    sgT, kdT, beT = sgT2[d], kdT2[d], beT2[d]
                    aT = V(up, None, up.h[:, 0:T_])
                    for t0 in range(0, T_, 512):
                        n = min(512, T_ - t0)
                        pp = ps()
                        P.pe("matmul", out=pp[:, 0:n], lhsT=w2s[d * 64:(d + 1) * 64, c * 128:(c + 1) * 128], rhs=twd[d * 64:(d + 1) * 64, t0:t0 + n], start=True, stop=True)
                        P.act("activation", out=sgT[:, t0:t0 + n], in_=pp[:, 0:n], func=AF.Sigmoid, bias=w0c[:, d * 4 + c:d * 4 + c + 1])
                        pp = ps()
                        P.pe("matmul", out=pp[:, 0:n], lhsT=a2s[d * 64:(d + 1) * 64, c * 128:(c + 1) * 128], rhs=adT[d * 64:(d + 1) * 64, t0:t0 + n], start=True, stop=True)
                        P.act("activation", out=V(up, None, up.h[:, t0:t0 + n]), in_=pp[:, 0:n], func=AF.Sigmoid, bias=a0c[:, d * 4 + c:d * 4 + c + 1])
                    P.dve("tensor_tensor", out=beT[:], in0=aT, in1=kkT[:], op=ALU.mult)
                    P.dve("tensor_scalar", out=kdT[:], in0=aT, scalar1=kac[:, c:c + 1], scalar2=omka[:, c:c + 1], op0=ALU.mult, op1=ALU.add)
                    P.dve("tensor_tensor", out=kdT[:], in0=kdT[:], in1=kT[:], op=ALU.mult)
                    P.dve("scalar_tensor_tensor", out=aT, in0=rT[:], scalar=rkc[:, c:c + 1], in1=kdT[:], op0=ALU.mult, op1=ALU.mult)
                    pB = ps()
                    for t in range(NT):
                        P.pe("matmul", out=pB[:, t * 2:(t + 1) * 2], lhsT=V(up, None, up.h[:, t * 128:(t + 1) * 128]), rhs=C("hsel"), start=(t == 0), stop=(t == NT - 1))
                    P.dve("tensor_tensor", out=V(bon, None, bon.h[:, :, :].rearrange("p a b -> p (a b)")), in0=V(bon, None, bon.h[:, :, :].rearrange("p a b -> p (a b)")), in1=pB[:, 0:2 * NT], op=ALU.add)
                    P.pool("memset", ap=WK[d]["H"][:], constant=0.0)
                    P.pool("memset", ap=WK[d]["Hb"][:], constant=0.0)

                def group(d, t):
                    w_ = WK[d]
                    sgT, kdT, beT = sgT2[d], kdT2[d], beT2[d]
                    sgtm, cums, epos, eneg, eexc, etc_ = w_["sgtm"], w_["cums"], w_["epos"], w_["eneg"], w_["eexc"], w_["etc"]
                    kap, kti, bti, rti, K2T, B2T, KB2, gC = w_["kap"], w_["kti"], w_["bti"], w_["rti"], w_["K2T"], w_["B2T"], w_["KB2"], w_["gC"]
                    Ns, AukT, ArkT, nArbT, Wsb, H, Hb = w_["Ns"], w_["AukT"], w_["ArkT"], w_["nArbT"], w_["Wsb"], w_["H"], w_["Hb"]

                    def psd():
                        st["psd%d" % d] = st.get("psd%d" % d, 0) + 1
                        return PS[4 * d + st["psd%d" % d] % 4]

                    triS = C("triF") if d == 0 else C("triB")
                    nmA, nmB = ("nm_sl", "nm_su") if d == 0 else ("nm_su", "nm_sl")
                    mB = "m_su" if d == 0 else "m_sl"
                    iB, niB = ("m_iu", "nm_iu") if d == 0 else ("m_il", "nm_il")
                    mk2 = lambda nm: bc(CT, C(nm).ap, [128, 2, 128], 1)
                    tl = slice(t * 128, (t + 1) * 128)
                    pp = psd()
                    P.pe("matmul", out=pp[:, 0:128], lhsT=sgT[:, tl], rhs=ident, start=True, stop=True)
                    P.act("copy", out=sgtm[:], in_=pp[:, 0:128])
                    yield
                    pc = psd()
                    P.pe("matmul", out=pc[:, 0:128], lhsT=sgtm[:], rhs=triS, start=True, stop=False)
                    P.pe("matmul", out=pc[:, 128:256], lhsT=sgtm[:], rhs=C("allS"), start=False, stop=True)
                    P.dve("tensor_copy", out=cums[:], in_=pc[:, 0:256])
                    yield
                    P.act("activation", out=epos[:], in_=cums[:, 0:128], func=AF.Exp)
                    P.act("activation", out=eneg[:], in_=cums[:, 0:128], func=AF.Exp, scale=-1.0)
                    P.dve("scalar_tensor_tensor", out=eexc[:], in0=sgT[:, tl], scalar=WDEC, in1=cums[:, 0:128], op0=ALU.mult, op1=ALU.add)
                    P.act("activation", out=eexc[:], in_=eexc[:], func=AF.Exp)
                    P.dve("tensor_tensor", out=etc_[:], in0=cums[:, 128:256], in1=cums[:, 0:128], op=ALU.subtract)
                    P.act("activation", out=etc_[:], in_=etc_[:], func=AF.Exp)
                    P.act("activation", out=gC[:], in_=cums[:, 128:129], func=AF.Exp)
                    P.pool("tensor_tensor", out=kap[:], in0=kkT[:, tl], in1=eexc[:], op=ALU.mult)
                    P.dve("tensor_tensor", out=kti[:], in0=kdT[:, tl], in1=eneg[:], op=ALU.mult)
                    P.pool("tensor_tensor", out=bti[:], in0=beT[:, tl], in1=eneg[:], op=ALU.mult)
                    P.dve("tensor_tensor", out=rti[:], in0=rT[:, tl], in1=epos[:], op=ALU.mult)
                    P.pool("tensor_tensor", out=K2T[:], in0=kdT[:, tl], in1=etc_[:], op=ALU.mult)
                    P.dve("tensor_tensor", out=B2T[:], in0=beT[:, tl], in1=etc_[:], op=ALU.mult)
                    yield
                    pT = psd()
                    P.pe("matmul", out=pT[:, 0:128], lhsT=K2T[:], rhs=ident, start=True, stop=False)
                    P.pe("matmul", out=pT[:, 128:256], lhsT=B2T[:], rhs=ident, start=False, stop=True)
                    P.act("copy", out=KB2[:, 0:128], in_=pT[:, 0:128])
                    P.dve("tensor_scalar", out=KB2[:, 128:256], in0=pT[:, 128:256], scalar1=-1.0, scalar2=None, op0=ALU.mult)
                    pN = psd(); pA = psd(); pB2 = psd()
                    for hp in range(2):
                        sl = slice(hp * 64, (hp + 1) * 64)
                        P.pe("matmul", out=pN[:, hp * 128:(hp + 1) * 128], lhsT=kap[sl, :], rhs=bti[sl, :], start=(hp == 0), stop=False)
                        P.pe("matmul", out=pN[:, 256 + hp * 128:256 + (hp + 1) * 128], lhsT=bti[sl, :], rhs=kap[sl, :], start=False, stop=(hp == 1))
                        P.pe("matmul", out=pA[:, hp * 128:(hp + 1) * 128], lhsT=kti[sl, :], rhs=kap[sl, :], start=(hp == 0), stop=False)
                        P.pe("matmul", out=pA[:, 256 + hp * 128:256 + (hp + 1) * 128], lhsT=kti[sl, :], rhs=rti[sl, :], start=False, stop=(hp == 1))
                        P.pe("matmul", out=pB2[:, hp * 128:(hp + 1) * 128], lhsT=bti[sl, :], rhs=rti[sl, :], start=(hp == 0), stop=(hp == 1))
                    N0 = Ns[0]
                    P.dve("tensor_tensor", out=r3(N0[:, 0:256], 2), in0=r3(pN[:, 0:256], 2), in1=mk2(nmA), op=ALU.mult)
                    P.dve("tensor_tensor", out=r3(N0[:, 256:512], 2), in0=r3(pN[:, 256:512], 2), in1=mk2(nmB), op=ALU.mult)
                    P.dve("tensor_tensor", out=r3(AukT[:], 2), in0=r3(pA[:, 0:256], 2), in1=mk2(mB), op=ALU.mult)
                    P.dve("tensor_tensor", out=r3(ArkT[:], 2), in0=r3(pA[:, 256:512], 2), in1=mk2(iB), op=ALU.mult)
                    P.dve("tensor_tensor", out=r3(nArbT[:], 2), in0=r3(pB2[:, 0:256], 2), in1=mk2(niB), op=ALU.mult)
                    yield
                    pW = psd()
                    for hp in range(2):
                        sl = slice(hp * 64, (hp + 1) * 64)
                        P.pe("matmul", out=pW[:, hp * 64:(hp + 1) * 64], lhsT=kap[sl, :], rhs=Hb[sl, :], start=(hp == 0), stop=False)
                        P.pe("matmul", out=pW[:, hp * 64:(hp + 1) * 64], lhsT=AukT[:, hp * 128:(hp + 1) * 128], rhs=Vp[:, t, hp * 64:(hp + 1) * 64], start=False, stop=(hp == 1))
                    P.act("copy", out=Wsb[:], in_=pW[:, 0:128])
                    yield
                    cur = 0
                    for lv in range(7):
                        Nc = Ns[cur]
                        pU = psd()
                        for hp in range(2):
                            P.pe("matmul", out=pU[:, hp * 64:(hp + 1) * 64], lhsT=Nc[:, 256 + hp * 128:256 + (hp + 1) * 128], rhs=Wsb[:, hp * 64:(hp + 1) * 64], start=(hp == 0), stop=(hp == 1))
                        if lv < 6:
                            pQ = psd()
                            for hp in range(2):
                                P.pe("matmul", out=pQ[:, hp * 128:(hp + 1) * 128], lhsT=Nc[:, 256 + hp * 128:256 + (hp + 1) * 128], rhs=Nc[:, hp * 128:(hp + 1) * 128], start=(hp == 0), stop=False)
                                P.pe("matmul", out=pQ[:, 256 + hp * 128:256 + (hp + 1) * 128], lhsT=Nc[:, hp * 128:(hp + 1) * 128], rhs=Nc[:, 256 + hp * 128:256 + (hp + 1) * 128], start=False, stop=(hp == 1))
                        P.dve("tensor_tensor", out=Wsb[:], in0=Wsb[:], in1=pU[:, 0:128], op=ALU.add)
                        if lv < 6:
                            cur ^= 1
                            P.act("copy", out=Ns[cur][:], in_=pQ[:, :])
                        yield
                    pYh = psd()
                    for hp in range(2):
                        sl = slice(hp * 64, (hp + 1) * 64)
                        o_ = pYh[:, hp * 64:(hp + 1) * 64]
                        P.pe("matmul", out=o_, lhsT=rti[sl, :], rhs=Hb[sl, :], start=(hp == 0), stop=False)
                        P.pe("matmul", out=o_, lhsT=ArkT[:, hp * 128:(hp + 1) * 128], rhs=Vp[:, t, hp * 64:(hp + 1) * 64], start=False, stop=False)
                        P.pe("matmul", out=o_, lhsT=nArbT[:, hp * 128:(hp + 1) * 128], rhs=Wsb[:, hp * 64:(hp + 1) * 64], start=False, stop=(hp == 1))
                    P.dve("tensor_tensor", out=Yp[:, t, :], in0=Yp[:, t, :], in1=pYh[:, 0:128], op=ALU.add)
                    pH = psd()
                    P.pe("matmul", out=pH[:, 0:128], lhsT=KB2[:, 0:128], rhs=Vp[:, t, :], start=True, stop=False)
                    P.pe("matmul", out=pH[:, 0:128], lhsT=KB2[:, 128:256], rhs=Wsb[:], start=False, stop=True)
                    for hp in range(2):
                        sl = slice(hp * 64, (hp + 1) * 64)
                        P.dve("scalar_tensor_tensor", out=H[sl, :], in0=H[sl, :], scalar=gC[sl, 0:1], in1=pH[sl, hp * 64:(hp + 1) * 64], op0=ALU.mult, op1=ALU.add)
                    P.act("copy", out=Hb[:], in_=H[:])
                    yield

                orders = [list(range(NT)), list(range(NTC - 1, -1, -1)) + list(range(NT - 1, NTC - 1, -1))]
                for s_ in range(NT):
                    gens = [group(0, orders[0][s_]), group(1, orders[1][s_])]
                    if cfg.dbg.get("rwseq", False):
                        for g_ in gens:
                            for _ in g_:
                                pass
                        continue
                    live = [True, True]
                    while any(live):
                        for d_ in range(2):
                            if live[d_]:
                                try:
                                    next(gens[d_])
                                except StopIteration:
                                    live[d_] = False
                for kk_, t in enumerate(out_tiles):
                    o_ = ob[kk_ % 2]
                    y3 = r3(Yp[:, t, :], 2)
                    P.dve("tensor_reduce", out=mu[:], in_=y3, axis=AX.X, op=ALU.add)
                    P.dve("tensor_scalar", out=mu[:], in0=mu[:], scalar1=1.0 / 64, scalar2=None, op0=ALU.mult)
                    P.dve("tensor_tensor", out=r3(cen[:], 2), in0=y3, in1=bc(mu, mu.h[:, :], [128, 2, 64], 2), op=ALU.subtract)
                    P.dve("tensor_tensor", out=ypost[:], in0=cen[:], in1=cen[:], op=ALU.mult)
                    P.dve("tensor_reduce", out=var[:], in_=r3(ypost[:], 2), axis=AX.X, op=ALU.add)
                    P.dve("tensor_scalar", out=var[:], in0=var[:], scalar1=1.0 / 64, scalar2=64e-5, op0=ALU.mult, op1=ALU.add)
                    P.act("activation", out=var[:], in_=var[:], func=AF.Sqrt)
                    P.dve("reciprocal", out=var[:], in_=var[:])
                    P.dve("tensor_tensor", out=r3(cen[:], 2), in0=r3(cen[:], 2), in1=bc(var, var.h[:, :], [128, 2, 64], 2), op=ALU.mult)
                    P.dve("tensor_tensor", out=cen[:], in0=cen[:], in1=lnw[:, c * 128:(c + 1) * 128], op=ALU.mult)
                    P.dve("tensor_tensor", out=cen[:], in0=cen[:], in1=lnb[:, c * 128:(c + 1) * 128], op=ALU.add)
                    P.dve("tensor_tensor", out=r3(ypost[:], 2), in0=r3(Vp[:, t, :], 2), in1=bc(bon, bon.h[:, t, :], [128, 2, 64], 2), op=ALU.mult)
                    P.dve("tensor_tensor", out=cen[:], in0=cen[:], in1=ypost[:], op=ALU.add)
                    pG = ps()
                    P.pe("matmul", out=pG[:, 0:128], lhsT=sgd[:, t * 128:(t + 1) * 128], rhs=g2s[:, c * 128:(c + 1) * 128], start=True, stop=True)
                    P.dve("tensor_tensor", out=o_[:], in0=cen[:], in1=pG[:, 0:128], op=ALU.mult)
                    P.dma(q(), O[t * 128:(t + 1) * 128, 512 + c * 128:512 + (c + 1) * 128], o_[:])
            P.barrier()

    def mix_ab(i, j, b, need_ctx):
        if "attn" not in cfg.dbg.get("skip", ()):
            attn(i, j, b, need_ctx, "a")
        if "rwkv" not in cfg.dbg.get("skip", ()):
            rwkv(i, j, b, need_ctx)

    def mix_cd(i, j, b, need_ctx):
        if "attn" not in cfg.dbg.get("skip", ()):
            attn(i, j, b, need_ctx, "c")
        if "ssd" not in cfg.dbg.get("skip", ()):
            ssd(i, j, b, need_ctx)

    def dcopy(dst, src, rows):
        for r0 in range(0, rows, 128):
            P.dma(q(), dst[r0:r0 + 128, :], src[r0:r0 + 128, :])

    phase_mod()
    final_evs = []
    for b in range(NB):
        dcopy(X[0:TC, :], A["ctx"][b], TC)
        dcopy(X[TC:T_, :], A["x"][b], TL)
        P.barrier()
        for i in range(DEPTH):
            need_ctx = i < DEPTH - 1
            j = i // 2
            if i % 2 == 0:
                phase_inproj(i, b, A["ab_w_in"][j], 2688, need_ctx)
            else:
                phase_inproj(i, b, A["cd_w_in"][j], 2320, need_ctx)
            if ("PT%d" % i) in dbg_out and b == 0:
                dcopy(dbg_out["PT%d" % i], PT, 2688)
                P.barrier()
            if "O_in" in cfg.dbg:
                dcopy(O, A["O_in"][i, b], T_)
                P.barrier()
            if i % 2 == 0:
                mix_ab(i, j, b, need_ctx)
            else:
                mix_cd(i, j, b, need_ctx)
            if ("O%d" % i) in dbg_out and b == 0:
                dcopy(dbg_out["O%d" % i], O, T_)
                P.barrier()
            phase_mlp(i, b, need_ctx)
            if ("X%d" % i) in dbg_out and b == 0:
                dcopy(dbg_out["X%d" % i], X, T_)
                P.barrier()
        final_evs += phase_final(b)
    for nm in dbg_out:
        pass
    P.barrier()
    P.emit(final_evs)
    P.close()
    return nc


def make_in_maps(cfg, inputs, n_cores):
    carr, _, rope = make_consts(cfg)
    NB = cfg.NB
    f = lambda a: np.ascontiguousarray(np.asarray(a, dtype=np.float32))
    shared = {}
    for k_ in ("w_mod", "b_mod", "norm_mix", "norm_mlp", "w_out", "mlp_w1", "mlp_w2", "ab_w_in", "a_q_norm", "a_k_norm",
               "b_mu_prev", "b_mu_next", "b_g2", "b_k_k", "b_k_a", "b_ln_w", "b_ln_b"):
        shared[k_] = f(inputs[k_])
    NE = shared["ab_w_in"].shape[0]
    shared["final_norm"] = f(inputs["final_norm"]).reshape(1, D)
    shared["b_w0"] = f(inputs["b_w0"]).reshape(NE, 1024); shared["b_a0"] = f(inputs["b_a0"]).reshape(NE, 1024)
    shared["b_w2"] = f(inputs["b_w2"]).reshape(NE, 128, 512); shared["b_a2"] = f(inputs["b_a2"]).reshape(NE, 128, 512)
    shared["b_r_k"] = f(inputs["b_r_k"]).reshape(NE, 512)
    col = lambda a, n: f(f(a).reshape(NE, n, 128).transpose(0, 2, 1))
    shared["b_mu_prev_col"] = col(inputs["b_mu_prev"], 15); shared["b_mu_next_col"] = col(inputs["b_mu_next"], 15)
    shared["b_w0_col"] = col(inputs["b_w0"], 8); shared["b_a0_col"] = col(inputs["b_a0"], 8)
    shared["b_k_k_col"] = col(inputs["b_k_k"], 4); shared["b_k_a_col"] = col(inputs["b_k_a"], 4); shared["b_r_k_col"] = col(inputs["b_r_k"], 4)
    if cfg.DEPTH // 2:
        NO = cfg.DEPTH // 2
        for k_ in ("cd_w_in", "c_sink", "d_conv_w", "d_conv_b", "d_D", "d_norm_w"):
            shared[k_] = f(inputs[k_])
        shared["d_conv_wT"] = f(f(inputs["d_conv_w"]).reshape(NO, 5, 8, 128).transpose(0, 3, 2, 1))
        shared["d_conv_b_col"] = f(f(inputs["d_conv_b"]).reshape(NO, 8, 128).transpose(0, 2, 1))
        shared["d_dt_bias"] = f(inputs["d_dt_bias"]).reshape(NO, 16); shared["d_A_log"] = f(inputs["d_A_log"]).reshape(NO, 16)
    shared["consts"] = carr; shared["rope"] = rope
    maps = []
    x, c, ctx, c_ctx = f(inputs["x"]), f(inputs["c"]), f(inputs["ctx"]), f(inputs["c_ctx"])
    for k_ in range(n_cores):
        m = dict(shared)
        sl = slice(k_ * NB, (k_ + 1) * NB)
        m["x"] = x[sl]; m["ctx"] = ctx[sl]
        m["c5T"] = np.ascontiguousarray(np.concatenate([c[sl], c_ctx[None, :]], axis=0).T)
        maps.append(m)
    return maps


_CACHE = {}


def kernel(**inputs):
    cfg = Cfg()
    n_cores = 8
    if "nc" not in _CACHE:
        _CACHE["nc"] = build(cfg)
    nc = _CACHE["nc"]
    maps = make_in_maps(cfg, inputs, n_cores)
    res = run_bass_kernel_spmd(nc, maps, core_ids=list(range(n_cores)))
    return np.concatenate([np.asarray(r["out"]) for r in res.results], axis=0).astype(np.float32)
```
